# Optimizing a Trainium2 kernel written in Bass

```python
import math
import jax, jax.numpy as jnp
from jax import lax
import numpy as np

D_MODEL = 1024
BATCH = 8
SEQ = 2048
DEPTH = 2
DEC_BATCH = 128
DEC_SEQ = 1
PAST_LEN = 16384
PAGE_SIZE = 128

CHUNK = 128
D_A = D_MODEL
GMLP_HEADS = 4
GMLP_HEAD_DIM = D_A // GMLP_HEADS
D_B = D_MODEL
SSM_GROUP = 16
SSM_GROUPS = D_B // SSM_GROUP
SSM_STATE = 64
D_FF = 4 * D_MODEL
N_IN = 2 * D_A + D_B + 2 * D_MODEL
EPS = 1e-6

kernel_name = "gmlp_s5_gated_hybrid_step"


def rmsnorm(x, g):
    xf = x.astype(jnp.float32)
    y = xf * lax.rsqrt(jnp.mean(xf * xf, axis=-1, keepdims=True) + EPS)
    return (y * g.astype(jnp.float32)).astype(x.dtype)


def causal_chunk_mix(v, w_s, b_s):
    bsz, L = v.shape[0], v.shape[1]
    lc = min(L, CHUNK)
    n = -(-L // lc)
    pad = n * lc - L
    w = jnp.tril(w_s[:, :lc, :lc])
    vp = jnp.pad(v, ((0, 0), (0, pad), (0, 0), (0, 0)))
    vp = vp.reshape(bsz, n, lc, GMLP_HEADS, GMLP_HEAD_DIM)
    out = jnp.einsum('hts,bcshd->bcthd', w, vp) + jnp.transpose(b_s[:, :lc])[None, None, :, :, None]
    return out.reshape(bsz, n * lc, GMLP_HEADS, GMLP_HEAD_DIM)[:, :L]


def _ssm_combine(left, right):
    a1, b1 = left
    a2, b2 = right
    return a2 * a1, a2 * b1 + b2


def s5_branch(xs, lam_re, lam_im, log_dt, b_re, b_im, c_re, c_im, d_skip, w_glu, b_glu, h0):
    f32 = jnp.float32
    bsz, L = xs.shape[0], xs.shape[1]
    xf = xs.astype(f32).reshape(bsz, L, SSM_GROUPS, SSM_GROUP)
    lam = lax.complex(lam_re.astype(f32), lam_im.astype(f32))
    dt = jnp.exp(log_dt.astype(f32))[:, None]
    a_bar = jnp.exp(lam * dt)
    b_mat = lax.complex(b_re.astype(f32), b_im.astype(f32))
    b_bar = ((a_bar - 1.0) / lam)[:, :, None] * b_mat
    c_mat = lax.complex(c_re.astype(f32), c_im.astype(f32))
    bu = jnp.einsum('blgh,gph->blgp', xf.astype(jnp.complex64), b_bar)
    if h0 is not None:
        bu = bu.at[:, 0].add(a_bar[None] * h0)
    a_full = jnp.broadcast_to(a_bar, bu.shape)
    _, hs = lax.associative_scan(_ssm_combine, (a_full, bu), axis=1)
    y = jnp.einsum('blgp,ghp->blgh', hs, c_mat).real + d_skip.astype(f32).reshape(SSM_GROUPS, SSM_GROUP) * xf
    z = jax.nn.gelu(y.reshape(bsz, L, D_B))
    out = z * jax.nn.sigmoid(z @ w_glu.astype(f32) + b_glu.astype(f32))
    return out.astype(xs.dtype), hs[:, -1]


def trunk(x, c, h0_re, h0_im, p, keep_v):
    bsz, L = x.shape[0], x.shape[1]
    re_list, im_list, v_list = [], [], []
    for l in range(DEPTH):
        mod = (jax.nn.silu(c) @ p['w_ada'][l] + p['b_ada'][l])[:, None, :]
        sh1, sc1, gt1, sh2, sc2, gt2 = jnp.split(mod, 6, axis=-1)
        h = rmsnorm(x, p['g_norm1'][l]) * (1.0 + sc1) + sh1
        proj = h @ p['w_in'][l]
        pu, pv, ps, pga, pgb = jnp.split(
            proj, [D_A, 2 * D_A, 2 * D_A + D_B, 2 * D_A + D_B + D_MODEL], axis=-1)
        u = jax.nn.gelu(pu)
        v = rmsnorm(jax.nn.gelu(pv), p['g_v'][l])
        mixed = causal_chunk_mix(v.reshape(bsz, L, GMLP_HEADS, GMLP_HEAD_DIM),
                                 p['w_spatial'][l], p['b_spatial'][l])
        y_a = u * mixed.reshape(bsz, L, D_A)
        h0 = None if h0_re is None else lax.complex(h0_re[l].astype(jnp.float32),
                                                     h0_im[l].astype(jnp.float32))
        y_b, h_last = s5_branch(ps, p['lam_re'][l], p['lam_im'][l], p['log_dt'][l],
                                p['b_re'][l], p['b_im'][l], p['c_re'][l], p['c_im'][l],
                                p['d_skip'][l], p['w_glu'][l], p['b_glu'][l], h0)
        merged = jax.nn.sigmoid(pga) * y_a + jax.nn.sigmoid(pgb) * y_b
        x = x + gt1 * (merged @ p['w_out'][l])
        h = rmsnorm(x, p['g_norm2'][l]) * (1.0 + sc2) + sh2
        x = x + gt2 * (jnp.square(jax.nn.relu(h @ p['w_ff1'][l])) @ p['w_ff2'][l])
        re_list.append(h_last.real)
        im_list.append(h_last.imag)
        if keep_v:
            v_list.append(v)
    y = rmsnorm(x, p['g_final'])
    v_stack = jnp.stack(v_list, axis=0) if keep_v else None
    return y, jnp.stack(re_list, axis=0), jnp.stack(im_list, axis=0), v_stack


def setup_inputs(seed: int = 0) -> dict:
    key = jax.random.key(seed)
    ks = jax.random.split(key, 32)
    f32 = jnp.float32
    nrm = lambda k, shape, s: jax.random.normal(k, shape, f32) * s
    lam_im = jnp.tile(math.pi * jnp.arange(SSM_STATE, dtype=f32)[None, None, :], (DEPTH, SSM_GROUPS, 1))
    return {
        "x_prompt": nrm(ks[0], (BATCH, SEQ, D_MODEL), 1.0),
        "x_sample": nrm(ks[1], (DEC_BATCH, DEC_SEQ, D_MODEL), 1.0),
        "c_prompt": nrm(ks[2], (BATCH, D_MODEL), 1.0),
        "c_sample": nrm(ks[3], (DEC_BATCH, D_MODEL), 1.0),
        "state_ssm_re": nrm(ks[4], (DEPTH, DEC_BATCH, SSM_GROUPS, SSM_STATE), 0.5),
        "state_ssm_im": nrm(ks[5], (DEPTH, DEC_BATCH, SSM_GROUPS, SSM_STATE), 0.5),
        "w_ada": nrm(ks[6], (DEPTH, D_MODEL, 6 * D_MODEL), 0.5 * D_MODEL ** -0.5),
        "b_ada": nrm(ks[7], (DEPTH, 6 * D_MODEL), 0.01),
        "g_norm1": 1.0 + nrm(ks[8], (DEPTH, D_MODEL), 0.05),
        "g_norm2": 1.0 + nrm(ks[9], (DEPTH, D_MODEL), 0.05),
        "w_in": nrm(ks[10], (DEPTH, D_MODEL, N_IN), D_MODEL ** -0.5),
        "g_v": 1.0 + nrm(ks[11], (DEPTH, D_A), 0.05),
        "w_spatial": nrm(ks[12], (DEPTH, GMLP_HEADS, CHUNK, CHUNK), CHUNK ** -0.5),
        "b_spatial": 1.0 + nrm(ks[13], (DEPTH, GMLP_HEADS, CHUNK), 0.1),
        "lam_re": -0.5 * jnp.exp(nrm(ks[14], (DEPTH, SSM_GROUPS, SSM_STATE), 0.05)),
        "lam_im": lam_im,
        "log_dt": jax.random.uniform(ks[15], (DEPTH, SSM_GROUPS), f32, math.log(1e-3), math.log(1e-1)),
        "b_re": nrm(ks[16], (DEPTH, SSM_GROUPS, SSM_STATE, SSM_GROUP), (2 * SSM_GROUP) ** -0.5),
        "b_im": nrm(ks[17], (DEPTH, SSM_GROUPS, SSM_STATE, SSM_GROUP), (2 * SSM_GROUP) ** -0.5),
        "c_re": nrm(ks[18], (DEPTH, SSM_GROUPS, SSM_GROUP, SSM_STATE), (2 * SSM_STATE) ** -0.5),
        "c_im": nrm(ks[19], (DEPTH, SSM_GROUPS, SSM_GROUP, SSM_STATE), (2 * SSM_STATE) ** -0.5),
        "d_skip": nrm(ks[20], (DEPTH, D_B), 1.0),
        "w_glu": nrm(ks[21], (DEPTH, D_B, D_B), D_B ** -0.5),
        "b_glu": nrm(ks[22], (DEPTH, D_B), 0.01),
        "w_out": nrm(ks[23], (DEPTH, D_MODEL, D_MODEL), D_MODEL ** -0.5),
        "w_ff1": nrm(ks[24], (DEPTH, D_MODEL, D_FF), D_MODEL ** -0.5),
        "w_ff2": nrm(ks[25], (DEPTH, D_FF, D_MODEL), D_FF ** -0.5),
        "g_final": 1.0 + nrm(ks[26], (D_MODEL,), 0.05),
    }


def reference(x_prompt, x_sample, c_prompt, c_sample, state_ssm_re, state_ssm_im,
              w_ada, b_ada, g_norm1, g_norm2, w_in, g_v, w_spatial, b_spatial,
              lam_re, lam_im, log_dt, b_re, b_im, c_re, c_im, d_skip, w_glu, b_glu,
              w_out, w_ff1, w_ff2, g_final):
    p = dict(w_ada=w_ada, b_ada=b_ada, g_norm1=g_norm1, g_norm2=g_norm2, w_in=w_in, g_v=g_v,
             w_spatial=w_spatial, b_spatial=b_spatial, lam_re=lam_re, lam_im=lam_im,
             log_dt=log_dt, b_re=b_re, b_im=b_im, c_re=c_re, c_im=c_im, d_skip=d_skip,
             w_glu=w_glu, b_glu=b_glu, w_out=w_out, w_ff1=w_ff1, w_ff2=w_ff2, g_final=g_final)
    y_prompt, ssm_re_prompt, ssm_im_prompt, _ = trunk(x_prompt, c_prompt, None, None, p, False)
    y_sample, ssm_re_sample, ssm_im_sample, gmlp_v_sample = trunk(
        x_sample, c_sample, state_ssm_re, state_ssm_im, p, True)
    return (y_prompt, y_sample, ssm_re_prompt, ssm_im_prompt, ssm_re_sample, ssm_im_sample, gmlp_v_sample)
```

```python
import contextlib
import numpy as np
import concourse.bass as bass
import concourse.mybir as mybir
from concourse.bass_utils import run_bass_kernel_spmd

F32 = mybir.dt.float32
BF16 = mybir.dt.bfloat16
I32 = mybir.dt.int32
AF = mybir.ActivationFunctionType
ALU = mybir.AluOpType
AX = mybir.AxisListType

ENGS = ["pe", "act", "dve", "pool", "sp"]
D = 1024
NT = 2064
TT = [(0, 512), (512, 512), (1024, 512), (1536, 512), (2048, 16)]
DEPTH = 2
EPS = 1e-6
WITH_S5 = True
S5_STAGE = 3


PHASES = []


class Prog:
    def __init__(self, nc):
        self.nc = nc
        self.ops = {e: [] for e in ENGS}
        self.last_w = {}
        self.readers = {}
        self.dma_cnt = {}

    def op(self, eng, fn, reads=(), writes=(), dma_key=None):
        idx = len(self.ops[eng])
        deps = set()
        for r in reads:
            w = self.last_w.get(r)
            if w is not None:
                deps.add(w)
        for r in writes:
            w = self.last_w.get(r)
            if w is not None:
                deps.add(w)
            for rd in self.readers.get(r, {}).values():
                deps.add(rd)
        rec = dict(fn=fn, deps=deps, sig=False, dma_key=dma_key, dma_count=None)
        if dma_key is not None:
            c = self.dma_cnt.get(dma_key, 0) + 16
            self.dma_cnt[dma_key] = c
            rec["dma_count"] = c
            me = ("dma", dma_key, c)
        else:
            me = ("eng", eng, idx)
        for r in writes:
            self.last_w[r] = me
            self.readers[r] = {}
        for r in reads:
            self.readers.setdefault(r, {})[eng if dma_key is None else ("dma", dma_key)] = me
        self.ops[eng].append(rec)
        return me

    def finalize(self, final_eng="sp"):
        nc = self.nc
        for e in ENGS:
            for idx, rec in enumerate(self.ops[e]):
                keep = set()
                for d in rec["deps"]:
                    if d[0] == "eng":
                        _, f, j = d
                        if f == e and e in ("pe", "sp"):
                            continue
                        self.ops[f][j]["sig"] = True
                    keep.add(d)
                rec["deps"] = keep
        sigcount = {}
        for e in ENGS:
            c = 0
            arr = []
            for rec in self.ops[e]:
                if rec["sig"]:
                    c += 1
                arr.append(c)
            sigcount[e] = arr
        stack = contextlib.ExitStack()
        sems = {e: stack.enter_context(nc.semaphore("s_" + e)) for e in ENGS}
        dsems = {k: stack.enter_context(nc.semaphore("d_%d" % i)) for i, k in enumerate(self.dma_cnt)}
        block = stack.enter_context(nc.Block())
        final_waits = list(self.dma_cnt.items())

        def emit(e, h):
            known = {}
            for idx, rec in enumerate(self.ops[e]):
                need = {}
                for d in rec["deps"]:
                    if d[0] == "eng":
                        key = ("e", d[1])
                        val = sigcount[d[1]][d[2]]
                    else:
                        key = ("d", d[1])
                        val = d[2]
                    if val > need.get(key, 0):
                        need[key] = val
                for key, val in need.items():
                    if val > known.get(key, 0):
                        s = sems[key[1]] if key[0] == "e" else dsems[key[1]]
                        h.wait_ge(s, val)
                        known[key] = val
                ins = rec["fn"](h)
                if rec["dma_key"] is not None:
                    ins.then_inc(dsems[rec["dma_key"]], 16)
                elif rec["sig"]:
                    ins.then_inc(sems[e], 1)
            if e == final_eng:
                for k, c in final_waits:
                    h.wait_ge(dsems[k], c)

        @block.tensor
        def _(h):
            emit("pe", h)

        @block.scalar
        def _(h):
            emit("act", h)

        @block.vector
        def _(h):
            emit("dve", h)

        @block.gpsimd
        def _(h):
            emit("pool", h)

        @block.sync
        def _(h):
            emit("sp", h)

        stack.close()


def build_nc():
    nc = bass.Bass("TRN2", target_bir_lowering=False)
    st = contextlib.ExitStack()

    def din(name, shape):
        return nc.dram_tensor(name, list(shape), F32, kind="ExternalInput").ap()

    def dout(name, shape):
        return nc.dram_tensor(name, list(shape), F32, kind="ExternalOutput").ap()

    xT = din("xT", [D, NT])
    cT = din("cT", [D, 17])
    vecs = din("vecs", [DEPTH, 128, 80])
    gfin = din("gfin", [128, 8])
    gvb = din("gvb", [DEPTH, 128, D])
    wsT = din("wsT", [DEPTH, 4, 128, 128])
    bsp = din("bsp", [DEPTH, 1, 512])
    w00 = din("w00", [DEPTH, 16, 4])
    b00 = din("b00", [DEPTH, 1, 64])
    ident_d = din("ident", [128, 128])
    triu_d = din("triu", [128, 128])
    w_ada = din("w_ada", [DEPTH, D, 6 * D])
    w_in = din("w_in", [DEPTH, D, 5 * D])
    w_glu = din("w_glu", [DEPTH, D, D])
    w_out = din("w_out", [DEPTH, D, D])
    w_ff1 = din("w_ff1", [DEPTH, D, 4 * D])
    w_ff2 = din("w_ff2", [DEPTH, 4 * D, D])
    lamre_d = din("lamre", [DEPTH, 128, 32])
    lamim_d = din("lamim", [DEPTH, 128, 32])
    ldt_d = din("ldt", [DEPTH, 128, 32])
    Bre_d = din("Bre", [DEPTH, 128, 512])
    Bim_d = din("Bim", [DEPTH, 128, 512])
    Cre_d = din("Cre", [DEPTH, 128, 512])
    Cim_d = din("Cim", [DEPTH, 128, 512])
    msm_d = din("msm", [128, 4])
    bdm_d = din("bdm", [128, 128])
    bd8_d = din("bd8", [128, 8])
    tauv_d = din("tauv", [128, 16])
    h0_d = din("h0", [DEPTH, 128, 1024])
    sp_o = dout("sp_o", [DEPTH, 128, 64])
    ss_o = dout("ss_o", [DEPTH, 128, 1024])
    yT = dout("yT", [D, NT])
    gv_o = dout("gv_o", [DEPTH, 16, D])

    def sb(name, shape, dt):
        return st.enter_context(nc.sbuf_tensor("sb_" + name, list(shape), dt))

    x32 = sb("x32", [128, 8 * NT], F32)
    hb = sb("hb", [128, 8 * NT], BF16)
    bA = sb("bA", [128, 8 * NT], BF16)
    bB = sb("bB", [128, 8 * NT], BF16)
    NWS = 4
    wts = sb("wts", [128, NWS, 8, 128], BF16)
    wbig = bB[:, 0:8192].rearrange("p (k n) -> p k n", n=1024)
    ident = sb("ident", [128, 128], F32)
    identb = sb("identb", [128, 128], BF16)
    triu = sb("triu", [128, 128], F32)
    ones_m = sb("ones_m", [128, 128], BF16)
    ones_r = sb("ones_r", [1, 128], BF16)
    epsb = sb("epsb", [128, 1], F32)
    vec_sb = sb("vec_sb", [128, DEPTH, 80], F32)
    gfin_sb = sb("gfin_sb", [128, 8], F32)
    c_sb = sb("c_sb", [128, 8, 17], F32)
    cs_bf = sb("cs_bf", [128, 8, 17], BF16)
    mod = sb("mod", [128, 1, 48, 17], F32)
    modA = sb("modA", [128, 1, 2, 8, 17], F32)
    sq = sb("sq", [128, 2, 512], BF16)
    nrm = sb("nrm", [128, 2, 512], F32)
    tmp32 = sb("tmp32", [128, 2, 512], F32)
    tbf = sb("tbf", [128, 1024], F32)
    tb = tbf[:].bitcast(BF16).rearrange("p (a n) -> p a n", n=512)
    vg = nrm[:].rearrange("p a n -> p (a n)")
    vsq = tmp32[:].rearrange("p a n -> p (a n)")
    vs32 = vsq[0:16, :]
    vss = sb("vss", [128, 2], F32)
    gvb_sb = tbf
    vs_bf = sb("vs_bf", [16, D], BF16)
    wsT32 = sq[:].rearrange("p a n -> p (a n)").bitcast(F32).rearrange("p (h t) -> p h t", t=128)
    wsTb = sb("wsTb", [128, 4, 128], BF16)
    bspb = sb("bspb", [1, 512], BF16)
    w00_sb = sb("w00_sb", [16, 4], F32)
    WI = sb("WI", [16, 4, 16], BF16)
    b00b = sb("b00b", [1, 64], BF16)
    ystage = tbf[:].rearrange("p (a n) -> p a n", n=512)
    s5 = sb("s5", [128, 14, 32], F32)
    apw = sb("apw", [128, 2, 16, 32], F32)
    tauv = sb("tauv", [128, 16], F32)
    smt = sb("smt", [128, 320], F32)
    h0k32 = sb("h0k32", [128, 4, 2, 16], F32)
    h0kb = sb("h0kb", [128, 4, 2, 16], BF16)
    BCk = sb("BCk", [128, 2, 2, 64], F32)
    gcur = sb("gcur", [128, 2, 2, 2, 64], F32)
    gtmp = sb("gtmp", [128, 2, 2, 64], F32)
    msm = sb("msm", [128, 4], F32)
    bdm = sb("bdm", [128, 128], F32)
    bd8 = sb("bd8", [128, 8], F32)
    chn = sb("chn", [128, 2, 1, 32], F32)
    Qv = bB[:, 0:8192].rearrange("p (g r c) -> p g r c", g=32, r=2)
    KTc = bB[:, 8192:10240].rearrange("p (k t h) -> p k t h", k=8, t=16)
    arT = bB[:, 10240:14336].rearrange("p (t r m) -> p t r m", t=16, r=2)
    arGWs = [bB[:, 14336 + 1024 * i:15360 + 1024 * i].rearrange("p (t r m) -> p t r m", t=4, r=2) for i in range(2)]
    arKTx = bB[:, 14336:16384].rearrange("p (t m) -> p t m", t=16)
    h0f = tbf[:].rearrange("p (g r t) -> p g r t", g=32, r=2)
    h0b = sq[:].rearrange("p a n -> p (a n)").rearrange("p (g r t) -> p g r t", g=32, r=2)
    Ssm = nrm[:].rearrange("p a n -> p (a n)").rearrange("p (g r t) -> p g r t", g=32, r=2)
    pst = [st.enter_context(nc.psum_tensor("ps%d" % i, [128, 512], F32)) for i in range(8)]

    P = Prog(nc)
    PHASES.clear()

    def mark(name):
        PHASES.append((name, len(P.ops['pe'])))
    ctr = {"ws": 0, "ps": 0, "u": 0, "gw": 0}

    def xk(k, t0, tn):
        return x32[:, k * NT + t0:k * NT + t0 + tn]

    def bufk(b, k, t0, tn):
        return b[:, k * NT + t0:k * NT + t0 + tn]

    def R(pref, k, ti):
        return "%s%d_%d" % (pref, k, ti)

    def A_overlap(lo, hi):
        out = []
        for kk in range(8):
            for ti_, (t0_, tn_) in enumerate(TT):
                a0 = kk * NT + t0_
                if a0 < hi and a0 + tn_ > lo:
                    out.append(R("A", kk, ti_))
        return out

    def B_overlap(lo, hi):
        out = []
        for kk in range(8):
            for ti_, (t0_, tn_) in enumerate(TT):
                a0 = kk * NT + t0_
                if a0 < hi and a0 + tn_ > lo:
                    out.append(R("B", kk, ti_))
        return out

    def vtm_overlap(lo, hi):
        return ["vtm%d" % i_ for i_ in range(16) if i_ * 1024 < hi and (i_ + 1) * 1024 > lo]

    def next_bank(lo=0, hi=8):
        b = lo + ctr["ps"] % (hi - lo)
        ctr["ps"] += 1
        return b

    def uid():
        ctr["u"] += 1
        return ctr["u"]

    P.op("sp", lambda h: h.dma_start(out=ident[:], in_=ident_d), writes=["ident"], dma_key="ident")
    P.op("sp", lambda h: h.dma_start(out=triu[:], in_=triu_d), writes=["triu"], dma_key="triu")
    P.op("sp", lambda h: h.dma_start(out=vec_sb[:], in_=vecs.rearrange("l p n -> p l n")), writes=["vec"], dma_key="vec")
    P.op("sp", lambda h: h.dma_start(out=gfin_sb[:], in_=gfin), writes=["gfin"], dma_key="gfin")
    P.op("sp", lambda h: h.dma_start(out=c_sb[:], in_=cT.rearrange("(k p) n -> p k n", p=128)), writes=["c_sb"], dma_key="c_sb")
    P.op("sp", lambda h: h.dma_start(out=tauv[:], in_=tauv_d), writes=["tauv"], dma_key="tauv")
    P.op("sp", lambda h: h.dma_start(out=msm[:], in_=msm_d), writes=["msm"], dma_key="msm")
    P.op("sp", lambda h: h.dma_start(out=bdm[:], in_=bdm_d), writes=["bdm"], dma_key="bdm")
    P.op("sp", lambda h: h.dma_start(out=bd8[:], in_=bd8_d), writes=["bd8"], dma_key="bd8")
    P.op("dve", lambda h: h.tensor_copy(out=identb[:], in_=ident[:]), reads=["ident"], writes=["identb"])
    P.op("dve", lambda h: h.memset(ones_m[:], 1.0 / 1024.0), writes=["ones_m"])
    P.op("dve", lambda h: h.memset(ones_r[:], 1.0), writes=["ones_r"])
    P.op("dve", lambda h: h.memset(epsb[:], EPS), writes=["epsb"])
    for k in range(8):
        for ti, (t0, tn) in enumerate(TT):
            P.op("sp", lambda h, k=k, t0=t0, tn=tn: h.dma_start(out=xk(k, t0, tn), in_=xT[k * 128:(k + 1) * 128, t0:t0 + tn]),
                 writes=[R("x", k, ti)], dma_key="xl%d_%d" % (k, ti))
    P.op("act", lambda h: h.activation(out=cs_bf[:], in_=c_sb[:], func=AF.Silu), reads=["c_sb"], writes=["cs_bf"])

    def load_wtile(wd_ap, K=8):
        s = ctr["ws"] % NWS
        ctr["ws"] += 1
        P.op("pool", lambda h: h.dma_start(out=wts[:, s, 0:K, :], in_=wd_ap.rearrange("(k p) n -> p k n", p=128)),
             writes=["wt%d" % s], dma_key="wt%d" % s)
        return s, "wt%d" % s

    def norm_fm(scale_fn, bias_fn, dst, dst_pref, final=False):
        def stage_a(ti):
            t0, tn = TT[ti]
            u = ti % 2
            bank = next_bank()
            rs = "msb" if u == 0 else "rstd"
            for k in range(8):
                w = uid() % 2
                P.op("act", lambda h, k=k, w=w: h.activation(out=sq[:, w, 0:tn], in_=xk(k, t0, tn), func=AF.Square),
                     reads=[R("x", k, ti)], writes=["sq%d" % w])
                P.op("pe", lambda h, k=k, w=w: h.matmul(pst[bank][:, 0:tn], lhsT=ones_m[:], rhs=sq[:, w, 0:tn], start=(k == 0), stop=(k == 7)),
                     reads=["ones_m", "sq%d" % w], writes=["ps%d" % bank])
            P.op("act", lambda h: h.activation(out=nrm[:, u, 0:tn], in_=pst[bank][:, 0:tn], func=AF.Sqrt, bias=epsb[:, 0:1]),
                 reads=["ps%d" % bank, "epsb"], writes=[rs])
            P.op("dve", lambda h: h.reciprocal(out=nrm[:, u, 0:tn], in_=nrm[:, u, 0:tn]), reads=[rs], writes=[rs])

        def stage_b(ti):
            t0, tn = TT[ti]
            u = ti % 2
            rs = "msb" if u == 0 else "rstd"
            for k in range(8):
                v = uid() % 2
                P.op("dve", lambda h, k=k, v=v: h.tensor_tensor(out=tmp32[:, v, 0:tn], in0=xk(k, t0, tn), in1=nrm[:, u, 0:tn], op=ALU.mult),
                     reads=[R("x", k, ti), rs], writes=["tmp32_%d" % v])
                if final:
                    w = uid() % 2
                    P.op("act", lambda h, k=k, v=v, w=w: h.activation(out=ystage[:, w, 0:tn], in_=tmp32[:, v, 0:tn], func=AF.Identity, scale=gfin_sb[:, k:k + 1]),
                         reads=["tmp32_%d" % v, "gfin"], writes=["tb%d" % (2 * w), "tb%d" % (2 * w + 1)])
                    P.op("sp", lambda h, k=k, w=w: h.dma_start(out=yT[k * 128:(k + 1) * 128, t0:t0 + tn], in_=ystage[:, w, 0:tn]),
                         reads=["tb%d" % (2 * w), "tb%d" % (2 * w + 1)], writes=["yT_%d_%d" % (k, ti)], dma_key="yst%d" % w)
                elif ti < 4:
                    if k % 2 == 0:
                        P.op("act", lambda h, k=k, v=v: h.activation(out=bufk(dst, k, t0, tn), in_=tmp32[:, v, 0:tn], func=AF.Identity,
                                                                     scale=scale_fn(k, 0), bias=bias_fn(k, 0)),
                             reads=["tmp32_%d" % v, "modA", "mod"], writes=[R(dst_pref, k, ti)])
                    else:
                        P.op("dve", lambda h, k=k, v=v: h.tensor_scalar(out=bufk(dst, k, t0, tn), in0=tmp32[:, v, 0:tn],
                                                                        scalar1=scale_fn(k, 0), scalar2=bias_fn(k, 0), op0=ALU.mult, op1=ALU.add),
                             reads=["tmp32_%d" % v, "modA", "mod"], writes=[R(dst_pref, k, ti)])
                else:
                    P.op("dve", lambda h, k=k, v=v: h.tensor_tensor(out=tmp32[:, v, 0:tn], in0=tmp32[:, v, 0:tn], in1=scale_fn(k, 1), op=ALU.mult),
                         reads=["tmp32_%d" % v, "modA"], writes=["tmp32_%d" % v])
                    P.op("dve", lambda h, k=k, v=v: h.tensor_tensor(out=bufk(dst, k, t0, tn), in0=tmp32[:, v, 0:tn], in1=bias_fn(k, 1), op=ALU.add),
                         reads=["tmp32_%d" % v, "mod"], writes=[R(dst_pref, k, ti)])

        stage_a(0)
        for ti in range(5):
            if ti + 1 < 5:
                stage_a(ti + 1)
            stage_b(ti)

    def multi_proj(specs, M, evac):
        look = 1 if 2 * len(specs) <= NWS else 0
        pending = {}

        def issue(m):
            if m < M and m not in pending:
                pending[m] = [load_wtile(wfn(m), K) for (wfn, src, sp_, K) in specs]

        for m in range(M):
            issue(m)
            slots = pending.pop(m)
            if look:
                issue(m + 1)
            for ti, (t0, tn) in enumerate(TT):
                banks = []
                for (wfn, src, sp_, K), (s, sr) in zip(specs, slots):
                    bank = next_bank()
                    banks.append(bank)
                    for k in range(K):
                        P.op("pe", lambda h, s=s, k=k, K=K, src=src, t0=t0, tn=tn, bank=bank: h.matmul(
                            pst[bank][:, 0:tn], lhsT=wts[:, s, k, :], rhs=bufk(src, k, t0, tn), start=(k == 0), stop=(k == K - 1)),
                            reads=[sr, R(sp_, k, ti)], writes=["ps%d" % bank])
                evac(m, ti, t0, tn, banks)

    def resid_evac(l, gate_idx):
        def f(m, ti, t0, tn, banks):
            bank = banks[0]
            g = gate_idx * 8 + m
            if ti < 4:
                P.op("dve", lambda h: h.scalar_tensor_tensor(out=xk(m, t0, tn), in0=pst[bank][:, 0:tn], scalar=mod[:, 0, g, 0:1],
                                                             in1=xk(m, t0, tn), op0=ALU.mult, op1=ALU.add),
                     reads=["ps%d" % bank, "mod", R("x", m, ti)], writes=[R("x", m, ti)])
            else:
                v = uid() % 2
                P.op("dve", lambda h: h.tensor_tensor(out=tmp32[:, v, 0:tn], in0=pst[bank][:, 0:tn], in1=mod[:, 0, g, 1:17], op=ALU.mult),
                     reads=["ps%d" % bank, "mod"], writes=["tmp32_%d" % v])
                P.op("dve", lambda h: h.tensor_tensor(out=xk(m, t0, tn), in0=xk(m, t0, tn), in1=tmp32[:, v, 0:tn], op=ALU.add),
                     reads=["tmp32_%d" % v, R("x", m, ti)], writes=[R("x", m, ti)])
        return f

    def ada_mod(l):
        for half in range(2):
            bank = next_bank()
            m0 = half * 24
            for m in range(m0, m0 + 24):
                s, sr = load_wtile(w_ada[l, :, m * 128:(m + 1) * 128])
                for k in range(8):
                    P.op("pe", lambda h, s=s, k=k, m=m, bank=bank, m0=m0: h.matmul(
                        pst[bank][:, (m - m0) * 17:(m - m0 + 1) * 17], lhsT=wts[:, s, k, :], rhs=cs_bf[:, k, :],
                        start=(k == 0), stop=(k == 7)), reads=[sr, "cs_bf"], writes=["ps%d" % bank])
            P.op("dve", lambda h, l=l, m0=m0, bank=bank: h.tensor_tensor(
                out=mod[:, 0, m0:m0 + 24, :], in0=pst[bank][:, 0:24 * 17].rearrange("p (m n) -> p m n", n=17),
                in1=vec_sb[:, l, m0:m0 + 24].unsqueeze(2).to_broadcast([128, 24, 17]), op=ALU.add),
                reads=["ps%d" % bank, "vec"], writes=["mod"])
        for j in range(2):
            sc0 = 8 + 24 * j
            P.op("dve", lambda h, l=l, j=j, sc0=sc0: h.tensor_scalar(out=modA[:, 0, j, :, :], in0=mod[:, 0, sc0:sc0 + 8, :],
                                                                    scalar1=1.0, scalar2=None, op0=ALU.add),
                 reads=["mod"], writes=["modA"])
            P.op("dve", lambda h, l=l, j=j: h.tensor_tensor(
                out=modA[:, 0, j, :, :], in0=modA[:, 0, j, :, :],
                in1=vec_sb[:, l, 48 + 8 * j:56 + 8 * j].unsqueeze(2).to_broadcast([128, 8, 17]), op=ALU.mult),
                reads=["modA", "vec"], writes=["modA"])


    TWO_PI = 2.0 * float(np.pi)
    S5R = ["Qh0", "Qh1", "KTc", "arT0", "arT1", "arT2", "arT3", "arGW0", "arGW1", "arKTx"]
    ALLB = [R("B", kk, ti_) for kk in range(8) for ti_ in range(5)]
    L_RE, L_IM, L_DT, L_LR, L_TH, C_R, C_I, A16R, A16I, A256R, A256I, SC0, SC1, SC2 = range(14)

    def S(i):
        return s5[:, i, :]

    def dv(fn, reads, writes, eng="dve"):
        P.op(eng, fn, reads=reads, writes=writes)

    def s5_tt(o, a, b, op, eng="dve"):
        dv(lambda h: h.tensor_tensor(out=S(o), in0=S(a), in1=S(b), op=op), ["s5"], ["s5"], eng)

    def s5_ts(o, a, s1, op0):
        dv(lambda h: h.tensor_scalar(out=S(o), in0=S(a), scalar1=s1, scalar2=None, op0=op0), ["s5"], ["s5"])

    T_A = tmp32[:, 0, :]
    T_B = tmp32[:, 1, :]
    T_C = nrm[:, 0, :]
    T_D = nrm[:, 1, :]
    T_I = sq[:].rearrange("p a n -> p (a n)").bitcast(I32)
    TR = ["tmp32_0", "tmp32_1", "msb", "rstd", "sq0", "sq1"]

    def rr_big(t, r):
        dv(lambda h: h.tensor_copy(out=T_I, in_=t), TR, TR)
        dv(lambda h: h.tensor_copy(out=T_D, in_=T_I), TR, TR)
        dv(lambda h: h.tensor_tensor(out=r, in0=t, in1=T_D, op=ALU.subtract), TR, TR)
        dv(lambda h: h.tensor_single_scalar(out=T_D, in_=r, scalar=0.5, op=ALU.is_gt), TR, TR)
        dv(lambda h: h.tensor_tensor(out=r, in0=r, in1=T_D, op=ALU.subtract), TR, TR)
        dv(lambda h: h.tensor_single_scalar(out=T_D, in_=r, scalar=-0.5, op=ALU.is_lt), TR, TR)
        dv(lambda h: h.tensor_tensor(out=r, in0=r, in1=T_D, op=ALU.add), TR, TR)

    def cview(t):
        return t.rearrange("p (q h) -> p q h", q=4)

    def bc4(tab, k):
        return tab[:, 4 * k:4 * k + 4].unsqueeze(2).to_broadcast([128, 4, 16])

    def cmul(eng, dst_re, dst_im, src_re, src_im, fr, fi, res):
        t1, t2 = cview(gtmp[:, 0 if eng == "dve" else 1, 0, :]), cview(gtmp[:, 0 if eng == "dve" else 1, 1, :])
        tr = "gtmp_" + eng
        dv(lambda h: h.tensor_tensor(out=t1, in0=src_re, in1=fr, op=ALU.mult), res + ["s5", "apw"], [tr], eng)
        dv(lambda h: h.tensor_tensor(out=t2, in0=src_im, in1=fi, op=ALU.mult), res + ["s5", "apw", tr], [tr], eng)
        dv(lambda h: h.tensor_tensor(out=dst_re, in0=t1, in1=t2, op=ALU.subtract), [tr] + res, res, eng)
        dv(lambda h: h.tensor_tensor(out=t1, in0=src_re, in1=fi, op=ALU.mult), res + ["s5", "apw", tr], [tr], eng)
        dv(lambda h: h.tensor_tensor(out=t2, in0=src_im, in1=fr, op=ALU.mult), res + ["s5", "apw", tr], [tr], eng)
        dv(lambda h: h.tensor_tensor(out=dst_im, in0=t1, in1=t2, op=ALU.add), [tr] + res, res, eng)

    def expand(eng, dst, src, mcol, reads, writes):
        dv(lambda h: h.tensor_tensor(out=dst.rearrange("p (q g h) -> p q g h", q=4, g=2),
                                     in0=src.unsqueeze(2).to_broadcast([128, 4, 2, 16]),
                                     in1=msm[:, mcol:mcol + 2].unsqueeze(1).unsqueeze(3).to_broadcast([128, 4, 2, 16]), op=ALU.mult),
           reads + ["msm"], writes, eng)

    def gen_quarter(eng, k, qq, X_re, X_im, Y_re, Y_im, out_re, out_im, reads, writes):
        ar_ = apw[:, 0, 4 * qq:4 * qq + 4, 4 * k:4 * k + 4].unsqueeze(3).to_broadcast([128, 4, 4, 32])
        ai_ = apw[:, 1, 4 * qq:4 * qq + 4, 4 * k:4 * k + 4].unsqueeze(3).to_broadcast([128, 4, 4, 32])
        if eng == "dve":
            ta_, tb_, r1, r2 = T_A, T_B, ["tmp32_0"], ["tmp32_1"]
        else:
            ta_, tb_, r1, r2 = tbf[:, 0:512], tbf[:, 512:1024], ["tb0", "tb1"], ["tb2", "tb3"]
        t1 = ta_.rearrange("p (t q m) -> p t q m", t=4, q=4)
        t2 = tb_.rearrange("p (t q m) -> p t q m", t=4, q=4)
        bx = lambda X: X.rearrange("p (q m) -> p q m", q=4).unsqueeze(1).to_broadcast([128, 4, 4, 32])
        o4 = lambda O: O.rearrange("p t (q m) -> p t q m", q=4)
        rd = reads + ["apw"]
        dv(lambda h: h.tensor_tensor(out=t1, in0=ar_, in1=bx(X_re), op=ALU.mult), rd, r1, eng)
        dv(lambda h: h.tensor_tensor(out=t2, in0=ai_, in1=bx(X_im), op=ALU.mult), rd, r2, eng)
        dv(lambda h: h.tensor_tensor(out=o4(out_re), in0=t1, in1=t2, op=ALU.subtract), r1 + r2, writes, eng)
        dv(lambda h: h.tensor_tensor(out=t1, in0=ar_, in1=bx(Y_im), op=ALU.mult), rd, r1, eng)
        dv(lambda h: h.tensor_tensor(out=t2, in0=ai_, in1=bx(Y_re), op=ALU.mult), rd, r2, eng)
        dv(lambda h: h.tensor_tensor(out=o4(out_im), in0=t1, in1=t2, op=ALU.add), r1 + r2, writes, eng)

    def s5_tables(l):
        P.op("sp", lambda h: h.dma_start(out=s5[:, L_RE, :], in_=lamre_d[l]), writes=["s5"], dma_key="s5a")
        P.op("sp", lambda h: h.dma_start(out=s5[:, L_IM, :], in_=lamim_d[l]), writes=["s5"], dma_key="s5b")
        P.op("sp", lambda h: h.dma_start(out=s5[:, L_DT, :], in_=ldt_d[l]), writes=["s5"], dma_key="s5c")
        dv(lambda h: h.activation(out=S(L_DT), in_=S(L_DT), func=AF.Exp), ["s5"], ["s5"], "act")
        s5_tt(L_LR, L_RE, L_DT, ALU.mult)
        s5_tt(L_TH, L_IM, L_DT, ALU.mult)
        s5_ts(L_TH, L_TH, 1.0 / TWO_PI, ALU.mult)
        b3 = lambda tab: tab.unsqueeze(1).to_broadcast([128, 16, 32])
        tv = tauv[:].unsqueeze(2).to_broadcast([128, 16, 32])
        v3 = lambda t: t.rearrange("p (t g) -> p t g", t=16)
        dv(lambda h: h.tensor_tensor(out=v3(T_A), in0=b3(S(L_LR)), in1=tv, op=ALU.mult), ["s5", "tauv"] + TR, TR)
        dv(lambda h: h.activation(out=T_A, in_=T_A, func=AF.Exp), TR, TR, "act")
        dv(lambda h: h.tensor_tensor(out=v3(T_B), in0=b3(S(L_TH)), in1=tv, op=ALU.mult), ["s5", "tauv"] + TR, TR)
        rr_big(T_B, T_C)
        dv(lambda h: h.activation(out=T_C, in_=T_C, func=AF.Sin, scale=6.28318), TR, TR, "act")
        dv(lambda h: h.tensor_tensor(out=apw[:, 1, :, :], in0=v3(T_A), in1=v3(T_C), op=ALU.mult), TR, ["apw"])
        dv(lambda h: h.tensor_scalar(out=T_B, in0=T_B, scalar1=0.25, scalar2=None, op0=ALU.add), TR, TR)
        rr_big(T_B, T_C)
        dv(lambda h: h.activation(out=T_C, in_=T_C, func=AF.Sin, scale=6.28318), TR, TR, "act")
        dv(lambda h: h.tensor_tensor(out=apw[:, 0, :, :], in0=v3(T_A), in1=v3(T_C), op=ALU.mult), TR, ["apw"])
        AR, AI = apw[:, 0, 1, :], apw[:, 1, 1, :]
        dv(lambda h: h.tensor_tensor(out=S(SC0), in0=S(L_RE), in1=S(L_RE), op=ALU.mult), ["s5"], ["s5"])
        dv(lambda h: h.tensor_tensor(out=S(SC1), in0=S(L_IM), in1=S(L_IM), op=ALU.mult), ["s5"], ["s5"])
        s5_tt(SC0, SC0, SC1, ALU.add)
        dv(lambda h: h.reciprocal(out=S(SC0), in_=S(SC0)), ["s5"], ["s5"])
        dv(lambda h: h.tensor_scalar(out=S(SC1), in0=AR, scalar1=-1.0, scalar2=None, op0=ALU.add), ["s5", "apw"], ["s5"])
        s5_tt(C_R, SC1, L_RE, ALU.mult)
        dv(lambda h: h.tensor_tensor(out=S(SC2), in0=AI, in1=S(L_IM), op=ALU.mult), ["s5", "apw"], ["s5"])
        s5_tt(C_R, C_R, SC2, ALU.add)
        s5_tt(C_R, C_R, SC0, ALU.mult)
        dv(lambda h: h.tensor_tensor(out=S(C_I), in0=AI, in1=S(L_RE), op=ALU.mult), ["s5", "apw"], ["s5"])
        s5_tt(SC2, SC1, L_IM, ALU.mult)
        s5_tt(C_I, C_I, SC2, ALU.subtract)
        s5_tt(C_I, C_I, SC0, ALU.mult)
        dv(lambda h: h.tensor_copy(out=S(A16R), in_=AR), ["s5", "apw"], ["s5"])
        dv(lambda h: h.tensor_copy(out=S(A16I), in_=AI), ["s5", "apw"], ["s5"])
        for rr_, ii_, n_ in ((A16R, A16I, 4), (A256R, A256I, 4)):
            if rr_ == A256R:
                dv(lambda h: h.tensor_copy(out=S(A256R), in_=S(A16R)), ["s5"], ["s5"])
                dv(lambda h: h.tensor_copy(out=S(A256I), in_=S(A16I)), ["s5"], ["s5"])
            for _ in range(n_):
                s5_tt(SC0, rr_, rr_, ALU.mult)
                s5_tt(SC1, ii_, ii_, ALU.mult)
                s5_tt(SC2, rr_, ii_, ALU.mult)
                s5_tt(rr_, SC0, SC1, ALU.subtract)
                s5_ts(ii_, SC2, 2.0, ALU.mult)

    def s5_layer(l):
        dv(lambda h: h.memset(gtmp[:, 0, 0, 0:1], 0.0), [], ALLB + S5R + ["gtmp_dve"])
        AR, AI = apw[:, 0, 1, :], apw[:, 1, 1, :]
        XB = lambda i: gcur[:, 0, i // 2, i % 2, :].rearrange("p (a b) -> p a b", a=1)[:, 0, :]

        def sview(k):
            return bA[:, k * NT:k * NT + 2048].rearrange("p (i c) -> p i c", i=16)

        def zview(k):
            return bA[:, k * NT:k * NT + 2048].rearrange("p (c i) -> p i c", i=16)

        gflat = gcur[:].rearrange("p a b c d -> p (a b c d)")
        X0, X1, X2, X3 = (gflat[:, i * 128:(i + 1) * 128] for i in range(4))
        arWC0 = X2.bitcast(BF16).rearrange("p (r m) -> p r m", r=2)
        Bk_re, Bk_im = cview(gtmp[:, 0, 0, :]), cview(gtmp[:, 0, 1, :])

        wslots = {}
        qstate = {}

        def pool_prologue(k):
            for bc_, (dre, dim_) in enumerate([(Bre_d, Bim_d), (Cre_d, Cim_d)]):
                P.op("sp", lambda h, bc_=bc_, dre=dre: h.dma_start(out=BCk[:, bc_, 0, :], in_=dre[l, :, 64 * k:64 * k + 64]),
                     writes=["BCk%d" % bc_], dma_key="bck%d0" % bc_)
                P.op("sp", lambda h, bc_=bc_, dim_=dim_: h.dma_start(out=BCk[:, bc_, 1, :], in_=dim_[l, :, 64 * k:64 * k + 64]),
                     writes=["BCk%d" % bc_], dma_key="bck%d1" % bc_)
            if k not in wslots:
                wslots[k] = load_wtile(w_in[l, :, 2048 + k * 128:2048 + (k + 1) * 128])
            if k + 1 < 8:
                wslots[k + 1] = load_wtile(w_in[l, :, 2048 + (k + 1) * 128:2048 + (k + 2) * 128])
            expand("pool", arWC0[:, 0, :], cview(BCk[:, 1, 0, :]), 0, ["BCk1"], ["gX2"])
            expand("pool", arWC0[:, 1, :], cview(BCk[:, 1, 1, :]), 2, ["BCk1"], ["gX2"])
            cmul("pool", cview(smt[:, 0:64]), cview(smt[:, 64:128]), cview(BCk[:, 0, 0, :]), cview(BCk[:, 0, 1, :]),
                 bc4(S(C_R), k), bc4(S(C_I), k), ["smt", "BCk0"])
            expand("pool", X0, cview(smt[:, 0:64]), 0, ["smt"], ["gX"])
            expand("pool", X1, cview(smt[:, 64:128]), 0, ["smt"], ["gX"])

        def sproj(k):
            slot, sr = wslots.pop(k)
            for ti, (t0, tn) in enumerate(TT):
                bank = next_bank()
                for kk in range(8):
                    P.op("pe", lambda h, kk=kk, slot=slot, t0=t0, tn=tn, bank=bank: h.matmul(
                        pst[bank][:, 0:tn], lhsT=wts[:, slot, kk, :], rhs=bufk(hb, kk, t0, tn), start=(kk == 0), stop=(kk == 7)),
                        reads=[sr, R("h", kk, ti)], writes=["ps%d" % bank])
                if ti < 4:
                    P.op("act", lambda h, ti=ti, bank=bank: h.activation(out=sview(k)[:, :, 32 * ti:32 * ti + 32],
                                                                         in_=pst[bank][:, 0:512].rearrange("p (c i) -> p i c", i=16), func=AF.Copy),
                         reads=["ps%d" % bank], writes=[R("A", k, t_) for t_ in range(4)] + vtm_overlap(k * NT, k * NT + 2048))
                else:
                    P.op("act", lambda h, t0=t0, tn=tn, bank=bank: h.activation(out=bufk(bA, k, t0, tn), in_=pst[bank][:, 0:tn], func=AF.Copy),
                         reads=["ps%d" % bank], writes=[R("A", k, ti)] + vtm_overlap(k * NT + t0, k * NT + t0 + tn))

        def quarter_gen(k, qq):
            gi = ctr["gw"] % 2
            ctr["gw"] += 1
            arGW, gwr = arGWs[gi], "arGW%d" % gi
            qstate[(k, qq)] = [arGW, gwr, None]
            gen_quarter("pool" if qq == 2 else "dve", k, qq, X0, X1, X0, X1, arGW[:, :, 0, :], arGW[:, :, 1, :], ["gX"], [gwr, "arKTx"])

        def quarter_pe(k, qq):
            arGW, gwr, _ = qstate[(k, qq)]
            bt = next_bank()
            psT = pst[bt][:].bitcast(BF16)
            for t4 in range(4):
                for ri in range(2):
                    P.op("pe", lambda h, t4=t4, ri=ri: h.transpose(psT[:, (t4 * 2 + ri) * 128:(t4 * 2 + ri + 1) * 128], arGW[:, t4, ri, :], identb[:]),
                         reads=[gwr, "identb"], writes=["ps%d" % bt])
            P.op("act", lambda h: h.activation(out=arT[:, 4 * qq:4 * qq + 4, :, :].rearrange("p t r m -> p (t r m)"), in_=psT, func=AF.Copy),
                 reads=["ps%d" % bt], writes=["arT%d" % qq])
            bk = next_bank()
            qstate[(k, qq)][2] = bk
            for t4 in range(4):
                P.op("pe", lambda h, t4=t4: h.matmul(pst[bk][:, t4 * 128:(t4 + 1) * 128], lhsT=arGW[:, t4, 0, :], rhs=arWC0[:, 0, :], start=True, stop=False),
                     reads=[gwr, "gX2"], writes=["ps%d" % bk])
                P.op("pe", lambda h, t4=t4: h.matmul(pst[bk][:, t4 * 128:(t4 + 1) * 128], lhsT=arGW[:, t4, 1, :], rhs=arWC0[:, 1, :], start=False, stop=True),
                     reads=[gwr, "gX2"], writes=["ps%d" % bk])

        def quarter_evac(k, qq):
            bk = qstate.pop((k, qq))[2]
            dv(lambda h: h.tensor_tensor(out=pst[bk][:].rearrange("p (t m) -> p t m", t=4), in0=pst[bk][:].rearrange("p (t m) -> p t m", t=4),
                                         in1=bdm[:].unsqueeze(1).to_broadcast([128, 4, 128]), op=ALU.mult),
               ["ps%d" % bk, "bdm"], ["ps%d" % bk])
            if qq == 0:
                dv(lambda h: h.scalar_tensor_tensor(out=pst[bk][:, 0:128], in0=ident[:], scalar=vec_sb[:, l, 64 + k:65 + k], in1=pst[bk][:, 0:128],
                                                    op0=ALU.mult, op1=ALU.add), ["ps%d" % bk, "ident", "vec"], ["ps%d" % bk])
            dv(lambda h: h.tensor_reduce(out=smt[:, 256:320].rearrange("p (t h) -> p t h", t=4),
                                         in_=pst[bk][:].rearrange("p (t g h) -> p t h g", t=4, g=8), axis=AX.X, op=ALU.add),
               ["ps%d" % bk], ["smt2"])
            dv(lambda h: h.tensor_copy(out=KTc[:, k, 4 * qq:4 * qq + 4, :], in_=smt[:, 256:320].rearrange("p (t h) -> p t h", t=4)),
               ["smt2"], ["KTc"])

        def states(k):
            sb_ = [next_bank() for _ in range(4)]
            for jq in range(4):
                for q in range(4):
                    bank = sb_[q]
                    for j in range(4 * jq, 4 * jq + 4):
                        for ri in range(2):
                            first = (j == 0 and ri == 0)
                            P.op("pe", lambda h, q=q, ri=ri, j=j, bank=bank, first=first: h.matmul(
                                pst[bank][:, ri * 128:(ri + 1) * 128], lhsT=arT[32 * q:32 * q + 32, 15 - j, ri, :], rhs=sview(k)[32 * q:32 * q + 32, j, :],
                                start=first, stop=first, skip_group_check=(not first), tile_position=(32 * q, 0)),
                                reads=["arT%d" % (3 - jq)] + [R("A", k, t_) for t_ in range(4)], writes=["ps%d" % bank])
            for q in range(4):
                bank = sb_[q]
                for ri in range(2):
                    P.op("pe", lambda h, q=q, ri=ri, bank=bank: h.matmul(
                        pst[bank][:, 256 + ri * 16:256 + (ri + 1) * 16], lhsT=arT[32 * q:32 * q + 32, 0, ri, :], rhs=bA[32 * q:32 * q + 32, k * NT + 2048:k * NT + 2064],
                        start=False, stop=False, skip_group_check=True, tile_position=(32 * q, 0)),
                        reads=["arT0", R("A", k, 4)], writes=["ps%d" % bank])
                P.op("act", lambda h, q=q, bank=bank: h.activation(
                    out=Qv[:, 4 * k + q, :, :], in_=pst[bank][:, 0:256].rearrange("p (r c) -> p r c", r=2), func=AF.Copy),
                    reads=["ps%d" % bank], writes=["Qh%d" % (k // 4)])
                P.op("act", lambda h, q=q, bank=bank: h.activation(
                    out=Ssm[:, 4 * k + q, :, :], in_=pst[bank][:, 256:288].rearrange("p (r t) -> p r t", r=2), func=AF.Copy),
                    reads=["ps%d" % bank], writes=["msb", "rstd"])

        def sample_state(k):
            P.op("sp", lambda h: h.dma_start(out=h0k32[:].rearrange("p q r t -> p (q r t)"), in_=h0_d[l, :, 128 * k:128 * k + 128]), writes=["h0k32"], dma_key="h0k32")
            h0k = h0k32[:]
            t13 = smt[:, 0:128].rearrange("p (q r t) -> p q r t", q=4, r=2)
            t24 = smt[:, 128:256].rearrange("p (q r t) -> p q r t", q=4, r=2)
            arb = AR[:, 4 * k:4 * k + 4].unsqueeze(2).unsqueeze(3).to_broadcast([128, 4, 2, 16])
            aib = AI[:, 4 * k:4 * k + 4].unsqueeze(2).unsqueeze(3).to_broadcast([128, 4, 2, 16])
            Sk = Ssm[:, 4 * k:4 * k + 4, :, :]
            HN = ["msb", "rstd"]
            dv(lambda h: h.tensor_tensor(out=t13, in0=h0k, in1=arb, op=ALU.mult), ["h0k32", "apw"], ["smt"], "pool")
            dv(lambda h: h.tensor_tensor(out=t24, in0=h0k, in1=aib, op=ALU.mult), ["h0k32", "apw"], ["smt"], "pool")
            dv(lambda h: h.tensor_tensor(out=Sk, in0=Sk, in1=t13, op=ALU.add), ["smt"] + HN, HN, "pool")
            dv(lambda h: h.tensor_tensor(out=Sk[:, :, 0, :], in0=Sk[:, :, 0, :], in1=t24[:, :, 1, :], op=ALU.subtract), ["smt"] + HN, HN, "pool")
            dv(lambda h: h.tensor_tensor(out=Sk[:, :, 1, :], in0=Sk[:, :, 1, :], in1=t24[:, :, 0, :], op=ALU.add), ["smt"] + HN, HN, "pool")

        for k in range(8):
            pool_prologue(k)
            if k > 0:
                sample_state(k - 1)
            quarter_gen(k, 3)
            sproj(k)
            quarter_pe(k, 3)
            quarter_gen(k, 2)
            quarter_pe(k, 2)
            quarter_evac(k, 3)
            quarter_gen(k, 1)
            quarter_pe(k, 1)
            quarter_evac(k, 2)
            quarter_gen(k, 0)
            quarter_pe(k, 0)
            quarter_evac(k, 1)
            quarter_evac(k, 0)
            states(k)
        sample_state(7)
        P.op("sp", lambda h: h.dma_start(out=ss_o[l], in_=Ssm.rearrange("p g r t -> p (g r t)")), reads=["msb", "rstd"], writes=["ss_o%d" % l], dma_key="sso")

        mark('L%d chain' % l)
        for hf, eng in enumerate(["dve", "pool"]):
            gs = slice(16 * hf, 16 * hf + 16)
            base = tmp32[:].rearrange("p a n -> p (a n)") if hf == 0 else nrm[:].rearrange("p a n -> p (a n)")
            cr_ = ["tmp32_0", "tmp32_1"] if hf == 0 else ["msb", "rstd"]
            v4 = lambda t: t.rearrange("p (g r b) -> p g r b", g=16, r=2)
            Hs, T13, T24, Nn = (v4(base[:, i * 256:(i + 1) * 256]) for i in range(4))
            qr = "Qh%d" % hf
            Qb = Qv[:, gs, :, :].rearrange("p g r (b i) -> p g r b i", i=16)
            bc_ = lambda idx, shp: s5[:, idx, gs].unsqueeze(2).unsqueeze(3).to_broadcast(shp)
            ArB, AiB = bc_(A16R, [128, 16, 2, 8]), bc_(A16I, [128, 16, 2, 8])
            dv(lambda h, Hs=Hs: h.memset(Hs, 0.0), [], cr_, eng)
            for pas in range(2):
                for i in range(16):
                    Qc = Qb[:, :, :, :, i]
                    dv(lambda h, Hs=Hs, T13=T13, ArB=ArB: h.tensor_tensor(out=T13, in0=Hs, in1=ArB, op=ALU.mult), cr_ + ["s5"], cr_, eng)
                    dv(lambda h, Hs=Hs, T24=T24, AiB=AiB: h.tensor_tensor(out=T24, in0=Hs, in1=AiB, op=ALU.mult), cr_ + ["s5"], cr_, eng)
                    dv(lambda h, Nn=Nn, Qc=Qc, T13=T13: h.tensor_tensor(out=Nn, in0=Qc, in1=T13, op=ALU.add), cr_ + [qr], cr_, eng)
                    if pas == 1:
                        dv(lambda h, Qc=Qc, Hs=Hs: h.tensor_copy(out=Qc, in_=Hs), cr_, [qr], eng)
                    dv(lambda h, Hs=Hs, Nn=Nn, T24=T24: h.tensor_tensor(out=Hs[:, :, 0, :], in0=Nn[:, :, 0, :], in1=T24[:, :, 1, :], op=ALU.subtract), cr_, cr_, eng)
                    dv(lambda h, Hs=Hs, Nn=Nn, T24=T24: h.tensor_tensor(out=Hs[:, :, 1, :], in0=Nn[:, :, 1, :], in1=T24[:, :, 0, :], op=ALU.add), cr_, cr_, eng)
                if pas == 0:
                    Cc = T13[:, :, :, 0]
                    Ta = T13[:, :, :, 1]
                    Tb = T13[:, :, :, 2]
                    Tc = T13[:, :, :, 3]
                    A2r, A2i = (s5[:, idx, gs].unsqueeze(2).to_broadcast([128, 16, 2]) for idx in (A256R, A256I))
                    dv(lambda h, Cc=Cc: h.memset(Cc, 0.0), [], cr_, eng)
                    for b in range(8):
                        Lb = Hs[:, :, :, b]
                        dv(lambda h, Ta=Ta, Cc=Cc, A2r=A2r: h.tensor_tensor(out=Ta, in0=Cc, in1=A2r, op=ALU.mult), cr_ + ["s5"], cr_, eng)
                        dv(lambda h, Tb=Tb, Cc=Cc, A2i=A2i: h.tensor_tensor(out=Tb, in0=Cc, in1=A2i, op=ALU.mult), cr_ + ["s5"], cr_, eng)
                        dv(lambda h, Tc=Tc, Lb=Lb, Ta=Ta: h.tensor_tensor(out=Tc, in0=Lb, in1=Ta, op=ALU.add), cr_, cr_, eng)
                        dv(lambda h, Lb=Lb, Cc=Cc: h.tensor_copy(out=Lb, in_=Cc), cr_, cr_, eng)
                        dv(lambda h, Cc=Cc, Tc=Tc, Tb=Tb: h.tensor_tensor(out=Cc[:, :, 0], in0=Tc[:, :, 0], in1=Tb[:, :, 1], op=ALU.subtract), cr_, cr_, eng)
                        dv(lambda h, Cc=Cc, Tc=Tc, Tb=Tb: h.tensor_tensor(out=Cc[:, :, 1], in0=Tc[:, :, 1], in1=Tb[:, :, 0], op=ALU.add), cr_, cr_, eng)
                    dv(lambda h, hf=hf, Cc=Cc: h.tensor_copy(out=chn[:, hf, 0, :].rearrange("p (g r) -> p g r", r=2), in_=Cc), cr_, ["chn%d" % hf], eng)
                    P.op("sp", lambda h, hf=hf: h.dma_start(out=sp_o[l, :, 32 * hf:32 * hf + 32], in_=chn[:, hf, 0, :]), reads=["chn%d" % hf],
                         writes=["sp_o%d_%d" % (l, hf)], dma_key="spo%d" % hf)

        mark('L%d s5B' % l)
        for k in range(8):
            P.op("pool", lambda h, k=k: h.dma_start(out=h0kb[:].rearrange("p q r t -> p (q r t)"), in_=h0_d[l, :, 128 * k:128 * k + 128]), writes=["h0kb"], dma_key="h0kb")
            P.op("sp", lambda h, k=k: h.dma_start(out=BCk[:, 1, 0, :], in_=Cre_d[l, :, 64 * k:64 * k + 64]), writes=["BCk1"], dma_key="bck10")
            P.op("sp", lambda h, k=k: h.dma_start(out=BCk[:, 1, 1, :], in_=Cim_d[l, :, 64 * k:64 * k + 64]), writes=["BCk1"], dma_key="bck11")
            dv(lambda h, k=k: h.tensor_tensor(out=arKTx.rearrange("p t (g h) -> p t g h", g=8),
                                              in0=KTc[:, k, :, :].unsqueeze(2).to_broadcast([128, 16, 8, 16]),
                                              in1=bd8[:].unsqueeze(1).unsqueeze(3).to_broadcast([128, 16, 8, 16]), op=ALU.mult),
               ["KTc", "bd8"], ["arKTx", "arGW0", "arGW1"])
            cmul("pool", cview(smt[:, 0:64]), cview(smt[:, 64:128]), cview(BCk[:, 1, 0, :]), cview(BCk[:, 1, 1, :]),
                 bc4(AR, k), bc4(AI, k), ["smt", "BCk1"])
            expand("pool", X0, cview(smt[:, 0:64]), 0, ["smt"], ["gX"])
            expand("pool", X1, cview(smt[:, 64:128]), 0, ["smt"], ["gX"])
            expand("pool", X2, cview(smt[:, 0:64]), 2, ["smt"], ["gX", "gX2"])
            expand("pool", X3, cview(smt[:, 64:128]), 2, ["smt"], ["gX", "gX2"])
            for qq in range(4):
                gen_quarter("pool" if qq == 3 else "dve", k, qq, X0, X1, X2, X3, arT[:, 4 * qq:4 * qq + 4, 0, :], arT[:, 4 * qq:4 * qq + 4, 1, :], ["gX"], ["arT%d" % qq])
            yb = [next_bank() for _ in range(4)]
            ys = next_bank()
            AK = [R("A", k, t_) for t_ in range(4)]
            for b in range(4):
                for tau in range(4 * b + 4):
                    i_lo = max(tau, 4 * b)
                    ni = 4 * b + 4 - i_lo
                    P.op("pe", lambda h, k=k, b=b, tau=tau, i_lo=i_lo, ni=ni, bank=yb[b]: h.matmul(
                        pst[bank][:, (i_lo - 4 * b) * 128:512].rearrange("p (i c) -> p i c", c=128),
                        lhsT=arKTx[:, tau, :], rhs=sview(k)[:, i_lo - tau:i_lo - tau + ni, :], start=(tau == 0), stop=(tau == 0), skip_group_check=(tau > 0)),
                        reads=["arKTx"] + AK, writes=["ps%d" % yb[b]])
                for i4 in range(4):
                    for q in range(4):
                        for ri in range(2):
                            P.op("pe", lambda h, k=k, b=b, i4=i4, q=q, ri=ri, bank=yb[b]: h.matmul(
                                pst[bank][32 * q:32 * q + 32, i4 * 128:(i4 + 1) * 128], lhsT=arT[:, 4 * b + i4, ri, 32 * q:32 * q + 32],
                                rhs=Qv[:, 4 * k + q, ri, :], start=False, stop=False, skip_group_check=True, tile_position=(0, 32 * q)),
                                reads=["arT%d" % b, "Qh%d" % (k // 4)], writes=["ps%d" % yb[b]])
            P.op("pe", lambda h, k=k, ys=ys: h.matmul(pst[ys][:, 0:16], lhsT=arKTx[:, 0, :], rhs=bA[:, k * NT + 2048:k * NT + 2064], start=True, stop=True),
                 reads=["arKTx", R("A", k, 4)], writes=["ps%d" % ys])
            for q in range(4):
                for ri in range(2):
                    P.op("pe", lambda h, k=k, q=q, ri=ri, ys=ys: h.matmul(
                        pst[ys][32 * q:32 * q + 32, 0:16], lhsT=arT[:, 0, ri, 32 * q:32 * q + 32], rhs=h0kb[:, q, ri, :],
                        start=False, stop=False, skip_group_check=True, tile_position=(0, 32 * q)),
                        reads=["arT0", "h0kb"], writes=["ps%d" % ys])
            for b in range(4):
                P.op("act", lambda h, k=k, b=b, bank=yb[b]: h.activation(out=zview(k)[:, 4 * b:4 * b + 4, :], in_=pst[bank][:].rearrange("p (i c) -> p i c", c=128),
                                                                         func=AF.Gelu_apprx_tanh),
                     reads=["ps%d" % yb[b]], writes=AK)
            P.op("act", lambda h, k=k, ys=ys: h.activation(out=bufk(bA, k, 2048, 16), in_=pst[ys][:, 0:16], func=AF.Gelu_apprx_tanh),
                 reads=["ps%d" % ys], writes=[R("A", k, 4)])
        dv(lambda h: h.memset(gtmp[:, 0, 0, 0:1], 0.0), [], ALLB + S5R + ["gtmp_dve"])

    def layer(l):
        mark('L%d ada' % l)
        s5_tables(l)
        ada_mod(l)
        mark('L%d norm1' % l)
        norm_fm(lambda k, smp: (modA[:, 0, 0, k, 0:1] if smp == 0 else modA[:, 0, 0, k, 1:17]),
                lambda k, smp: (mod[:, 0, 0 + k, 0:1] if smp == 0 else mod[:, 0, 0 + k, 1:17]), hb, "h")

        mark('L%d v' % l)
        TBALL = ["tb0", "tb1", "tb2", "tb3"]
        VG = ["msb", "rstd"]
        VSQ = ["tmp32_0", "tmp32_1"]
        P.op("pool", lambda h: h.dma_start(out=wbig, in_=w_in[l, :, 1024:2048].rearrange("(k p) n -> p k n", p=128)),
             writes=["wbig"] + B_overlap(0, 8192), dma_key="wbig")
        P.op("sp", lambda h: h.dma_start(out=gvb_sb[:], in_=gvb[l]), writes=TBALL, dma_key="gvb")
        P.op("sp", lambda h: h.dma_start(out=wsT32, in_=wsT[l].rearrange("h s t -> s h t")), writes=["sq0", "sq1"], dma_key="wsT32")
        P.op("pool", lambda h: h.dma_start(out=bspb[:], in_=bsp[l]), writes=["bspb"], dma_key="bspb")
        P.op("sp", lambda h: h.dma_start(out=w00_sb[:], in_=w00[l]), writes=["w00"], dma_key="w00")
        P.op("pool", lambda h: h.dma_start(out=b00b[:], in_=b00[l]), writes=["b00b"], dma_key="b00b")
        P.op("dve", lambda h: h.tensor_tensor(out=wsTb[:], in0=wsT32, in1=triu[:].unsqueeze(1).to_broadcast([128, 4, 128]), op=ALU.mult),
             reads=["sq0", "sq1", "triu"], writes=["wsTb"])
        for hh in range(4):
            P.op("dve", lambda h, hh=hh: h.tensor_scalar(out=WI[:, hh, :], in0=ident[0:16, 0:16], scalar1=w00_sb[:, hh:hh + 1], scalar2=None, op0=ALU.mult),
                 reads=["ident", "w00"], writes=["WI"])
        VGB = [(nrm[:].rearrange("p a n -> p (a n)"), ["msb", "rstd"]), (tmp32[:].rearrange("p a n -> p (a n)"), ["tmp32_0", "tmp32_1"])]
        vjunk = sq[:].rearrange("p a n -> p (a n)")
        for i in range(17):
            rows = 128 if i < 16 else 16
            tok0 = i * 128
            ti = min(i // 4, 4)
            vgt, VG = VGB[i % 2]
            banks = [next_bank(), next_bank()]
            for cb in range(2):
                for k in range(8):
                    P.op("pe", lambda h, k=k, cb=cb, tok0=tok0, rows=rows, bank=banks[cb]: h.matmul(
                        pst[bank][0:rows, :], lhsT=hb[:, k * NT + tok0:k * NT + tok0 + rows], rhs=wbig[:, k, cb * 512:(cb + 1) * 512],
                        start=(k == 0), stop=(k == 7)), reads=["wbig", R("h", k, ti)], writes=["ps%d" % banks[cb]])
                P.op("act", lambda h, cb=cb, rows=rows, bank=banks[cb], vgt=vgt: h.activation(out=vgt[0:rows, cb * 512:(cb + 1) * 512], in_=pst[bank][0:rows, :],
                                                                                           func=AF.Gelu_apprx_tanh),
                     reads=["ps%d" % banks[cb]], writes=[VG[cb]])
            P.op("act", lambda h, rows=rows, vgt=vgt: h.activation(out=vjunk[0:rows, :], in_=vgt[0:rows, :], func=AF.Square, accum_out=vss[0:rows, 0:1]),
                 reads=VG + ["vss"], writes=["sq0", "sq1", "vss"])
            P.op("dve", lambda h, rows=rows: h.tensor_scalar(out=vss[0:rows, 0:1], in0=vss[0:rows, 0:1], scalar1=1.0 / 1024.0, scalar2=EPS,
                                                           op0=ALU.mult, op1=ALU.add), reads=["vss"], writes=["vss"])
            P.op("act", lambda h, rows=rows: h.activation(out=vss[0:rows, 1:2], in_=vss[0:rows, 0:1], func=AF.Sqrt),
                 reads=["vss"], writes=["vrs"])
            P.op("dve", lambda h, rows=rows: h.reciprocal(out=vss[0:rows, 1:2], in_=vss[0:rows, 1:2]),
                 reads=["vrs"], writes=["vrs"])
            if i < 16:
                P.op("dve", lambda h, i=i, vgt=vgt: h.scalar_tensor_tensor(out=bA[:, i * 1024:(i + 1) * 1024], in0=vgt[:, :], scalar=vss[:, 1:2],
                                                                        in1=gvb_sb[:], op0=ALU.mult, op1=ALU.mult),
                     reads=VG + ["vrs"] + TBALL, writes=["vtm%d" % i] + A_overlap(i * 1024, (i + 1) * 1024))
            else:
                vs32t, VS = VGB[(i + 1) % 2]
                vs32v = vs32t[0:16, :]
                P.op("dve", lambda h, vgt=vgt: h.scalar_tensor_tensor(out=vs32v, in0=vgt[0:16, :], scalar=vss[0:16, 1:2],
                                                                   in1=gvb_sb[0:16, :], op0=ALU.mult, op1=ALU.mult),
                     reads=VG + ["vrs"] + TBALL, writes=VS)
                P.op("dve", lambda h: h.tensor_copy(out=vs_bf[:], in_=vs32v), reads=VS, writes=["vs_bf"])
                P.op("sp", lambda h: h.dma_start(out=gv_o[l], in_=vs32v), reads=VS, writes=["gv_o%d" % l], dma_key="gvst")

        mark('L%d gmlp' % l)
        def gmlp_evac(m, ti, t0, tn, banks):
            hh = m // 2
            mb = next_bank()
            if ti < 4:
                for c in range(4):
                    ch = ti * 4 + c
                    P.op("pe", lambda h, c=c, ch=ch: h.matmul(pst[mb][:, c * 128:(c + 1) * 128], lhsT=bA[:, ch * 1024 + m * 128:ch * 1024 + (m + 1) * 128],
                                                              rhs=wsTb[:, hh, :], start=True, stop=False),
                         reads=["vtm%d" % ch, "wsTb"], writes=["ps%d" % mb])
                    P.op("pe", lambda h, c=c: h.matmul(pst[mb][:, c * 128:(c + 1) * 128], lhsT=ones_r[0:1, :], rhs=bspb[0:1, hh * 128:(hh + 1) * 128],
                                                       start=False, stop=True),
                         reads=["ones_r", "bspb"], writes=["ps%d" % mb])
            else:
                P.op("pe", lambda h: h.matmul(pst[mb][:, 0:16], lhsT=vs_bf[0:16, m * 128:(m + 1) * 128], rhs=WI[0:16, hh, :], start=True, stop=False),
                     reads=["vs_bf", "WI"], writes=["ps%d" % mb])
                P.op("pe", lambda h: h.matmul(pst[mb][:, 0:16], lhsT=ones_r[0:1, :], rhs=b00b[0:1, hh * 16:(hh + 1) * 16], start=False, stop=True),
                     reads=["ones_r", "b00b"], writes=["ps%d" % mb])
            a, b = uid() % 4, uid() % 4
            if b == a:
                b = (a + 1) % 4
            P.op("act", lambda h: h.activation(out=tb[:, a, 0:tn], in_=pst[banks[0]][:, 0:tn], func=AF.Gelu_apprx_tanh),
                 reads=["ps%d" % banks[0]], writes=["tb%d" % a])
            P.op("act", lambda h: h.activation(out=tb[:, b, 0:tn], in_=pst[banks[1]][:, 0:tn], func=AF.Sigmoid),
                 reads=["ps%d" % banks[1]], writes=["tb%d" % b])
            P.op("dve", lambda h: h.tensor_tensor(out=tb[:, a, 0:tn], in0=pst[mb][:, 0:tn], in1=tb[:, a, 0:tn], op=ALU.mult),
                 reads=["ps%d" % mb, "tb%d" % a], writes=["tb%d" % a])
            P.op("dve", lambda h: h.tensor_tensor(out=bufk(bB, m, t0, tn), in0=tb[:, a, 0:tn], in1=tb[:, b, 0:tn], op=ALU.mult),
                 reads=["tb%d" % a, "tb%d" % b], writes=[R("B", m, ti)] + (["wbig"] if m * NT + t0 < 8192 else []))

        multi_proj([(lambda m: w_in[l, :, m * 128:(m + 1) * 128], hb, "h", 8),
                    (lambda m: w_in[l, :, 3072 + m * 128:3072 + (m + 1) * 128], hb, "h", 8)], 8, gmlp_evac)

        mark('L%d wout1' % l)
        multi_proj([(lambda m: w_out[l, :, m * 128:(m + 1) * 128], bB, "B", 8)], 8, resid_evac(l, 2))

        mark('L%d s5A' % l)
        if WITH_S5:
            s5_layer(l)
        else:
            for k in range(8):
                for ti, (t0, tn) in enumerate(TT):
                    P.op("pool", lambda h, k=k, t0=t0, tn=tn: h.memset(bufk(bA, k, t0, tn), 0.0),
                         writes=[R("A", k, ti)] + vtm_overlap(k * NT + t0, k * NT + t0 + tn))

        mark('L%d glu' % l)
        def glu_evac(m, ti, t0, tn, banks):
            a, b = uid() % 4, uid() % 4
            if b == a:
                b = (a + 1) % 4
            P.op("act", lambda h: h.activation(out=tb[:, a, 0:tn], in_=pst[banks[0]][:, 0:tn], func=AF.Sigmoid, bias=vec_sb[:, l, 72 + m:73 + m]),
                 reads=["ps%d" % banks[0], "vec"], writes=["tb%d" % a])
            P.op("act", lambda h: h.activation(out=tb[:, b, 0:tn], in_=pst[banks[1]][:, 0:tn], func=AF.Sigmoid),
                 reads=["ps%d" % banks[1]], writes=["tb%d" % b])
            P.op("dve", lambda h: h.tensor_tensor(out=tb[:, a, 0:tn], in0=bufk(bA, m, t0, tn), in1=tb[:, a, 0:tn], op=ALU.mult),
                 reads=[R("A", m, ti), "tb%d" % a], writes=["tb%d" % a])
            P.op("dve", lambda h: h.tensor_tensor(out=bufk(bB, m, t0, tn), in0=tb[:, a, 0:tn], in1=tb[:, b, 0:tn], op=ALU.mult),
                 reads=["tb%d" % a, "tb%d" % b], writes=[R("B", m, ti)])

        multi_proj([(lambda m: w_glu[l, :, m * 128:(m + 1) * 128], bA, "A", 8),
                    (lambda m: w_in[l, :, 4096 + m * 128:4096 + (m + 1) * 128], hb, "h", 8)], 8, glu_evac)
        multi_proj([(lambda m: w_out[l, :, m * 128:(m + 1) * 128], bB, "B", 8)], 8, resid_evac(l, 2))

        mark('L%d ffn' % l)
        norm_fm(lambda k, smp: (modA[:, 0, 1, k, 0:1] if smp == 0 else modA[:, 0, 1, k, 1:17]),
                lambda k, smp: (mod[:, 0, 24 + k, 0:1] if smp == 0 else mod[:, 0, 24 + k, 1:17]), hb, "h")
        def ff1(g):
            fbuf, fpref = (bA, "A") if g % 2 == 0 else (bB, "B")

            def ff1_evac(m, ti, t0, tn, banks):
                a = uid() % 4
                P.op("act", lambda h: h.activation(out=tb[:, a, 0:tn], in_=pst[banks[0]][:, 0:tn], func=AF.Relu),
                     reads=["ps%d" % banks[0]], writes=["tb%d" % a])
                P.op("dve", lambda h: h.tensor_tensor(out=bufk(fbuf, m, t0, tn), in0=tb[:, a, 0:tn], in1=tb[:, a, 0:tn], op=ALU.mult),
                     reads=["tb%d" % a], writes=[R(fpref, m, ti)])

            multi_proj([(lambda m: w_ff1[l, :, g * 1024 + m * 128:g * 1024 + (m + 1) * 128], hb, "h", 8)], 8, ff1_evac)

        def ff2(g):
            fbuf, fpref = (bA, "A") if g % 2 == 0 else (bB, "B")
            multi_proj([(lambda m: w_ff2[l, g * 1024:(g + 1) * 1024, m * 128:(m + 1) * 128], fbuf, fpref, 8)], 8, resid_evac(l, 5))

        ff1(0)
        for g in range(4):
            if g + 1 < 4:
                ff1(g + 1)
            ff2(g)

    for l_ in range(DEPTH):
        layer(l_)

    mark('final')
    norm_fm(None, None, None, None, final=True)

    P.finalize()
    st.close()
    return nc


_NC_CACHE = {}
PHASES = []


def _host_inputs(inp):
    f = np.float32
    g = lambda n: np.asarray(inp[n], dtype=f)
    x_prompt, x_sample = g("x_prompt"), g("x_sample")
    c_prompt, c_sample = g("c_prompt"), g("c_sample")
    b_ada, g1, g2, dsk, bglu = g("b_ada"), g("g_norm1"), g("g_norm2"), g("d_skip"), g("b_glu")
    vecs = np.zeros((DEPTH, 128, 80), f)
    for l in range(DEPTH):
        vecs[l, :, 0:48] = b_ada[l].reshape(48, 128).T
        vecs[l, :, 48:56] = g1[l].reshape(8, 128).T
        vecs[l, :, 56:64] = g2[l].reshape(8, 128).T
        vecs[l, :, 64:72] = dsk[l].reshape(8, 128).T
        vecs[l, :, 72:80] = bglu[l].reshape(8, 128).T
    gfin = np.ascontiguousarray(g("g_final").reshape(8, 128).T)
    gvb = np.ascontiguousarray(np.broadcast_to(g("g_v")[:, None, :], (DEPTH, 128, D)))
    ws = g("w_spatial")
    wsT = np.ascontiguousarray(np.transpose(ws, (0, 1, 3, 2)))
    bs = g("b_spatial")
    bsp = np.ascontiguousarray(bs.reshape(DEPTH, 1, 512))
    w00 = np.ascontiguousarray(np.broadcast_to(ws[:, None, :, 0, 0], (DEPTH, 16, 4)))
    b00 = np.ascontiguousarray(np.broadcast_to(bs[:, :, 0:1], (DEPTH, 4, 16)).reshape(DEPTH, 1, 64))
    lam_re, lam_im, log_dt = g("lam_re"), g("lam_im"), g("log_dt")
    sm2 = lambda a: np.ascontiguousarray(a.reshape(DEPTH, 32, 2, 64).transpose(0, 2, 3, 1).reshape(DEPTH, 128, 32))
    lamre, lamim = sm2(lam_re), sm2(lam_im)
    ldt = np.ascontiguousarray(np.broadcast_to(log_dt.reshape(DEPTH, 32, 2).transpose(0, 2, 1)[:, :, None, :], (DEPTH, 2, 64, 32)).reshape(DEPTH, 128, 32))
    smB = lambda a: np.ascontiguousarray(a.reshape(DEPTH, 32, 2, 64, 16).transpose(0, 2, 3, 1, 4).reshape(DEPTH, 128, 512))
    smC = lambda a: np.ascontiguousarray(a.reshape(DEPTH, 32, 2, 16, 64).transpose(0, 2, 4, 1, 3).reshape(DEPTH, 128, 512))
    msm = np.zeros((128, 4), f)
    msm[0:64, 0] = 1; msm[64:128, 1] = 1; msm[:, 2:4] = -msm[:, 0:2]
    ii = np.arange(128)
    bdm = (ii[:, None] // 16 == ii[None, :] // 16).astype(f)
    bd8 = (ii[:, None] // 16 == np.arange(8)[None, :]).astype(f)
    sre, sim = g("state_ssm_re"), g("state_ssm_im")
    shared = dict(lamre=lamre, lamim=lamim, ldt=ldt, Bre=smB(g("b_re")), Bim=smB(g("b_im")), Cre=smC(g("c_re")), Cim=smC(g("c_im")),
                  msm=msm, bdm=bdm, bd8=bd8, tauv=np.ascontiguousarray(np.broadcast_to(np.arange(16, dtype=f)[None, :], (128, 16))), vecs=vecs, gfin=gfin, gvb=gvb, wsT=wsT, bsp=bsp, w00=w00, b00=b00,
                  ident=np.eye(128, dtype=f), triu=np.triu(np.ones((128, 128), f)),
                  w_ada=g("w_ada"), w_in=g("w_in"), w_glu=g("w_glu"), w_out=g("w_out"), w_ff1=g("w_ff1"), w_ff2=g("w_ff2"))
    maps = []
    for c in range(8):
        xs = x_sample[16 * c:16 * c + 16, 0, :]
        xT = np.ascontiguousarray(np.concatenate([x_prompt[c], xs], axis=0).T)
        cT = np.ascontiguousarray(np.concatenate([c_prompt[c:c + 1], c_sample[16 * c:16 * c + 16]], axis=0).T)
        hre = sre[:, 16 * c:16 * c + 16].reshape(DEPTH, 16, 32, 2, 64).transpose(0, 3, 4, 2, 1)
        him = sim[:, 16 * c:16 * c + 16].reshape(DEPTH, 16, 32, 2, 64).transpose(0, 3, 4, 2, 1)
        h0 = np.ascontiguousarray(np.stack([hre, him], axis=4).reshape(DEPTH, 128, 1024))
        m = dict(shared)
        m.update(xT=xT, cT=cT, h0=h0)
        maps.append(m)
    return maps


def kernel(**inputs):
    if "nc" not in _NC_CACHE:
        _NC_CACHE["nc"] = build_nc()
    nc = _NC_CACHE["nc"]
    maps = _host_inputs(inputs)
    res = run_bass_kernel_spmd(nc, maps, core_ids=list(range(8)))
    rs = res.results
    y_prompt = np.stack([rs[c]["yT"][:, :2048].T for c in range(8)], axis=0)
    y_sample = np.concatenate([rs[c]["yT"][:, 2048:].T for c in range(8)], axis=0)[:, None, :]
    gv = np.concatenate([rs[c]["gv_o"] for c in range(8)], axis=1)[:, :, None, :]
    sp = np.stack([rs[c]["sp_o"].reshape(DEPTH, 2, 64, 32, 2) for c in range(8)], axis=1)
    sp = sp.transpose(0, 1, 5, 4, 2, 3).reshape(DEPTH, 8, 2, 64, 64)
    ss = np.stack([rs[c]["ss_o"].reshape(DEPTH, 2, 64, 32, 2, 16) for c in range(8)], axis=1)
    ss = ss.transpose(0, 5, 1, 6, 4, 2, 3).reshape(DEPTH, 2, 128, 64, 64)
    return (np.ascontiguousarray(y_prompt), np.ascontiguousarray(y_sample),
            np.ascontiguousarray(sp[:, :, 0]), np.ascontiguousarray(sp[:, :, 1]),
            np.ascontiguousarray(ss[:, 0]), np.ascontiguousarray(ss[:, 1]),
            np.ascontiguousarray(gv))
```

```python
import contextlib
import numpy as np
import concourse.bass as bass
import concourse.mybir as mybir
from concourse.bass_utils import run_bass_kernel_spmd

F32 = mybir.dt.float32
BF16 = mybir.dt.bfloat16
I32 = mybir.dt.int32
AF = mybir.ActivationFunctionType
ALU = mybir.AluOpType
AX = mybir.AxisListType

ENGS = ["pe", "act", "dve", "pool", "sp"]
D = 1024
NT = 2064
TT = [(0, 512), (512, 512), (1024, 512), (1536, 512), (2048, 16)]
DEPTH = 2
EPS = 1e-6
WITH_S5 = True
S5_STAGE = 3


PHASES = []


class Prog:
    def __init__(self, nc):
        self.nc = nc
        self.ops = {e: [] for e in ENGS}
        self.last_w = {}
        self.readers = {}
        self.dma_cnt = {}

    def op(self, eng, fn, reads=(), writes=(), dma_key=None):
        idx = len(self.ops[eng])
        deps = set()
        for r in reads:
            w = self.last_w.get(r)
            if w is not None:
                deps.add(w)
        for r in writes:
            w = self.last_w.get(r)
            if w is not None:
                deps.add(w)
            for rd in self.readers.get(r, {}).values():
                deps.add(rd)
        rec = dict(fn=fn, deps=deps, sig=False, dma_key=dma_key, dma_count=None)
        if dma_key is not None:
            c = self.dma_cnt.get(dma_key, 0) + 16
            self.dma_cnt[dma_key] = c
            rec["dma_count"] = c
            me = ("dma", dma_key, c)
        else:
            me = ("eng", eng, idx)
        for r in writes:
            self.last_w[r] = me
            self.readers[r] = {}
        for r in reads:
            self.readers.setdefault(r, {})[eng if dma_key is None else ("dma", dma_key)] = me
        self.ops[eng].append(rec)
        return me

    def finalize(self, final_eng="sp"):
        nc = self.nc
        for e in ENGS:
            for idx, rec in enumerate(self.ops[e]):
                keep = set()
                for d in rec["deps"]:
                    if d[0] == "eng":
                        _, f, j = d
                        if f == e and e in ("pe", "sp"):
                            continue
                        self.ops[f][j]["sig"] = True
                    keep.add(d)
                rec["deps"] = keep
        sigcount = {}
        for e in ENGS:
            c = 0
            arr = []
            for rec in self.ops[e]:
                if rec["sig"]:
                    c += 1
                arr.append(c)
            sigcount[e] = arr
        stack = contextlib.ExitStack()
        sems = {e: stack.enter_context(nc.semaphore("s_" + e)) for e in ENGS}
        dsems = {k: stack.enter_context(nc.semaphore("d_%d" % i)) for i, k in enumerate(self.dma_cnt)}
        block = stack.enter_context(nc.Block())
        final_waits = list(self.dma_cnt.items())

        def emit(e, h):
            known = {}
            for idx, rec in enumerate(self.ops[e]):
                need = {}
                for d in rec["deps"]:
                    if d[0] == "eng":
                        key = ("e", d[1])
                        val = sigcount[d[1]][d[2]]
                    else:
                        key = ("d", d[1])
                        val = d[2]
                    if val > need.get(key, 0):
                        need[key] = val
                for key, val in need.items():
                    if val > known.get(key, 0):
                        s = sems[key[1]] if key[0] == "e" else dsems[key[1]]
                        h.wait_ge(s, val)
                        known[key] = val
                ins = rec["fn"](h)
                if rec["dma_key"] is not None:
                    ins.then_inc(dsems[rec["dma_key"]], 16)
                elif rec["sig"]:
                    ins.then_inc(sems[e], 1)
            if e == final_eng:
                for k, c in final_waits:
                    h.wait_ge(dsems[k], c)

        @block.tensor
        def _(h):
            emit("pe", h)

        @block.scalar
        def _(h):
            emit("act", h)

        @block.vector
        def _(h):
            emit("dve", h)

        @block.gpsimd
        def _(h):
            emit("pool", h)

        @block.sync
        def _(h):
            emit("sp", h)

        stack.close()


def build_nc():
    nc = bass.Bass("TRN2", target_bir_lowering=False)
    st = contextlib.ExitStack()

    def din(name, shape):
        return nc.dram_tensor(name, list(shape), F32, kind="ExternalInput").ap()

    def dout(name, shape):
        return nc.dram_tensor(name, list(shape), F32, kind="ExternalOutput").ap()

    xT = din("xT", [D, NT])
    cT = din("cT", [D, 17])
    vecs = din("vecs", [DEPTH, 128, 80])
    gfin = din("gfin", [128, 8])
    gvb = din("gvb", [DEPTH, 128, D])
    wsT = din("wsT", [DEPTH, 4, 128, 128])
    bsp = din("bsp", [DEPTH, 1, 512])
    w00 = din("w00", [DEPTH, 16, 4])
    b00 = din("b00", [DEPTH, 1, 64])
    ident_d = din("ident", [128, 128])
    triu_d = din("triu", [128, 128])
    w_ada = din("w_ada", [DEPTH, D, 6 * D])
    w_in = din("w_in", [DEPTH, D, 5 * D])
    w_glu = din("w_glu", [DEPTH, D, D])
    w_out = din("w_out", [DEPTH, D, D])
    w_ff1 = din("w_ff1", [DEPTH, D, 4 * D])
    w_ff2 = din("w_ff2", [DEPTH, 4 * D, D])
    lamre_d = din("lamre", [DEPTH, 128, 32])
    lamim_d = din("lamim", [DEPTH, 128, 32])
    ldt_d = din("ldt", [DEPTH, 128, 32])
    Bre_d = din("Bre", [DEPTH, 128, 512])
    Bim_d = din("Bim", [DEPTH, 128, 512])
    Cre_d = din("Cre", [DEPTH, 128, 512])
    Cim_d = din("Cim", [DEPTH, 128, 512])
    msm_d = din("msm", [128, 4])
    bdm_d = din("bdm", [128, 128])
    bd8_d = din("bd8", [128, 8])
    tauv_d = din("tauv", [128, 16])
    h0_d = din("h0", [DEPTH, 128, 1024])
    sp_o = dout("sp_o", [DEPTH, 128, 64])
    ss_o = dout("ss_o", [DEPTH, 128, 1024])
    yT = dout("yT", [D, NT])
    gv_o = dout("gv_o", [DEPTH, 16, D])

    def sb(name, shape, dt):
        return st.enter_context(nc.sbuf_tensor("sb_" + name, list(shape), dt))

    x32 = sb("x32", [128, 8 * NT], F32)
    hb = sb("hb", [128, 8 * NT], BF16)
    bA = sb("bA", [128, 8 * NT], BF16)
    bB = sb("bB", [128, 8 * NT], BF16)
    NWS = 4
    wts = sb("wts", [128, NWS, 8, 128], BF16)
    wbig = bB[:, 0:8192].rearrange("p (k n) -> p k n", n=1024)
    ident = sb("ident", [128, 128], F32)
    identb = sb("identb", [128, 128], BF16)
    triu = sb("triu", [128, 128], F32)
    ones_m = sb("ones_m", [128, 128], BF16)
    ones_r = sb("ones_r", [1, 128], BF16)
    epsb = sb("epsb", [128, 1], F32)
    vec_sb = sb("vec_sb", [128, DEPTH, 80], F32)
    gfin_sb = sb("gfin_sb", [128, 8], F32)
    c_sb = sb("c_sb", [128, 8, 17], F32)
    cs_bf = sb("cs_bf", [128, 8, 17], BF16)
    mod = sb("mod", [128, 1, 48, 17], F32)
    modA = sb("modA", [128, 1, 2, 8, 17], F32)
    sq = sb("sq", [128, 2, 512], BF16)
    nrm = sb("nrm", [128, 2, 512], F32)
    tmp32 = sb("tmp32", [128, 2, 512], F32)
    tbf = sb("tbf", [128, 1024], F32)
    tb = tbf[:].bitcast(BF16).rearrange("p (a n) -> p a n", n=512)
    vg = nrm[:].rearrange("p a n -> p (a n)")
    vsq = tmp32[:].rearrange("p a n -> p (a n)")
    vs32 = vsq[0:16, :]
    vss = sb("vss", [128, 2], F32)
    gvb_sb = tbf
    vs_bf = sb("vs_bf", [16, D], BF16)
    wsT32 = sq[:].rearrange("p a n -> p (a n)").bitcast(F32).rearrange("p (h t) -> p h t", t=128)
    wsTb = sb("wsTb", [128, 4, 128], BF16)
    bspb = sb("bspb", [1, 512], BF16)
    w00_sb = sb("w00_sb", [16, 4], F32)
    WI = sb("WI", [16, 4, 16], BF16)
    b00b = sb("b00b", [1, 64], BF16)
    ystage = tbf[:].rearrange("p (a n) -> p a n", n=512)
    s5 = sb("s5", [128, 14, 32], F32)
    apw = sb("apw", [128, 2, 16, 32], F32)
    tauv = sb("tauv", [128, 16], F32)
    smt = sb("smt", [128, 320], F32)
    BCk = sb("BCk", [128, 2, 2, 64], F32)
    gcur = sb("gcur", [128, 2, 2, 2, 64], F32)
    gtmp = sb("gtmp", [128, 2, 2, 64], F32)
    msm = sb("msm", [128, 4], F32)
    bdm = sb("bdm", [128, 128], F32)
    bd8 = sb("bd8", [128, 8], F32)
    chn = sb("chn", [128, 2, 1, 32], F32)
    Qv = bB[:, 0:8192].rearrange("p (g r c) -> p g r c", g=32, r=2)
    KTc = bB[:, 8192:10240].rearrange("p (k t h) -> p k t h", k=8, t=16)
    arT = bB[:, 10240:14336].rearrange("p (t r m) -> p t r m", t=16, r=2)
    arGWs = [bB[:, 14336 + 1024 * i:15360 + 1024 * i].rearrange("p (t r m) -> p t r m", t=4, r=2) for i in range(2)]
    arKTx = bB[:, 14336:16384].rearrange("p (t m) -> p t m", t=16)
    h0f = tbf[:].rearrange("p (g r t) -> p g r t", g=32, r=2)
    h0b = sq[:].rearrange("p a n -> p (a n)").rearrange("p (g r t) -> p g r t", g=32, r=2)
    Ssm = nrm[:].rearrange("p a n -> p (a n)").rearrange("p (g r t) -> p g r t", g=32, r=2)
    pst = [st.enter_context(nc.psum_tensor("ps%d" % i, [128, 512], F32)) for i in range(8)]

    P = Prog(nc)
    PHASES.clear()

    def mark(name):
        PHASES.append((name, len(P.ops['pe'])))
    ctr = {"ws": 0, "ps": 0, "u": 0, "gw": 0}

    def xk(k, t0, tn):
        return x32[:, k * NT + t0:k * NT + t0 + tn]

    def bufk(b, k, t0, tn):
        return b[:, k * NT + t0:k * NT + t0 + tn]

    def R(pref, k, ti):
        return "%s%d_%d" % (pref, k, ti)

    def A_overlap(lo, hi):
        out = []
        for kk in range(8):
            for ti_, (t0_, tn_) in enumerate(TT):
                a0 = kk * NT + t0_
                if a0 < hi and a0 + tn_ > lo:
                    out.append(R("A", kk, ti_))
        return out

    def B_overlap(lo, hi):
        out = []
        for kk in range(8):
            for ti_, (t0_, tn_) in enumerate(TT):
                a0 = kk * NT + t0_
                if a0 < hi and a0 + tn_ > lo:
                    out.append(R("B", kk, ti_))
        return out

    def vtm_overlap(lo, hi):
        return ["vtm%d" % i_ for i_ in range(16) if i_ * 1024 < hi and (i_ + 1) * 1024 > lo]

    def next_bank(lo=0, hi=8):
        b = lo + ctr["ps"] % (hi - lo)
        ctr["ps"] += 1
        return b

    def uid():
        ctr["u"] += 1
        return ctr["u"]

    P.op("sp", lambda h: h.dma_start(out=ident[:], in_=ident_d), writes=["ident"], dma_key="ident")
    P.op("sp", lambda h: h.dma_start(out=triu[:], in_=triu_d), writes=["triu"], dma_key="triu")
    P.op("sp", lambda h: h.dma_start(out=vec_sb[:], in_=vecs.rearrange("l p n -> p l n")), writes=["vec"], dma_key="vec")
    P.op("sp", lambda h: h.dma_start(out=gfin_sb[:], in_=gfin), writes=["gfin"], dma_key="gfin")
    P.op("sp", lambda h: h.dma_start(out=c_sb[:], in_=cT.rearrange("(k p) n -> p k n", p=128)), writes=["c_sb"], dma_key="c_sb")
    P.op("sp", lambda h: h.dma_start(out=tauv[:], in_=tauv_d), writes=["tauv"], dma_key="tauv")
    P.op("sp", lambda h: h.dma_start(out=msm[:], in_=msm_d), writes=["msm"], dma_key="msm")
    P.op("sp", lambda h: h.dma_start(out=bdm[:], in_=bdm_d), writes=["bdm"], dma_key="bdm")
    P.op("sp", lambda h: h.dma_start(out=bd8[:], in_=bd8_d), writes=["bd8"], dma_key="bd8")
    P.op("dve", lambda h: h.tensor_copy(out=identb[:], in_=ident[:]), reads=["ident"], writes=["identb"])
    P.op("dve", lambda h: h.memset(ones_m[:], 1.0 / 1024.0), writes=["ones_m"])
    P.op("dve", lambda h: h.memset(ones_r[:], 1.0), writes=["ones_r"])
    P.op("dve", lambda h: h.memset(epsb[:], EPS), writes=["epsb"])
    for k in range(8):
        for ti, (t0, tn) in enumerate(TT):
            P.op("sp", lambda h, k=k, t0=t0, tn=tn: h.dma_start(out=xk(k, t0, tn), in_=xT[k * 128:(k + 1) * 128, t0:t0 + tn]),
                 writes=[R("x", k, ti)], dma_key="xl%d_%d" % (k, ti))
    P.op("act", lambda h: h.activation(out=cs_bf[:], in_=c_sb[:], func=AF.Silu), reads=["c_sb"], writes=["cs_bf"])

    def load_wtile(wd_ap, K=8):
        s = ctr["ws"] % NWS
        ctr["ws"] += 1
        P.op("pool", lambda h: h.dma_start(out=wts[:, s, 0:K, :], in_=wd_ap.rearrange("(k p) n -> p k n", p=128)),
             writes=["wt%d" % s], dma_key="wt%d" % s)
        return s, "wt%d" % s

    def norm_fm(scale_fn, bias_fn, dst, dst_pref, final=False):
        def stage_a(ti):
            t0, tn = TT[ti]
            u = ti % 2
            bank = next_bank()
            rs = "msb" if u == 0 else "rstd"
            for k in range(8):
                w = uid() % 2
                P.op("act", lambda h, k=k, w=w: h.activation(out=sq[:, w, 0:tn], in_=xk(k, t0, tn), func=AF.Square),
                     reads=[R("x", k, ti)], writes=["sq%d" % w])
                P.op("pe", lambda h, k=k, w=w: h.matmul(pst[bank][:, 0:tn], lhsT=ones_m[:], rhs=sq[:, w, 0:tn], start=(k == 0), stop=(k == 7)),
                     reads=["ones_m", "sq%d" % w], writes=["ps%d" % bank])
            P.op("act", lambda h: h.activation(out=nrm[:, u, 0:tn], in_=pst[bank][:, 0:tn], func=AF.Sqrt, bias=epsb[:, 0:1]),
                 reads=["ps%d" % bank, "epsb"], writes=[rs])
            P.op("dve", lambda h: h.reciprocal(out=nrm[:, u, 0:tn], in_=nrm[:, u, 0:tn]), reads=[rs], writes=[rs])

        def stage_b(ti):
            t0, tn = TT[ti]
            u = ti % 2
            rs = "msb" if u == 0 else "rstd"
            for k in range(8):
                if final:
                    eng = "dve"
                    P.op(eng, lambda h, k=k: h.scalar_tensor_tensor(out=xk(k, t0, tn), in0=xk(k, t0, tn), scalar=gfin_sb[:, k:k + 1], in1=nrm[:, u, 0:tn],
                                                                    op0=ALU.mult, op1=ALU.mult),
                         reads=[R("x", k, ti), rs, "gfin"], writes=[R("x", k, ti)])
                    continue
                v = uid() % 2
                P.op("dve", lambda h, k=k, v=v: h.tensor_tensor(out=tmp32[:, v, 0:tn], in0=xk(k, t0, tn), in1=nrm[:, u, 0:tn], op=ALU.mult),
                     reads=[R("x", k, ti), rs], writes=["tmp32_%d" % v])
                if ti < 4:
                    if k % 2 == 0:
                        P.op("act", lambda h, k=k, v=v: h.activation(out=bufk(dst, k, t0, tn), in_=tmp32[:, v, 0:tn], func=AF.Identity,
                                                                     scale=scale_fn(k, 0), bias=bias_fn(k, 0)),
                             reads=["tmp32_%d" % v, "modA", "mod"], writes=[R(dst_pref, k, ti)])
                    else:
                        P.op("dve", lambda h, k=k, v=v: h.tensor_scalar(out=bufk(dst, k, t0, tn), in0=tmp32[:, v, 0:tn],
                                                                        scalar1=scale_fn(k, 0), scalar2=bias_fn(k, 0), op0=ALU.mult, op1=ALU.add),
                             reads=["tmp32_%d" % v, "modA", "mod"], writes=[R(dst_pref, k, ti)])
                else:
                    P.op("dve", lambda h, k=k, v=v: h.tensor_tensor(out=tmp32[:, v, 0:tn], in0=tmp32[:, v, 0:tn], in1=scale_fn(k, 1), op=ALU.mult),
                         reads=["tmp32_%d" % v, "modA"], writes=["tmp32_%d" % v])
                    P.op("dve", lambda h, k=k, v=v: h.tensor_tensor(out=bufk(dst, k, t0, tn), in0=tmp32[:, v, 0:tn], in1=bias_fn(k, 1), op=ALU.add),
                         reads=["tmp32_%d" % v, "mod"], writes=[R(dst_pref, k, ti)])

        stage_a(0)
        for ti in range(5):
            if ti + 1 < 5:
                stage_a(ti + 1)
            stage_b(ti)
        if final:
            for k in range(8):
                P.op("sp" if k % 2 == 0 else "act", lambda h, k=k: h.dma_start(out=yT[k * 128:(k + 1) * 128, :], in_=x32[:, k * NT:(k + 1) * NT]),
                     reads=[R("x", k, ti_) for ti_ in range(5)], writes=["yT_%d" % k], dma_key="yst%d" % k)

    def multi_proj(specs, M, evac):
        look = 1 if 2 * len(specs) <= NWS else 0
        pending = {}

        def issue(m):
            if m < M and m not in pending:
                pending[m] = [load_wtile(wfn(m), K) for (wfn, src, sp_, K) in specs]

        for m in range(M):
            issue(m)
            slots = pending.pop(m)
            if look:
                issue(m + 1)
            for ti, (t0, tn) in enumerate(TT):
                banks = []
                for (wfn, src, sp_, K), (s, sr) in zip(specs, slots):
                    bank = next_bank()
                    banks.append(bank)
                    for k in range(K):
                        P.op("pe", lambda h, s=s, k=k, K=K, src=src, t0=t0, tn=tn, bank=bank: h.matmul(
                            pst[bank][:, 0:tn], lhsT=wts[:, s, k, :], rhs=bufk(src, k, t0, tn), start=(k == 0), stop=(k == K - 1)),
                            reads=[sr, R(sp_, k, ti)], writes=["ps%d" % bank])
                evac(m, ti, t0, tn, banks)

    def resid_evac(l, gate_idx):
        def f(m, ti, t0, tn, banks):
            bank = banks[0]
            g = gate_idx * 8 + m
            if ti < 4:
                P.op("dve", lambda h: h.scalar_tensor_tensor(out=xk(m, t0, tn), in0=pst[bank][:, 0:tn], scalar=mod[:, 0, g, 0:1],
                                                             in1=xk(m, t0, tn), op0=ALU.mult, op1=ALU.add),
                     reads=["ps%d" % bank, "mod", R("x", m, ti)], writes=[R("x", m, ti)])
            else:
                v = uid() % 2
                P.op("dve", lambda h: h.tensor_tensor(out=tmp32[:, v, 0:tn], in0=pst[bank][:, 0:tn], in1=mod[:, 0, g, 1:17], op=ALU.mult),
                     reads=["ps%d" % bank, "mod"], writes=["tmp32_%d" % v])
                P.op("dve", lambda h: h.tensor_tensor(out=xk(m, t0, tn), in0=xk(m, t0, tn), in1=tmp32[:, v, 0:tn], op=ALU.add),
                     reads=["tmp32_%d" % v, R("x", m, ti)], writes=[R("x", m, ti)])
        return f

    def ada_mod(l):
        for half in range(2):
            bank = next_bank()
            m0 = half * 24
            for m in range(m0, m0 + 24):
                s, sr = load_wtile(w_ada[l, :, m * 128:(m + 1) * 128])
                for k in range(8):
                    P.op("pe", lambda h, s=s, k=k, m=m, bank=bank, m0=m0: h.matmul(
                        pst[bank][:, (m - m0) * 17:(m - m0 + 1) * 17], lhsT=wts[:, s, k, :], rhs=cs_bf[:, k, :],
                        start=(k == 0), stop=(k == 7)), reads=[sr, "cs_bf"], writes=["ps%d" % bank])
            P.op("dve", lambda h, l=l, m0=m0, bank=bank: h.tensor_tensor(
                out=mod[:, 0, m0:m0 + 24, :], in0=pst[bank][:, 0:24 * 17].rearrange("p (m n) -> p m n", n=17),
                in1=vec_sb[:, l, m0:m0 + 24].unsqueeze(2).to_broadcast([128, 24, 17]), op=ALU.add),
                reads=["ps%d" % bank, "vec"], writes=["mod"])
        for j in range(2):
            sc0 = 8 + 24 * j
            P.op("dve", lambda h, l=l, j=j, sc0=sc0: h.tensor_scalar(out=modA[:, 0, j, :, :], in0=mod[:, 0, sc0:sc0 + 8, :],
                                                                    scalar1=1.0, scalar2=None, op0=ALU.add),
                 reads=["mod"], writes=["modA"])
            P.op("dve", lambda h, l=l, j=j: h.tensor_tensor(
                out=modA[:, 0, j, :, :], in0=modA[:, 0, j, :, :],
                in1=vec_sb[:, l, 48 + 8 * j:56 + 8 * j].unsqueeze(2).to_broadcast([128, 8, 17]), op=ALU.mult),
                reads=["modA", "vec"], writes=["modA"])


    TWO_PI = 2.0 * float(np.pi)
    S5R = ["Qh0", "Qh1", "KTc", "arT0", "arT1", "arT2", "arT3", "arGW0", "arGW1", "arKTx"]
    ALLB = [R("B", kk, ti_) for kk in range(8) for ti_ in range(5)]
    L_RE, L_IM, L_DT, L_LR, L_TH, C_R, C_I, A16R, A16I, A256R, A256I, SC0, SC1, SC2 = range(14)

    def S(i):
        return s5[:, i, :]

    def dv(fn, reads, writes, eng="dve"):
        P.op(eng, fn, reads=reads, writes=writes)

    def s5_tt(o, a, b, op, eng="dve"):
        dv(lambda h: h.tensor_tensor(out=S(o), in0=S(a), in1=S(b), op=op), ["s5"], ["s5"], eng)

    def s5_ts(o, a, s1, op0):
        dv(lambda h: h.tensor_scalar(out=S(o), in0=S(a), scalar1=s1, scalar2=None, op0=op0), ["s5"], ["s5"])

    T_A = tmp32[:, 0, :]
    T_B = tmp32[:, 1, :]
    T_C = nrm[:, 0, :]
    T_D = nrm[:, 1, :]
    T_I = sq[:].rearrange("p a n -> p (a n)").bitcast(I32)
    TR = ["tmp32_0", "tmp32_1", "msb", "rstd", "sq0", "sq1"]

    def rr_big(t, r):
        dv(lambda h: h.tensor_copy(out=T_I, in_=t), TR, TR)
        dv(lambda h: h.tensor_copy(out=T_D, in_=T_I), TR, TR)
        dv(lambda h: h.tensor_tensor(out=r, in0=t, in1=T_D, op=ALU.subtract), TR, TR)
        dv(lambda h: h.tensor_single_scalar(out=T_D, in_=r, scalar=0.5, op=ALU.is_gt), TR, TR)
        dv(lambda h: h.tensor_tensor(out=r, in0=r, in1=T_D, op=ALU.subtract), TR, TR)
        dv(lambda h: h.tensor_single_scalar(out=T_D, in_=r, scalar=-0.5, op=ALU.is_lt), TR, TR)
        dv(lambda h: h.tensor_tensor(out=r, in0=r, in1=T_D, op=ALU.add), TR, TR)

    def cview(t):
        return t.rearrange("p (q h) -> p q h", q=4)

    def bc4(tab, k):
        return tab[:, 4 * k:4 * k + 4].unsqueeze(2).to_broadcast([128, 4, 16])

    def cmul(eng, dst_re, dst_im, src_re, src_im, fr, fi, res):
        t1, t2 = cview(gtmp[:, 0 if eng == "dve" else 1, 0, :]), cview(gtmp[:, 0 if eng == "dve" else 1, 1, :])
        tr = "gtmp_" + eng
        dv(lambda h: h.tensor_tensor(out=t1, in0=src_re, in1=fr, op=ALU.mult), res + ["s5", "apw"], [tr], eng)
        dv(lambda h: h.tensor_tensor(out=t2, in0=src_im, in1=fi, op=ALU.mult), res + ["s5", "apw", tr], [tr], eng)
        dv(lambda h: h.tensor_tensor(out=dst_re, in0=t1, in1=t2, op=ALU.subtract), [tr] + res, res, eng)
        dv(lambda h: h.tensor_tensor(out=t1, in0=src_re, in1=fi, op=ALU.mult), res + ["s5", "apw", tr], [tr], eng)
        dv(lambda h: h.tensor_tensor(out=t2, in0=src_im, in1=fr, op=ALU.mult), res + ["s5", "apw", tr], [tr], eng)
        dv(lambda h: h.tensor_tensor(out=dst_im, in0=t1, in1=t2, op=ALU.add), [tr] + res, res, eng)

    def expand(eng, dst, src, mcol, reads, writes):
        dv(lambda h: h.tensor_tensor(out=dst.rearrange("p (q g h) -> p q g h", q=4, g=2),
                                     in0=src.unsqueeze(2).to_broadcast([128, 4, 2, 16]),
                                     in1=msm[:, mcol:mcol + 2].unsqueeze(1).unsqueeze(3).to_broadcast([128, 4, 2, 16]), op=ALU.mult),
           reads + ["msm"], writes, eng)

    def gen_quarter(eng, k, qq, X_re, X_im, Y_re, Y_im, out_re, out_im, reads, writes):
        ar_ = apw[:, 0, 4 * qq:4 * qq + 4, 4 * k:4 * k + 4].unsqueeze(3).to_broadcast([128, 4, 4, 32])
        ai_ = apw[:, 1, 4 * qq:4 * qq + 4, 4 * k:4 * k + 4].unsqueeze(3).to_broadcast([128, 4, 4, 32])
        t1 = T_A.rearrange("p (t q m) -> p t q m", t=4, q=4)
        t2 = T_B.rearrange("p (t q m) -> p t q m", t=4, q=4)
        bx = lambda X: X.rearrange("p (q m) -> p q m", q=4).unsqueeze(1).to_broadcast([128, 4, 4, 32])
        o4 = lambda O: O.rearrange("p t (q m) -> p t q m", q=4)
        rd = reads + ["apw"]
        dv(lambda h: h.tensor_tensor(out=t1, in0=ar_, in1=bx(X_re), op=ALU.mult), rd, ["tmp32_0"], eng)
        dv(lambda h: h.tensor_tensor(out=t2, in0=ai_, in1=bx(X_im), op=ALU.mult), rd, ["tmp32_1"], eng)
        dv(lambda h: h.tensor_tensor(out=o4(out_re), in0=t1, in1=t2, op=ALU.subtract), ["tmp32_0", "tmp32_1"], writes, eng)
        dv(lambda h: h.tensor_tensor(out=t1, in0=ar_, in1=bx(Y_im), op=ALU.mult), rd, ["tmp32_0"], eng)
        dv(lambda h: h.tensor_tensor(out=t2, in0=ai_, in1=bx(Y_re), op=ALU.mult), rd, ["tmp32_1"], eng)
        dv(lambda h: h.tensor_tensor(out=o4(out_im), in0=t1, in1=t2, op=ALU.add), ["tmp32_0", "tmp32_1"], writes, eng)

    def s5_tables(l):
        P.op("sp", lambda h: h.dma_start(out=s5[:, L_RE, :], in_=lamre_d[l]), writes=["s5"], dma_key="s5a")
        P.op("sp", lambda h: h.dma_start(out=s5[:, L_IM, :], in_=lamim_d[l]), writes=["s5"], dma_key="s5b")
        P.op("sp", lambda h: h.dma_start(out=s5[:, L_DT, :], in_=ldt_d[l]), writes=["s5"], dma_key="s5c")
        dv(lambda h: h.activation(out=S(L_DT), in_=S(L_DT), func=AF.Exp), ["s5"], ["s5"], "act")
        s5_tt(L_LR, L_RE, L_DT, ALU.mult)
        s5_tt(L_TH, L_IM, L_DT, ALU.mult)
        s5_ts(L_TH, L_TH, 1.0 / TWO_PI, ALU.mult)
        b3 = lambda tab: tab.unsqueeze(1).to_broadcast([128, 16, 32])
        tv = tauv[:].unsqueeze(2).to_broadcast([128, 16, 32])
        v3 = lambda t: t.rearrange("p (t g) -> p t g", t=16)
        dv(lambda h: h.tensor_tensor(out=v3(T_A), in0=b3(S(L_LR)), in1=tv, op=ALU.mult), ["s5", "tauv"] + TR, TR)
        dv(lambda h: h.activation(out=T_A, in_=T_A, func=AF.Exp), TR, TR, "act")
        dv(lambda h: h.tensor_tensor(out=v3(T_B), in0=b3(S(L_TH)), in1=tv, op=ALU.mult), ["s5", "tauv"] + TR, TR)
        rr_big(T_B, T_C)
        dv(lambda h: h.activation(out=T_C, in_=T_C, func=AF.Sin, scale=6.28318), TR, TR, "act")
        dv(lambda h: h.tensor_tensor(out=apw[:, 1, :, :], in0=v3(T_A), in1=v3(T_C), op=ALU.mult), TR, ["apw"])
        dv(lambda h: h.tensor_scalar(out=T_B, in0=T_B, scalar1=0.25, scalar2=None, op0=ALU.add), TR, TR)
        rr_big(T_B, T_C)
        dv(lambda h: h.activation(out=T_C, in_=T_C, func=AF.Sin, scale=6.28318), TR, TR, "act")
        dv(lambda h: h.tensor_tensor(out=apw[:, 0, :, :], in0=v3(T_A), in1=v3(T_C), op=ALU.mult), TR, ["apw"])
        AR, AI = apw[:, 0, 1, :], apw[:, 1, 1, :]
        dv(lambda h: h.tensor_tensor(out=S(SC0), in0=S(L_RE), in1=S(L_RE), op=ALU.mult), ["s5"], ["s5"])
        dv(lambda h: h.tensor_tensor(out=S(SC1), in0=S(L_IM), in1=S(L_IM), op=ALU.mult), ["s5"], ["s5"])
        s5_tt(SC0, SC0, SC1, ALU.add)
        dv(lambda h: h.reciprocal(out=S(SC0), in_=S(SC0)), ["s5"], ["s5"])
        dv(lambda h: h.tensor_scalar(out=S(SC1), in0=AR, scalar1=-1.0, scalar2=None, op0=ALU.add), ["s5", "apw"], ["s5"])
        s5_tt(C_R, SC1, L_RE, ALU.mult)
        dv(lambda h: h.tensor_tensor(out=S(SC2), in0=AI, in1=S(L_IM), op=ALU.mult), ["s5", "apw"], ["s5"])
        s5_tt(C_R, C_R, SC2, ALU.add)
        s5_tt(C_R, C_R, SC0, ALU.mult)
        dv(lambda h: h.tensor_tensor(out=S(C_I), in0=AI, in1=S(L_RE), op=ALU.mult), ["s5", "apw"], ["s5"])
        s5_tt(SC2, SC1, L_IM, ALU.mult)
        s5_tt(C_I, C_I, SC2, ALU.subtract)
        s5_tt(C_I, C_I, SC0, ALU.mult)
        dv(lambda h: h.tensor_copy(out=S(A16R), in_=AR), ["s5", "apw"], ["s5"])
        dv(lambda h: h.tensor_copy(out=S(A16I), in_=AI), ["s5", "apw"], ["s5"])
        for rr_, ii_, n_ in ((A16R, A16I, 4), (A256R, A256I, 4)):
            if rr_ == A256R:
                dv(lambda h: h.tensor_copy(out=S(A256R), in_=S(A16R)), ["s5"], ["s5"])
                dv(lambda h: h.tensor_copy(out=S(A256I), in_=S(A16I)), ["s5"], ["s5"])
            for _ in range(n_):
                s5_tt(SC0, rr_, rr_, ALU.mult)
                s5_tt(SC1, ii_, ii_, ALU.mult)
                s5_tt(SC2, rr_, ii_, ALU.mult)
                s5_tt(rr_, SC0, SC1, ALU.subtract)
                s5_ts(ii_, SC2, 2.0, ALU.mult)

    def s5_layer(l):
        dv(lambda h: h.memset(gtmp[:, 0, 0, 0:1], 0.0), [], ALLB + S5R + ["gtmp_dve"])
        P.op("sp", lambda h: h.dma_start(out=h0f.rearrange("p g r t -> p (g r t)"), in_=h0_d[l]), writes=["tb0", "tb1", "tb2", "tb3"], dma_key="h0")
        dv(lambda h: h.tensor_copy(out=h0b, in_=h0f), ["tb0", "tb1", "tb2", "tb3"], ["sq0", "sq1"])
        AR, AI = apw[:, 0, 1, :], apw[:, 1, 1, :]
        XB = lambda i: gcur[:, 0, i // 2, i % 2, :].rearrange("p (a b) -> p a b", a=1)[:, 0, :]

        def sview(k):
            return bA[:, k * NT:k * NT + 2048].rearrange("p (i c) -> p i c", i=16)

        def zview(k):
            return bA[:, k * NT:k * NT + 2048].rearrange("p (c i) -> p i c", i=16)

        gflat = gcur[:].rearrange("p a b c d -> p (a b c d)")
        X0, X1, X2, X3 = (gflat[:, i * 128:(i + 1) * 128] for i in range(4))
        arWC0 = X2.bitcast(BF16).rearrange("p (r m) -> p r m", r=2)
        Bk_re, Bk_im = cview(gtmp[:, 0, 0, :]), cview(gtmp[:, 0, 1, :])

        wslots = {}
        qstate = {}

        def pool_prologue(k):
            for bc_, (dre, dim_) in enumerate([(Bre_d, Bim_d), (Cre_d, Cim_d)]):
                P.op("sp", lambda h, bc_=bc_, dre=dre: h.dma_start(out=BCk[:, bc_, 0, :], in_=dre[l, :, 64 * k:64 * k + 64]),
                     writes=["BCk%d" % bc_], dma_key="bck%d0" % bc_)
                P.op("sp", lambda h, bc_=bc_, dim_=dim_: h.dma_start(out=BCk[:, bc_, 1, :], in_=dim_[l, :, 64 * k:64 * k + 64]),
                     writes=["BCk%d" % bc_], dma_key="bck%d1" % bc_)
            if k not in wslots:
                wslots[k] = load_wtile(w_in[l, :, 2048 + k * 128:2048 + (k + 1) * 128])
            if k + 1 < 8:
                wslots[k + 1] = load_wtile(w_in[l, :, 2048 + (k + 1) * 128:2048 + (k + 2) * 128])
            expand("pool", arWC0[:, 0, :], cview(BCk[:, 1, 0, :]), 0, ["BCk1"], ["gX2"])
            expand("pool", arWC0[:, 1, :], cview(BCk[:, 1, 1, :]), 2, ["BCk1"], ["gX2"])
            cmul("pool", cview(smt[:, 0:64]), cview(smt[:, 64:128]), cview(BCk[:, 0, 0, :]), cview(BCk[:, 0, 1, :]),
                 bc4(S(C_R), k), bc4(S(C_I), k), ["smt", "BCk0"])
            expand("pool", X0, cview(smt[:, 0:64]), 0, ["smt"], ["gX"])
            expand("pool", X1, cview(smt[:, 64:128]), 0, ["smt"], ["gX"])

        def sproj(k):
            slot, sr = wslots.pop(k)
            for ti, (t0, tn) in enumerate(TT):
                bank = next_bank()
                for kk in range(8):
                    P.op("pe", lambda h, kk=kk, slot=slot, t0=t0, tn=tn, bank=bank: h.matmul(
                        pst[bank][:, 0:tn], lhsT=wts[:, slot, kk, :], rhs=bufk(hb, kk, t0, tn), start=(kk == 0), stop=(kk == 7)),
                        reads=[sr, R("h", kk, ti)], writes=["ps%d" % bank])
                if ti < 4:
                    P.op("act", lambda h, ti=ti, bank=bank: h.activation(out=sview(k)[:, :, 32 * ti:32 * ti + 32],
                                                                         in_=pst[bank][:, 0:512].rearrange("p (c i) -> p i c", i=16), func=AF.Copy),
                         reads=["ps%d" % bank], writes=[R("A", k, t_) for t_ in range(4)] + vtm_overlap(k * NT, k * NT + 2048))
                else:
                    P.op("act", lambda h, t0=t0, tn=tn, bank=bank: h.activation(out=bufk(bA, k, t0, tn), in_=pst[bank][:, 0:tn], func=AF.Copy),
                         reads=["ps%d" % bank], writes=[R("A", k, ti)] + vtm_overlap(k * NT + t0, k * NT + t0 + tn))

        def quarter_gen(k, qq):
            gi = ctr["gw"] % 2
            ctr["gw"] += 1
            arGW, gwr = arGWs[gi], "arGW%d" % gi
            qstate[(k, qq)] = [arGW, gwr, None]
            gen_quarter("dve", k, qq, X0, X1, X0, X1, arGW[:, :, 0, :], arGW[:, :, 1, :], ["gX"], [gwr, "arKTx"])

        def quarter_pe(k, qq):
            arGW, gwr, _ = qstate[(k, qq)]
            bt = next_bank()
            psT = pst[bt][:].bitcast(BF16)
            for t4 in range(4):
                for ri in range(2):
                    P.op("pe", lambda h, t4=t4, ri=ri: h.transpose(psT[:, (t4 * 2 + ri) * 128:(t4 * 2 + ri + 1) * 128], arGW[:, t4, ri, :], identb[:]),
                         reads=[gwr, "identb"], writes=["ps%d" % bt])
            P.op("act", lambda h: h.activation(out=arT[:, 4 * qq:4 * qq + 4, :, :].rearrange("p t r m -> p (t r m)"), in_=psT, func=AF.Copy),
                 reads=["ps%d" % bt], writes=["arT%d" % qq])
            bk = next_bank()
            qstate[(k, qq)][2] = bk
            for t4 in range(4):
                P.op("pe", lambda h, t4=t4: h.matmul(pst[bk][:, t4 * 128:(t4 + 1) * 128], lhsT=arGW[:, t4, 0, :], rhs=arWC0[:, 0, :], start=True, stop=False),
                     reads=[gwr, "gX2"], writes=["ps%d" % bk])
                P.op("pe", lambda h, t4=t4: h.matmul(pst[bk][:, t4 * 128:(t4 + 1) * 128], lhsT=arGW[:, t4, 1, :], rhs=arWC0[:, 1, :], start=False, stop=True),
                     reads=[gwr, "gX2"], writes=["ps%d" % bk])

        def quarter_evac(k, qq):
            bk = qstate.pop((k, qq))[2]
            dv(lambda h: h.tensor_tensor(out=pst[bk][:].rearrange("p (t m) -> p t m", t=4), in0=pst[bk][:].rearrange("p (t m) -> p t m", t=4),
                                         in1=bdm[:].unsqueeze(1).to_broadcast([128, 4, 128]), op=ALU.mult),
               ["ps%d" % bk, "bdm"], ["ps%d" % bk])
            if qq == 0:
                dv(lambda h: h.scalar_tensor_tensor(out=pst[bk][:, 0:128], in0=ident[:], scalar=vec_sb[:, l, 64 + k:65 + k], in1=pst[bk][:, 0:128],
                                                    op0=ALU.mult, op1=ALU.add), ["ps%d" % bk, "ident", "vec"], ["ps%d" % bk])
            dv(lambda h: h.tensor_reduce(out=smt[:, 256:320].rearrange("p (t h) -> p t h", t=4),
                                         in_=pst[bk][:].rearrange("p (t g h) -> p t h g", t=4, g=8), axis=AX.X, op=ALU.add),
               ["ps%d" % bk], ["smt2"])
            dv(lambda h: h.tensor_copy(out=KTc[:, k, 4 * qq:4 * qq + 4, :], in_=smt[:, 256:320].rearrange("p (t h) -> p t h", t=4)),
               ["smt2"], ["KTc"])

        def states(k):
            sb_ = [next_bank() for _ in range(4)]
            for jq in range(4):
                for q in range(4):
                    bank = sb_[q]
                    for j in range(4 * jq, 4 * jq + 4):
                        for ri in range(2):
                            first = (j == 0 and ri == 0)
                            P.op("pe", lambda h, q=q, ri=ri, j=j, bank=bank, first=first: h.matmul(
                                pst[bank][:, ri * 128:(ri + 1) * 128], lhsT=arT[32 * q:32 * q + 32, 15 - j, ri, :], rhs=sview(k)[32 * q:32 * q + 32, j, :],
                                start=first, stop=first, skip_group_check=(not first), tile_position=(32 * q, 0)),
                                reads=["arT%d" % (3 - jq)] + [R("A", k, t_) for t_ in range(4)], writes=["ps%d" % bank])
            for q in range(4):
                bank = sb_[q]
                for ri in range(2):
                    P.op("pe", lambda h, q=q, ri=ri, bank=bank: h.matmul(
                        pst[bank][:, 256 + ri * 16:256 + (ri + 1) * 16], lhsT=arT[32 * q:32 * q + 32, 0, ri, :], rhs=bA[32 * q:32 * q + 32, k * NT + 2048:k * NT + 2064],
                        start=False, stop=False, skip_group_check=True, tile_position=(32 * q, 0)),
                        reads=["arT0", R("A", k, 4)], writes=["ps%d" % bank])
                P.op("act", lambda h, q=q, bank=bank: h.activation(
                    out=Qv[:, 4 * k + q, :, :], in_=pst[bank][:, 0:256].rearrange("p (r c) -> p r c", r=2), func=AF.Copy),
                    reads=["ps%d" % bank], writes=["Qh%d" % (k // 4)])
                P.op("act", lambda h, q=q, bank=bank: h.activation(
                    out=Ssm[:, 4 * k + q, :, :], in_=pst[bank][:, 256:288].rearrange("p (r t) -> p r t", r=2), func=AF.Copy),
                    reads=["ps%d" % bank], writes=["msb", "rstd"])

        def sample_state(k):
            h0k = h0f[:, 4 * k:4 * k + 4, :, :]
            t13 = smt[:, 0:128].rearrange("p (q r t) -> p q r t", q=4, r=2)
            t24 = smt[:, 128:256].rearrange("p (q r t) -> p q r t", q=4, r=2)
            arb = AR[:, 4 * k:4 * k + 4].unsqueeze(2).unsqueeze(3).to_broadcast([128, 4, 2, 16])
            aib = AI[:, 4 * k:4 * k + 4].unsqueeze(2).unsqueeze(3).to_broadcast([128, 4, 2, 16])
            Sk = Ssm[:, 4 * k:4 * k + 4, :, :]
            HN = ["msb", "rstd"]
            dv(lambda h: h.tensor_tensor(out=t13, in0=h0k, in1=arb, op=ALU.mult), ["tb0", "tb1", "tb2", "tb3", "apw"], ["smt"], "pool")
            dv(lambda h: h.tensor_tensor(out=t24, in0=h0k, in1=aib, op=ALU.mult), ["tb0", "tb1", "tb2", "tb3", "apw"], ["smt"], "pool")
            dv(lambda h: h.tensor_tensor(out=Sk, in0=Sk, in1=t13, op=ALU.add), ["smt"] + HN, HN, "pool")
            dv(lambda h: h.tensor_tensor(out=Sk[:, :, 0, :], in0=Sk[:, :, 0, :], in1=t24[:, :, 1, :], op=ALU.subtract), ["smt"] + HN, HN, "pool")
            dv(lambda h: h.tensor_tensor(out=Sk[:, :, 1, :], in0=Sk[:, :, 1, :], in1=t24[:, :, 0, :], op=ALU.add), ["smt"] + HN, HN, "pool")

        for k in range(8):
            pool_prologue(k)
            if k > 0:
                sample_state(k - 1)
            quarter_gen(k, 3)
            sproj(k)
            quarter_pe(k, 3)
            quarter_gen(k, 2)
            quarter_pe(k, 2)
            quarter_evac(k, 3)
            quarter_gen(k, 1)
            quarter_pe(k, 1)
            quarter_evac(k, 2)
            quarter_gen(k, 0)
            quarter_pe(k, 0)
            quarter_evac(k, 1)
            quarter_evac(k, 0)
            states(k)
        sample_state(7)
        P.op("sp", lambda h: h.dma_start(out=ss_o[l], in_=Ssm.rearrange("p g r t -> p (g r t)")), reads=["msb", "rstd"], writes=["ss_o%d" % l], dma_key="sso")

        mark('L%d chain' % l)
        for hf, eng in enumerate(["dve", "pool"]):
            gs = slice(16 * hf, 16 * hf + 16)
            base = tmp32[:].rearrange("p a n -> p (a n)") if hf == 0 else nrm[:].rearrange("p a n -> p (a n)")
            cr_ = ["tmp32_0", "tmp32_1"] if hf == 0 else ["msb", "rstd"]
            v4 = lambda t: t.rearrange("p (g r b) -> p g r b", g=16, r=2)
            Hs, T13, T24, Nn = (v4(base[:, i * 256:(i + 1) * 256]) for i in range(4))
            qr = "Qh%d" % hf
            Qb = Qv[:, gs, :, :].rearrange("p g r (b i) -> p g r b i", i=16)
            bc_ = lambda idx, shp: s5[:, idx, gs].unsqueeze(2).unsqueeze(3).to_broadcast(shp)
            ArB, AiB = bc_(A16R, [128, 16, 2, 8]), bc_(A16I, [128, 16, 2, 8])
            dv(lambda h, Hs=Hs: h.memset(Hs, 0.0), [], cr_, eng)
            for pas in range(2):
                for i in range(16):
                    Qc = Qb[:, :, :, :, i]
                    dv(lambda h, Hs=Hs, T13=T13, ArB=ArB: h.tensor_tensor(out=T13, in0=Hs, in1=ArB, op=ALU.mult), cr_ + ["s5"], cr_, eng)
                    dv(lambda h, Hs=Hs, T24=T24, AiB=AiB: h.tensor_tensor(out=T24, in0=Hs, in1=AiB, op=ALU.mult), cr_ + ["s5"], cr_, eng)
                    dv(lambda h, Nn=Nn, Qc=Qc, T13=T13: h.tensor_tensor(out=Nn, in0=Qc, in1=T13, op=ALU.add), cr_ + [qr], cr_, eng)
                    if pas == 1:
                        dv(lambda h, Qc=Qc, Hs=Hs: h.tensor_copy(out=Qc, in_=Hs), cr_, [qr], eng)
                    dv(lambda h, Hs=Hs, Nn=Nn, T24=T24: h.tensor_tensor(out=Hs[:, :, 0, :], in0=Nn[:, :, 0, :], in1=T24[:, :, 1, :], op=ALU.subtract), cr_, cr_, eng)
                    dv(lambda h, Hs=Hs, Nn=Nn, T24=T24: h.tensor_tensor(out=Hs[:, :, 1, :], in0=Nn[:, :, 1, :], in1=T24[:, :, 0, :], op=ALU.add), cr_, cr_, eng)
                if pas == 0:
                    Cc = T13[:, :, :, 0]
                    Ta = T13[:, :, :, 1]
                    Tb = T13[:, :, :, 2]
                    Tc = T13[:, :, :, 3]
                    A2r, A2i = (s5[:, idx, gs].unsqueeze(2).to_broadcast([128, 16, 2]) for idx in (A256R, A256I))
                    dv(lambda h, Cc=Cc: h.memset(Cc, 0.0), [], cr_, eng)
                    for b in range(8):
                        Lb = Hs[:, :, :, b]
                        dv(lambda h, Ta=Ta, Cc=Cc, A2r=A2r: h.tensor_tensor(out=Ta, in0=Cc, in1=A2r, op=ALU.mult), cr_ + ["s5"], cr_, eng)
                        dv(lambda h, Tb=Tb, Cc=Cc, A2i=A2i: h.tensor_tensor(out=Tb, in0=Cc, in1=A2i, op=ALU.mult), cr_ + ["s5"], cr_, eng)
                        dv(lambda h, Tc=Tc, Lb=Lb, Ta=Ta: h.tensor_tensor(out=Tc, in0=Lb, in1=Ta, op=ALU.add), cr_, cr_, eng)
                        dv(lambda h, Lb=Lb, Cc=Cc: h.tensor_copy(out=Lb, in_=Cc), cr_, cr_, eng)
                        dv(lambda h, Cc=Cc, Tc=Tc, Tb=Tb: h.tensor_tensor(out=Cc[:, :, 0], in0=Tc[:, :, 0], in1=Tb[:, :, 1], op=ALU.subtract), cr_, cr_, eng)
                        dv(lambda h, Cc=Cc, Tc=Tc, Tb=Tb: h.tensor_tensor(out=Cc[:, :, 1], in0=Tc[:, :, 1], in1=Tb[:, :, 0], op=ALU.add), cr_, cr_, eng)
                    dv(lambda h, hf=hf, Cc=Cc: h.tensor_copy(out=chn[:, hf, 0, :].rearrange("p (g r) -> p g r", r=2), in_=Cc), cr_, ["chn%d" % hf], eng)
                    P.op("sp", lambda h, hf=hf: h.dma_start(out=sp_o[l, :, 32 * hf:32 * hf + 32], in_=chn[:, hf, 0, :]), reads=["chn%d" % hf],
                         writes=["sp_o%d_%d" % (l, hf)], dma_key="spo%d" % hf)

        mark('L%d s5B' % l)
        for k in range(8):
            P.op("sp", lambda h, k=k: h.dma_start(out=BCk[:, 1, 0, :], in_=Cre_d[l, :, 64 * k:64 * k + 64]), writes=["BCk1"], dma_key="bck10")
            P.op("sp", lambda h, k=k: h.dma_start(out=BCk[:, 1, 1, :], in_=Cim_d[l, :, 64 * k:64 * k + 64]), writes=["BCk1"], dma_key="bck11")
            dv(lambda h, k=k: h.tensor_tensor(out=arKTx.rearrange("p t (g h) -> p t g h", g=8),
                                              in0=KTc[:, k, :, :].unsqueeze(2).to_broadcast([128, 16, 8, 16]),
                                              in1=bd8[:].unsqueeze(1).unsqueeze(3).to_broadcast([128, 16, 8, 16]), op=ALU.mult),
               ["KTc", "bd8"], ["arKTx", "arGW0", "arGW1"])
            cmul("pool", cview(smt[:, 0:64]), cview(smt[:, 64:128]), cview(BCk[:, 1, 0, :]), cview(BCk[:, 1, 1, :]),
                 bc4(AR, k), bc4(AI, k), ["smt", "BCk1"])
            expand("pool", X0, cview(smt[:, 0:64]), 0, ["smt"], ["gX"])
            expand("pool", X1, cview(smt[:, 64:128]), 0, ["smt"], ["gX"])
            expand("pool", X2, cview(smt[:, 0:64]), 2, ["smt"], ["gX", "gX2"])
            expand("pool", X3, cview(smt[:, 64:128]), 2, ["smt"], ["gX", "gX2"])
            for qq in range(4):
                gen_quarter("dve", k, qq, X0, X1, X2, X3, arT[:, 4 * qq:4 * qq + 4, 0, :], arT[:, 4 * qq:4 * qq + 4, 1, :], ["gX"], ["arT%d" % qq])
            yb = [next_bank() for _ in range(4)]
            ys = next_bank()
            AK = [R("A", k, t_) for t_ in range(4)]
            for b in range(4):
                for tau in range(4 * b + 4):
                    i_lo = max(tau, 4 * b)
                    ni = 4 * b + 4 - i_lo
                    P.op("pe", lambda h, k=k, b=b, tau=tau, i_lo=i_lo, ni=ni, bank=yb[b]: h.matmul(
                        pst[bank][:, (i_lo - 4 * b) * 128:512].rearrange("p (i c) -> p i c", c=128),
                        lhsT=arKTx[:, tau, :], rhs=sview(k)[:, i_lo - tau:i_lo - tau + ni, :], start=(tau == 0), stop=(tau == 0), skip_group_check=(tau > 0)),
                        reads=["arKTx"] + AK, writes=["ps%d" % yb[b]])
                for i4 in range(4):
                    for q in range(4):
                        for ri in range(2):
                            P.op("pe", lambda h, k=k, b=b, i4=i4, q=q, ri=ri, bank=yb[b]: h.matmul(
                                pst[bank][32 * q:32 * q + 32, i4 * 128:(i4 + 1) * 128], lhsT=arT[:, 4 * b + i4, ri, 32 * q:32 * q + 32],
                                rhs=Qv[:, 4 * k + q, ri, :], start=False, stop=False, skip_group_check=True, tile_position=(0, 32 * q)),
                                reads=["arT%d" % b, "Qh%d" % (k // 4)], writes=["ps%d" % yb[b]])
            P.op("pe", lambda h, k=k, ys=ys: h.matmul(pst[ys][:, 0:16], lhsT=arKTx[:, 0, :], rhs=bA[:, k * NT + 2048:k * NT + 2064], start=True, stop=True),
                 reads=["arKTx", R("A", k, 4)], writes=["ps%d" % ys])
            for q in range(4):
                for ri in range(2):
                    P.op("pe", lambda h, k=k, q=q, ri=ri, ys=ys: h.matmul(
                        pst[ys][32 * q:32 * q + 32, 0:16], lhsT=arT[:, 0, ri, 32 * q:32 * q + 32], rhs=h0b[:, 4 * k + q, ri, :],
                        start=False, stop=False, skip_group_check=True, tile_position=(0, 32 * q)),
                        reads=["arT0", "sq0", "sq1"], writes=["ps%d" % ys])
            for b in range(4):
                P.op("act", lambda h, k=k, b=b, bank=yb[b]: h.activation(out=zview(k)[:, 4 * b:4 * b + 4, :], in_=pst[bank][:].rearrange("p (i c) -> p i c", c=128),
                                                                         func=AF.Gelu_apprx_tanh),
                     reads=["ps%d" % yb[b]], writes=AK)
            P.op("act", lambda h, k=k, ys=ys: h.activation(out=bufk(bA, k, 2048, 16), in_=pst[ys][:, 0:16], func=AF.Gelu_apprx_tanh),
                 reads=["ps%d" % ys], writes=[R("A", k, 4)])
        dv(lambda h: h.memset(gtmp[:, 0, 0, 0:1], 0.0), [], ALLB + S5R + ["gtmp_dve"])

    def layer(l):
        mark('L%d ada' % l)
        s5_tables(l)
        ada_mod(l)
        mark('L%d norm1' % l)
        norm_fm(lambda k, smp: (modA[:, 0, 0, k, 0:1] if smp == 0 else modA[:, 0, 0, k, 1:17]),
                lambda k, smp: (mod[:, 0, 0 + k, 0:1] if smp == 0 else mod[:, 0, 0 + k, 1:17]), hb, "h")

        mark('L%d v' % l)
        TBALL = ["tb0", "tb1", "tb2", "tb3"]
        VG = ["msb", "rstd"]
        VSQ = ["tmp32_0", "tmp32_1"]
        P.op("pool", lambda h: h.dma_start(out=wbig, in_=w_in[l, :, 1024:2048].rearrange("(k p) n -> p k n", p=128)),
             writes=["wbig"] + B_overlap(0, 8192), dma_key="wbig")
        P.op("sp", lambda h: h.dma_start(out=gvb_sb[:], in_=gvb[l]), writes=TBALL, dma_key="gvb")
        P.op("sp", lambda h: h.dma_start(out=wsT32, in_=wsT[l].rearrange("h s t -> s h t")), writes=["sq0", "sq1"], dma_key="wsT32")
        P.op("pool", lambda h: h.dma_start(out=bspb[:], in_=bsp[l]), writes=["bspb"], dma_key="bspb")
        P.op("sp", lambda h: h.dma_start(out=w00_sb[:], in_=w00[l]), writes=["w00"], dma_key="w00")
        P.op("pool", lambda h: h.dma_start(out=b00b[:], in_=b00[l]), writes=["b00b"], dma_key="b00b")
        P.op("dve", lambda h: h.tensor_tensor(out=wsTb[:], in0=wsT32, in1=triu[:].unsqueeze(1).to_broadcast([128, 4, 128]), op=ALU.mult),
             reads=["sq0", "sq1", "triu"], writes=["wsTb"])
        for hh in range(4):
            P.op("dve", lambda h, hh=hh: h.tensor_scalar(out=WI[:, hh, :], in0=ident[0:16, 0:16], scalar1=w00_sb[:, hh:hh + 1], scalar2=None, op0=ALU.mult),
                 reads=["ident", "w00"], writes=["WI"])
        VGB = [(nrm[:].rearrange("p a n -> p (a n)"), ["msb", "rstd"]), (tmp32[:].rearrange("p a n -> p (a n)"), ["tmp32_0", "tmp32_1"])]
        vjunk = sq[:].rearrange("p a n -> p (a n)")
        for i in range(17):
            rows = 128 if i < 16 else 16
            tok0 = i * 128
            ti = min(i // 4, 4)
            vgt, VG = VGB[i % 2]
            banks = [next_bank(), next_bank()]
            for cb in range(2):
                for k in range(8):
                    P.op("pe", lambda h, k=k, cb=cb, tok0=tok0, rows=rows, bank=banks[cb]: h.matmul(
                        pst[bank][0:rows, :], lhsT=hb[:, k * NT + tok0:k * NT + tok0 + rows], rhs=wbig[:, k, cb * 512:(cb + 1) * 512],
                        start=(k == 0), stop=(k == 7)), reads=["wbig", R("h", k, ti)], writes=["ps%d" % banks[cb]])
                P.op("act", lambda h, cb=cb, rows=rows, bank=banks[cb], vgt=vgt: h.activation(out=vgt[0:rows, cb * 512:(cb + 1) * 512], in_=pst[bank][0:rows, :],
                                                                                           func=AF.Gelu_apprx_tanh),
                     reads=["ps%d" % banks[cb]], writes=[VG[cb]])
            P.op("act", lambda h, rows=rows, vgt=vgt: h.activation(out=vjunk[0:rows, :], in_=vgt[0:rows, :], func=AF.Square, accum_out=vss[0:rows, 0:1]),
                 reads=VG + ["vss"], writes=["sq0", "sq1", "vss"])
            P.op("dve", lambda h, rows=rows: h.tensor_scalar(out=vss[0:rows, 0:1], in0=vss[0:rows, 0:1], scalar1=1.0 / 1024.0, scalar2=EPS,
                                                           op0=ALU.mult, op1=ALU.add), reads=["vss"], writes=["vss"])
            P.op("act", lambda h, rows=rows: h.activation(out=vss[0:rows, 1:2], in_=vss[0:rows, 0:1], func=AF.Sqrt),
                 reads=["vss"], writes=["vrs"])
            P.op("dve", lambda h, rows=rows: h.reciprocal(out=vss[0:rows, 1:2], in_=vss[0:rows, 1:2]),
                 reads=["vrs"], writes=["vrs"])
            if i < 16:
                P.op("dve", lambda h, i=i, vgt=vgt: h.scalar_tensor_tensor(out=bA[:, i * 1024:(i + 1) * 1024], in0=vgt[:, :], scalar=vss[:, 1:2],
                                                                        in1=gvb_sb[:], op0=ALU.mult, op1=ALU.mult),
                     reads=VG + ["vrs"] + TBALL, writes=["vtm%d" % i] + A_overlap(i * 1024, (i + 1) * 1024))
            else:
                vs32t, VS = VGB[(i + 1) % 2]
                vs32v = vs32t[0:16, :]
                P.op("dve", lambda h, vgt=vgt: h.scalar_tensor_tensor(out=vs32v, in0=vgt[0:16, :], scalar=vss[0:16, 1:2],
                                                                   in1=gvb_sb[0:16, :], op0=ALU.mult, op1=ALU.mult),
                     reads=VG + ["vrs"] + TBALL, writes=VS)
                P.op("dve", lambda h: h.tensor_copy(out=vs_bf[:], in_=vs32v), reads=VS, writes=["vs_bf"])
                P.op("sp", lambda h: h.dma_start(out=gv_o[l], in_=vs32v), reads=VS, writes=["gv_o%d" % l], dma_key="gvst")

        mark('L%d gmlp' % l)
        def gmlp_evac(m, ti, t0, tn, banks):
            hh = m // 2
            mb = next_bank()
            if ti < 4:
                for c in range(4):
                    ch = ti * 4 + c
                    P.op("pe", lambda h, c=c, ch=ch: h.matmul(pst[mb][:, c * 128:(c + 1) * 128], lhsT=bA[:, ch * 1024 + m * 128:ch * 1024 + (m + 1) * 128],
                                                              rhs=wsTb[:, hh, :], start=True, stop=False),
                         reads=["vtm%d" % ch, "wsTb"], writes=["ps%d" % mb])
                    P.op("pe", lambda h, c=c: h.matmul(pst[mb][:, c * 128:(c + 1) * 128], lhsT=ones_r[0:1, :], rhs=bspb[0:1, hh * 128:(hh + 1) * 128],
                                                       start=False, stop=True),
                         reads=["ones_r", "bspb"], writes=["ps%d" % mb])
            else:
                P.op("pe", lambda h: h.matmul(pst[mb][:, 0:16], lhsT=vs_bf[0:16, m * 128:(m + 1) * 128], rhs=WI[0:16, hh, :], start=True, stop=False),
                     reads=["vs_bf", "WI"], writes=["ps%d" % mb])
                P.op("pe", lambda h: h.matmul(pst[mb][:, 0:16], lhsT=ones_r[0:1, :], rhs=b00b[0:1, hh * 16:(hh + 1) * 16], start=False, stop=True),
                     reads=["ones_r", "b00b"], writes=["ps%d" % mb])
            a, b = uid() % 4, uid() % 4
            if b == a:
                b = (a + 1) % 4
            P.op("act", lambda h: h.activation(out=tb[:, a, 0:tn], in_=pst[banks[0]][:, 0:tn], func=AF.Gelu_apprx_tanh),
                 reads=["ps%d" % banks[0]], writes=["tb%d" % a])
            P.op("act", lambda h: h.activation(out=tb[:, b, 0:tn], in_=pst[banks[1]][:, 0:tn], func=AF.Sigmoid),
                 reads=["ps%d" % banks[1]], writes=["tb%d" % b])
            P.op("dve", lambda h: h.tensor_tensor(out=tb[:, a, 0:tn], in0=pst[mb][:, 0:tn], in1=tb[:, a, 0:tn], op=ALU.mult),
                 reads=["ps%d" % mb, "tb%d" % a], writes=["tb%d" % a])
            P.op("dve", lambda h: h.tensor_tensor(out=bufk(bB, m, t0, tn), in0=tb[:, a, 0:tn], in1=tb[:, b, 0:tn], op=ALU.mult),
                 reads=["tb%d" % a, "tb%d" % b], writes=[R("B", m, ti)] + (["wbig"] if m * NT + t0 < 8192 else []))

        multi_proj([(lambda m: w_in[l, :, m * 128:(m + 1) * 128], hb, "h", 8),
                    (lambda m: w_in[l, :, 3072 + m * 128:3072 + (m + 1) * 128], hb, "h", 8)], 8, gmlp_evac)

        mark('L%d wout1' % l)
        multi_proj([(lambda m: w_out[l, :, m * 128:(m + 1) * 128], bB, "B", 8)], 8, resid_evac(l, 2))

        mark('L%d s5A' % l)
        if WITH_S5:
            s5_layer(l)
        else:
            for k in range(8):
                for ti, (t0, tn) in enumerate(TT):
                    P.op("pool", lambda h, k=k, t0=t0, tn=tn: h.memset(bufk(bA, k, t0, tn), 0.0),
                         writes=[R("A", k, ti)] + vtm_overlap(k * NT + t0, k * NT + t0 + tn))

        mark('L%d glu' % l)
        def glu_evac(m, ti, t0, tn, banks):
            a, b = uid() % 4, uid() % 4
            if b == a:
                b = (a + 1) % 4
            P.op("act", lambda h: h.activation(out=tb[:, a, 0:tn], in_=pst[banks[0]][:, 0:tn], func=AF.Sigmoid, bias=vec_sb[:, l, 72 + m:73 + m]),
                 reads=["ps%d" % banks[0], "vec"], writes=["tb%d" % a])
            P.op("act", lambda h: h.activation(out=tb[:, b, 0:tn], in_=pst[banks[1]][:, 0:tn], func=AF.Sigmoid),
                 reads=["ps%d" % banks[1]], writes=["tb%d" % b])
            P.op("dve", lambda h: h.tensor_tensor(out=tb[:, a, 0:tn], in0=bufk(bA, m, t0, tn), in1=tb[:, a, 0:tn], op=ALU.mult),
                 reads=[R("A", m, ti), "tb%d" % a], writes=["tb%d" % a])
            P.op("dve", lambda h: h.tensor_tensor(out=bufk(bB, m, t0, tn), in0=tb[:, a, 0:tn], in1=tb[:, b, 0:tn], op=ALU.mult),
                 reads=["tb%d" % a, "tb%d" % b], writes=[R("B", m, ti)])

        multi_proj([(lambda m: w_glu[l, :, m * 128:(m + 1) * 128], bA, "A", 8),
                    (lambda m: w_in[l, :, 4096 + m * 128:4096 + (m + 1) * 128], hb, "h", 8)], 8, glu_evac)
        multi_proj([(lambda m: w_out[l, :, m * 128:(m + 1) * 128], bB, "B", 8)], 8, resid_evac(l, 2))

        mark('L%d ffn' % l)
        norm_fm(lambda k, smp: (modA[:, 0, 1, k, 0:1] if smp == 0 else modA[:, 0, 1, k, 1:17]),
                lambda k, smp: (mod[:, 0, 24 + k, 0:1] if smp == 0 else mod[:, 0, 24 + k, 1:17]), hb, "h")
        def ff1(g):
            fbuf, fpref = (bA, "A") if g % 2 == 0 else (bB, "B")

            def ff1_evac(m, ti, t0, tn, banks):
                a = uid() % 4
                P.op("act", lambda h: h.activation(out=tb[:, a, 0:tn], in_=pst[banks[0]][:, 0:tn], func=AF.Relu),
                     reads=["ps%d" % banks[0]], writes=["tb%d" % a])
                P.op("dve", lambda h: h.tensor_tensor(out=bufk(fbuf, m, t0, tn), in0=tb[:, a, 0:tn], in1=tb[:, a, 0:tn], op=ALU.mult),
                     reads=["tb%d" % a], writes=[R(fpref, m, ti)])

            multi_proj([(lambda m: w_ff1[l, :, g * 1024 + m * 128:g * 1024 + (m + 1) * 128], hb, "h", 8)], 8, ff1_evac)

        def ff2(g):
            fbuf, fpref = (bA, "A") if g % 2 == 0 else (bB, "B")
            multi_proj([(lambda m: w_ff2[l, g * 1024:(g + 1) * 1024, m * 128:(m + 1) * 128], fbuf, fpref, 8)], 8, resid_evac(l, 5))

        ff1(0)
        for g in range(4):
            if g + 1 < 4:
                ff1(g + 1)
            ff2(g)

    for l_ in range(DEPTH):
        layer(l_)

    mark('final')
    norm_fm(None, None, None, None, final=True)

    P.finalize()
    st.close()
    return nc


_NC_CACHE = {}
PHASES = []


def _host_inputs(inp):
    f = np.float32
    g = lambda n: np.asarray(inp[n], dtype=f)
    x_prompt, x_sample = g("x_prompt"), g("x_sample")
    c_prompt, c_sample = g("c_prompt"), g("c_sample")
    b_ada, g1, g2, dsk, bglu = g("b_ada"), g("g_norm1"), g("g_norm2"), g("d_skip"), g("b_glu")
    vecs = np.zeros((DEPTH, 128, 80), f)
    for l in range(DEPTH):
        vecs[l, :, 0:48] = b_ada[l].reshape(48, 128).T
        vecs[l, :, 48:56] = g1[l].reshape(8, 128).T
        vecs[l, :, 56:64] = g2[l].reshape(8, 128).T
        vecs[l, :, 64:72] = dsk[l].reshape(8, 128).T
        vecs[l, :, 72:80] = bglu[l].reshape(8, 128).T
    gfin = np.ascontiguousarray(g("g_final").reshape(8, 128).T)
    gvb = np.ascontiguousarray(np.broadcast_to(g("g_v")[:, None, :], (DEPTH, 128, D)))
    ws = g("w_spatial")
    wsT = np.ascontiguousarray(np.transpose(ws, (0, 1, 3, 2)))
    bs = g("b_spatial")
    bsp = np.ascontiguousarray(bs.reshape(DEPTH, 1, 512))
    w00 = np.ascontiguousarray(np.broadcast_to(ws[:, None, :, 0, 0], (DEPTH, 16, 4)))
    b00 = np.ascontiguousarray(np.broadcast_to(bs[:, :, 0:1], (DEPTH, 4, 16)).reshape(DEPTH, 1, 64))
    lam_re, lam_im, log_dt = g("lam_re"), g("lam_im"), g("log_dt")
    sm2 = lambda a: np.ascontiguousarray(a.reshape(DEPTH, 32, 2, 64).transpose(0, 2, 3, 1).reshape(DEPTH, 128, 32))
    lamre, lamim = sm2(lam_re), sm2(lam_im)
    ldt = np.ascontiguousarray(np.broadcast_to(log_dt.reshape(DEPTH, 32, 2).transpose(0, 2, 1)[:, :, None, :], (DEPTH, 2, 64, 32)).reshape(DEPTH, 128, 32))
    smB = lambda a: np.ascontiguousarray(a.reshape(DEPTH, 32, 2, 64, 16).transpose(0, 2, 3, 1, 4).reshape(DEPTH, 128, 512))
    smC = lambda a: np.ascontiguousarray(a.reshape(DEPTH, 32, 2, 16, 64).transpose(0, 2, 4, 1, 3).reshape(DEPTH, 128, 512))
    msm = np.zeros((128, 4), f)
    msm[0:64, 0] = 1; msm[64:128, 1] = 1; msm[:, 2:4] = -msm[:, 0:2]
    ii = np.arange(128)
    bdm = (ii[:, None] // 16 == ii[None, :] // 16).astype(f)
    bd8 = (ii[:, None] // 16 == np.arange(8)[None, :]).astype(f)
    sre, sim = g("state_ssm_re"), g("state_ssm_im")
    shared = dict(lamre=lamre, lamim=lamim, ldt=ldt, Bre=smB(g("b_re")), Bim=smB(g("b_im")), Cre=smC(g("c_re")), Cim=smC(g("c_im")),
                  msm=msm, bdm=bdm, bd8=bd8, tauv=np.ascontiguousarray(np.broadcast_to(np.arange(16, dtype=f)[None, :], (128, 16))), vecs=vecs, gfin=gfin, gvb=gvb, wsT=wsT, bsp=bsp, w00=w00, b00=b00,
                  ident=np.eye(128, dtype=f), triu=np.triu(np.ones((128, 128), f)),
                  w_ada=g("w_ada"), w_in=g("w_in"), w_glu=g("w_glu"), w_out=g("w_out"), w_ff1=g("w_ff1"), w_ff2=g("w_ff2"))
    maps = []
    for c in range(8):
        xs = x_sample[16 * c:16 * c + 16, 0, :]
        xT = np.ascontiguousarray(np.concatenate([x_prompt[c], xs], axis=0).T)
        cT = np.ascontiguousarray(np.concatenate([c_prompt[c:c + 1], c_sample[16 * c:16 * c + 16]], axis=0).T)
        hre = sre[:, 16 * c:16 * c + 16].reshape(DEPTH, 16, 32, 2, 64).transpose(0, 3, 4, 2, 1)
        him = sim[:, 16 * c:16 * c + 16].reshape(DEPTH, 16, 32, 2, 64).transpose(0, 3, 4, 2, 1)
        h0 = np.ascontiguousarray(np.stack([hre, him], axis=4).reshape(DEPTH, 128, 1024))
        m = dict(shared)
        m.update(xT=xT, cT=cT, h0=h0)
        maps.append(m)
    return maps


def kernel(**inputs):
    if "nc" not in _NC_CACHE:
        _NC_CACHE["nc"] = build_nc()
    nc = _NC_CACHE["nc"]
    maps = _host_inputs(inputs)
    res = run_bass_kernel_spmd(nc, maps, core_ids=list(range(8)))
    rs = res.results
    y_prompt = np.stack([rs[c]["yT"][:, :2048].T for c in range(8)], axis=0)
    y_sample = np.concatenate([rs[c]["yT"][:, 2048:].T for c in range(8)], axis=0)[:, None, :]
    gv = np.concatenate([rs[c]["gv_o"] for c in range(8)], axis=1)[:, :, None, :]
    sp = np.stack([rs[c]["sp_o"].reshape(DEPTH, 2, 64, 32, 2) for c in range(8)], axis=1)
    sp = sp.transpose(0, 1, 5, 4, 2, 3).reshape(DEPTH, 8, 2, 64, 64)
    ss = np.stack([rs[c]["ss_o"].reshape(DEPTH, 2, 64, 32, 2, 16) for c in range(8)], axis=1)
    ss = ss.transpose(0, 5, 1, 6, 4, 2, 3).reshape(DEPTH, 2, 128, 64, 64)
    return (np.ascontiguousarray(y_prompt), np.ascontiguousarray(y_sample),
            np.ascontiguousarray(sp[:, :, 0]), np.ascontiguousarray(sp[:, :, 1]),
            np.ascontiguousarray(ss[:, 0]), np.ascontiguousarray(ss[:, 1]),
            np.ascontiguousarray(gv))
```

```python
import contextlib
import numpy as np
import concourse.bass as bass
import concourse.mybir as mybir
from concourse.bass_utils import run_bass_kernel_spmd

F32 = mybir.dt.float32
BF16 = mybir.dt.bfloat16
I32 = mybir.dt.int32
AF = mybir.ActivationFunctionType
ALU = mybir.AluOpType
AX = mybir.AxisListType

ENGS = ["pe", "act", "dve", "pool", "sp"]
D = 1024
NT = 2064
TT = [(0, 512), (512, 512), (1024, 512), (1536, 512), (2048, 16)]
DEPTH = 2
EPS = 1e-6
WITH_S5 = True
S5_STAGE = 3


PHASES = []


class Prog:
    def __init__(self, nc):
        self.nc = nc
        self.ops = {e: [] for e in ENGS}
        self.last_w = {}
        self.readers = {}
        self.dma_cnt = {}

    def op(self, eng, fn, reads=(), writes=(), dma_key=None):
        idx = len(self.ops[eng])
        deps = set()
        for r in reads:
            w = self.last_w.get(r)
            if w is not None:
                deps.add(w)
        for r in writes:
            w = self.last_w.get(r)
            if w is not None:
                deps.add(w)
            for rd in self.readers.get(r, {}).values():
                deps.add(rd)
        rec = dict(fn=fn, deps=deps, sig=False, dma_key=dma_key, dma_count=None)
        if dma_key is not None:
            c = self.dma_cnt.get(dma_key, 0) + 16
            self.dma_cnt[dma_key] = c
            rec["dma_count"] = c
            me = ("dma", dma_key, c)
        else:
            me = ("eng", eng, idx)
        for r in writes:
            self.last_w[r] = me
            self.readers[r] = {}
        for r in reads:
            self.readers.setdefault(r, {})[eng if dma_key is None else ("dma", dma_key)] = me
        self.ops[eng].append(rec)
        return me

    def finalize(self, final_eng="sp"):
        nc = self.nc
        for e in ENGS:
            for idx, rec in enumerate(self.ops[e]):
                keep = set()
                for d in rec["deps"]:
                    if d[0] == "eng":
                        _, f, j = d
                        if f == e and e in ("pe", "sp"):
                            continue
                        self.ops[f][j]["sig"] = True
                    keep.add(d)
                rec["deps"] = keep
        sigcount = {}
        for e in ENGS:
            c = 0
            arr = []
            for rec in self.ops[e]:
                if rec["sig"]:
                    c += 1
                arr.append(c)
            sigcount[e] = arr
        stack = contextlib.ExitStack()
        sems = {e: stack.enter_context(nc.semaphore("s_" + e)) for e in ENGS}
        dsems = {k: stack.enter_context(nc.semaphore("d_%d" % i)) for i, k in enumerate(self.dma_cnt)}
        block = stack.enter_context(nc.Block())
        final_waits = list(self.dma_cnt.items())

        def emit(e, h):
            known = {}
            for idx, rec in enumerate(self.ops[e]):
                need = {}
                for d in rec["deps"]:
                    if d[0] == "eng":
                        key = ("e", d[1])
                        val = sigcount[d[1]][d[2]]
                    else:
                        key = ("d", d[1])
                        val = d[2]
                    if val > need.get(key, 0):
                        need[key] = val
                for key, val in need.items():
                    if val > known.get(key, 0):
                        s = sems[key[1]] if key[0] == "e" else dsems[key[1]]
                        h.wait_ge(s, val)
                        known[key] = val
                ins = rec["fn"](h)
                if rec["dma_key"] is not None:
                    ins.then_inc(dsems[rec["dma_key"]], 16)
                elif rec["sig"]:
                    ins.then_inc(sems[e], 1)
            if e == final_eng:
                for k, c in final_waits:
                    h.wait_ge(dsems[k], c)

        @block.tensor
        def _(h):
            emit("pe", h)

        @block.scalar
        def _(h):
            emit("act", h)

        @block.vector
        def _(h):
            emit("dve", h)

        @block.gpsimd
        def _(h):
            emit("pool", h)

        @block.sync
        def _(h):
            emit("sp", h)

        stack.close()


def build_nc():
    nc = bass.Bass("TRN2", target_bir_lowering=False)
    st = contextlib.ExitStack()

    def din(name, shape):
        return nc.dram_tensor(name, list(shape), F32, kind="ExternalInput").ap()

    def dout(name, shape):
        return nc.dram_tensor(name, list(shape), F32, kind="ExternalOutput").ap()

    xT = din("xT", [D, NT])
    cT = din("cT", [D, 17])
    vecs = din("vecs", [DEPTH, 128, 80])
    gfin = din("gfin", [128, 8])
    gvb = din("gvb", [DEPTH, 128, D])
    wsT = din("wsT", [DEPTH, 4, 128, 128])
    bsp = din("bsp", [DEPTH, 1, 512])
    w00 = din("w00", [DEPTH, 16, 4])
    b00 = din("b00", [DEPTH, 1, 64])
    ident_d = din("ident", [128, 128])
    triu_d = din("triu", [128, 128])
    w_ada = din("w_ada", [DEPTH, D, 6 * D])
    w_in = din("w_in", [DEPTH, D, 5 * D])
    w_glu = din("w_glu", [DEPTH, D, D])
    w_out = din("w_out", [DEPTH, D, D])
    w_ff1 = din("w_ff1", [DEPTH, D, 4 * D])
    w_ff2 = din("w_ff2", [DEPTH, 4 * D, D])
    lamre_d = din("lamre", [DEPTH, 128, 32])
    lamim_d = din("lamim", [DEPTH, 128, 32])
    ldt_d = din("ldt", [DEPTH, 128, 32])
    Bre_d = din("Bre", [DEPTH, 128, 512])
    Bim_d = din("Bim", [DEPTH, 128, 512])
    Cre_d = din("Cre", [DEPTH, 128, 512])
    Cim_d = din("Cim", [DEPTH, 128, 512])
    msm_d = din("msm", [128, 4])
    bdm_d = din("bdm", [128, 128])
    bd8_d = din("bd8", [128, 8])
    tauv_d = din("tauv", [128, 16])
    h0_d = din("h0", [DEPTH, 128, 1024])
    sp_o = dout("sp_o", [DEPTH, 128, 64])
    ss_o = dout("ss_o", [DEPTH, 128, 1024])
    yT = dout("yT", [D, NT])
    gv_o = dout("gv_o", [DEPTH, 16, D])

    def sb(name, shape, dt):
        return st.enter_context(nc.sbuf_tensor("sb_" + name, list(shape), dt))

    x32 = sb("x32", [128, 8 * NT], F32)
    hb = sb("hb", [128, 8 * NT], BF16)
    bA = sb("bA", [128, 8 * NT], BF16)
    bB = sb("bB", [128, 8 * NT], BF16)
    NWS = 4
    wts = sb("wts", [128, NWS, 8, 128], BF16)
    wbig = bB[:, 0:8192].rearrange("p (k n) -> p k n", n=1024)
    ident = sb("ident", [128, 128], F32)
    identb = sb("identb", [128, 128], BF16)
    triu = sb("triu", [128, 128], F32)
    ones_m = sb("ones_m", [128, 128], BF16)
    ones_r = sb("ones_r", [1, 128], BF16)
    epsb = sb("epsb", [128, 1], F32)
    vec_sb = sb("vec_sb", [128, DEPTH, 80], F32)
    gfin_sb = sb("gfin_sb", [128, 8], F32)
    c_sb = sb("c_sb", [128, 8, 17], F32)
    cs_bf = sb("cs_bf", [128, 8, 17], BF16)
    mod = sb("mod", [128, 1, 48, 17], F32)
    modA = sb("modA", [128, 1, 2, 8, 17], F32)
    sq = sb("sq", [128, 2, 512], BF16)
    nrm = sb("nrm", [128, 2, 512], F32)
    tmp32 = sb("tmp32", [128, 2, 512], F32)
    tbf = sb("tbf", [128, 1024], F32)
    tb = tbf[:].bitcast(BF16).rearrange("p (a n) -> p a n", n=512)
    vg = nrm[:].rearrange("p a n -> p (a n)")
    vsq = tmp32[:].rearrange("p a n -> p (a n)")
    vs32 = vsq[0:16, :]
    vss = sb("vss", [128, 2], F32)
    gvb_sb = tbf
    vs_bf = sb("vs_bf", [16, D], BF16)
    wsT32 = sq[:].rearrange("p a n -> p (a n)").bitcast(F32).rearrange("p (h t) -> p h t", t=128)
    wsTb = sb("wsTb", [128, 4, 128], BF16)
    bspb = sb("bspb", [1, 512], BF16)
    w00_sb = sb("w00_sb", [16, 4], F32)
    WI = sb("WI", [16, 4, 16], BF16)
    b00b = sb("b00b", [1, 64], BF16)
    ystage = tbf[:].rearrange("p (a n) -> p a n", n=512)
    s5 = sb("s5", [128, 14, 32], F32)
    apw = sb("apw", [128, 2, 16, 32], F32)
    tauv = sb("tauv", [128, 16], F32)
    smt = sb("smt", [128, 320], F32)
    BCk = sb("BCk", [128, 2, 2, 64], F32)
    gcur = sb("gcur", [128, 2, 2, 2, 64], F32)
    gtmp = sb("gtmp", [128, 2, 2, 64], F32)
    msm = sb("msm", [128, 4], F32)
    bdm = sb("bdm", [128, 128], F32)
    bd8 = sb("bd8", [128, 8], F32)
    chn = sb("chn", [128, 2, 1, 32], F32)
    Qv = bB[:, 0:8192].rearrange("p (g r c) -> p g r c", g=32, r=2)
    KTc = bB[:, 8192:10240].rearrange("p (k t h) -> p k t h", k=8, t=16)
    arT = bB[:, 10240:14336].rearrange("p (t r m) -> p t r m", t=16, r=2)
    arGWs = [bB[:, 14336 + 1024 * i:15360 + 1024 * i].rearrange("p (t r m) -> p t r m", t=4, r=2) for i in range(2)]
    arKTx = bB[:, 14336:16384].rearrange("p (t m) -> p t m", t=16)
    h0f = tbf[:].rearrange("p (g r t) -> p g r t", g=32, r=2)
    h0b = sq[:].rearrange("p a n -> p (a n)").rearrange("p (g r t) -> p g r t", g=32, r=2)
    Ssm = nrm[:].rearrange("p a n -> p (a n)").rearrange("p (g r t) -> p g r t", g=32, r=2)
    pst = [st.enter_context(nc.psum_tensor("ps%d" % i, [128, 512], F32)) for i in range(8)]

    P = Prog(nc)
    PHASES.clear()

    def mark(name):
        PHASES.append((name, len(P.ops['pe'])))
    ctr = {"ws": 0, "ps": 0, "u": 0, "gw": 0}

    def xk(k, t0, tn):
        return x32[:, k * NT + t0:k * NT + t0 + tn]

    def bufk(b, k, t0, tn):
        return b[:, k * NT + t0:k * NT + t0 + tn]

    def R(pref, k, ti):
        return "%s%d_%d" % (pref, k, ti)

    def A_overlap(lo, hi):
        out = []
        for kk in range(8):
            for ti_, (t0_, tn_) in enumerate(TT):
                a0 = kk * NT + t0_
                if a0 < hi and a0 + tn_ > lo:
                    out.append(R("A", kk, ti_))
        return out

    def B_overlap(lo, hi):
        out = []
        for kk in range(8):
            for ti_, (t0_, tn_) in enumerate(TT):
                a0 = kk * NT + t0_
                if a0 < hi and a0 + tn_ > lo:
                    out.append(R("B", kk, ti_))
        return out

    def vtm_overlap(lo, hi):
        return ["vtm%d" % i_ for i_ in range(16) if i_ * 1024 < hi and (i_ + 1) * 1024 > lo]

    brange = [0, 8]

    def next_bank():
        lo, hi = brange
        b = lo + ctr["ps"] % (hi - lo)
        ctr["ps"] += 1
        return b

    def uid():
        ctr["u"] += 1
        return ctr["u"]

    P.op("sp", lambda h: h.dma_start(out=ident[:], in_=ident_d), writes=["ident"], dma_key="ident")
    P.op("sp", lambda h: h.dma_start(out=triu[:], in_=triu_d), writes=["triu"], dma_key="triu")
    P.op("sp", lambda h: h.dma_start(out=vec_sb[:], in_=vecs.rearrange("l p n -> p l n")), writes=["vec"], dma_key="vec")
    P.op("sp", lambda h: h.dma_start(out=gfin_sb[:], in_=gfin), writes=["gfin"], dma_key="gfin")
    P.op("sp", lambda h: h.dma_start(out=c_sb[:], in_=cT.rearrange("(k p) n -> p k n", p=128)), writes=["c_sb"], dma_key="c_sb")
    P.op("sp", lambda h: h.dma_start(out=tauv[:], in_=tauv_d), writes=["tauv"], dma_key="tauv")
    P.op("sp", lambda h: h.dma_start(out=msm[:], in_=msm_d), writes=["msm"], dma_key="msm")
    P.op("sp", lambda h: h.dma_start(out=bdm[:], in_=bdm_d), writes=["bdm"], dma_key="bdm")
    P.op("sp", lambda h: h.dma_start(out=bd8[:], in_=bd8_d), writes=["bd8"], dma_key="bd8")
    P.op("dve", lambda h: h.tensor_copy(out=identb[:], in_=ident[:]), reads=["ident"], writes=["identb"])
    P.op("dve", lambda h: h.memset(ones_m[:], 1.0 / 1024.0), writes=["ones_m"])
    P.op("dve", lambda h: h.memset(ones_r[:], 1.0), writes=["ones_r"])
    P.op("dve", lambda h: h.memset(epsb[:], EPS), writes=["epsb"])
    for k in range(8):
        for ti, (t0, tn) in enumerate(TT):
            P.op("sp", lambda h, k=k, t0=t0, tn=tn: h.dma_start(out=xk(k, t0, tn), in_=xT[k * 128:(k + 1) * 128, t0:t0 + tn]),
                 writes=[R("x", k, ti)], dma_key="xl%d_%d" % (k, ti))
    P.op("act", lambda h: h.activation(out=cs_bf[:], in_=c_sb[:], func=AF.Silu), reads=["c_sb"], writes=["cs_bf"])

    def load_wtile(wd_ap, K=8):
        s = ctr["ws"] % NWS
        ctr["ws"] += 1
        P.op("pool", lambda h: h.dma_start(out=wts[:, s, 0:K, :], in_=wd_ap.rearrange("(k p) n -> p k n", p=128)),
             writes=["wt%d" % s], dma_key="wt%d" % s)
        return s, "wt%d" % s

    def norm_fm(scale_fn, bias_fn, dst, dst_pref, final=False):
        def stage_a(ti):
            t0, tn = TT[ti]
            u = ti % 2
            bank = next_bank()
            rs = "msb" if u == 0 else "rstd"
            for k in range(8):
                w = uid() % 2
                P.op("act", lambda h, k=k, w=w: h.activation(out=sq[:, w, 0:tn], in_=xk(k, t0, tn), func=AF.Square),
                     reads=[R("x", k, ti)], writes=["sq%d" % w])
                P.op("pe", lambda h, k=k, w=w: h.matmul(pst[bank][:, 0:tn], lhsT=ones_m[:], rhs=sq[:, w, 0:tn], start=(k == 0), stop=(k == 7)),
                     reads=["ones_m", "sq%d" % w], writes=["ps%d" % bank])
            P.op("act", lambda h: h.activation(out=nrm[:, u, 0:tn], in_=pst[bank][:, 0:tn], func=AF.Sqrt, bias=epsb[:, 0:1]),
                 reads=["ps%d" % bank, "epsb"], writes=[rs])
            P.op("dve", lambda h: h.reciprocal(out=nrm[:, u, 0:tn], in_=nrm[:, u, 0:tn]), reads=[rs], writes=[rs])

        def stage_b(ti):
            t0, tn = TT[ti]
            u = ti % 2
            rs = "msb" if u == 0 else "rstd"
            for k in range(8):
                if final:
                    eng = "dve"
                    P.op(eng, lambda h, k=k: h.scalar_tensor_tensor(out=xk(k, t0, tn), in0=xk(k, t0, tn), scalar=gfin_sb[:, k:k + 1], in1=nrm[:, u, 0:tn],
                                                                    op0=ALU.mult, op1=ALU.mult),
                         reads=[R("x", k, ti), rs, "gfin"], writes=[R("x", k, ti)])
                    continue
                v = uid() % 2
                P.op("dve", lambda h, k=k, v=v: h.tensor_tensor(out=tmp32[:, v, 0:tn], in0=xk(k, t0, tn), in1=nrm[:, u, 0:tn], op=ALU.mult),
                     reads=[R("x", k, ti), rs], writes=["tmp32_%d" % v])
                if ti < 4:
                    if k % 2 == 0:
                        P.op("act", lambda h, k=k, v=v: h.activation(out=bufk(dst, k, t0, tn), in_=tmp32[:, v, 0:tn], func=AF.Identity,
                                                                     scale=scale_fn(k, 0), bias=bias_fn(k, 0)),
                             reads=["tmp32_%d" % v, "modA", "mod"], writes=[R(dst_pref, k, ti)])
                    else:
                        P.op("dve", lambda h, k=k, v=v: h.tensor_scalar(out=bufk(dst, k, t0, tn), in0=tmp32[:, v, 0:tn],
                                                                        scalar1=scale_fn(k, 0), scalar2=bias_fn(k, 0), op0=ALU.mult, op1=ALU.add),
                             reads=["tmp32_%d" % v, "modA", "mod"], writes=[R(dst_pref, k, ti)])
                else:
                    P.op("dve", lambda h, k=k, v=v: h.tensor_tensor(out=tmp32[:, v, 0:tn], in0=tmp32[:, v, 0:tn], in1=scale_fn(k, 1), op=ALU.mult),
                         reads=["tmp32_%d" % v, "modA"], writes=["tmp32_%d" % v])
                    P.op("dve", lambda h, k=k, v=v: h.tensor_tensor(out=bufk(dst, k, t0, tn), in0=tmp32[:, v, 0:tn], in1=bias_fn(k, 1), op=ALU.add),
                         reads=["tmp32_%d" % v, "mod"], writes=[R(dst_pref, k, ti)])

        stage_a(0)
        for ti in range(5):
            if ti + 1 < 5:
                stage_a(ti + 1)
            stage_b(ti)
        if final:
            for k in range(8):
                P.op("sp" if k % 2 == 0 else "act", lambda h, k=k: h.dma_start(out=yT[k * 128:(k + 1) * 128, :], in_=x32[:, k * NT:(k + 1) * NT]),
                     reads=[R("x", k, ti_) for ti_ in range(5)], writes=["yT_%d" % k], dma_key="yst%d" % k)

    def multi_proj(specs, M, evac):
        look = 1 if 2 * len(specs) <= NWS else 0
        pending = {}

        def issue(m):
            if m < M and m not in pending:
                pending[m] = [load_wtile(wfn(m), K) for (wfn, src, sp_, K) in specs]

        for m in range(M):
            issue(m)
            slots = pending.pop(m)
            if look:
                issue(m + 1)
            for ti, (t0, tn) in enumerate(TT):
                banks = []
                for (wfn, src, sp_, K), (s, sr) in zip(specs, slots):
                    bank = next_bank()
                    banks.append(bank)
                    for k in range(K):
                        P.op("pe", lambda h, s=s, k=k, K=K, src=src, t0=t0, tn=tn, bank=bank: h.matmul(
                            pst[bank][:, 0:tn], lhsT=wts[:, s, k, :], rhs=bufk(src, k, t0, tn), start=(k == 0), stop=(k == K - 1)),
                            reads=[sr, R(sp_, k, ti)], writes=["ps%d" % bank])
                evac(m, ti, t0, tn, banks)
            bg_step()

    def resid_evac(l, gate_idx):
        def f(m, ti, t0, tn, banks):
            bank = banks[0]
            g = gate_idx * 8 + m
            if ti < 4:
                P.op("dve", lambda h: h.scalar_tensor_tensor(out=xk(m, t0, tn), in0=pst[bank][:, 0:tn], scalar=mod[:, 0, g, 0:1],
                                                             in1=xk(m, t0, tn), op0=ALU.mult, op1=ALU.add),
                     reads=["ps%d" % bank, "mod", R("x", m, ti)], writes=[R("x", m, ti)])
            else:
                v = uid() % 2
                P.op("dve", lambda h: h.tensor_tensor(out=tmp32[:, v, 0:tn], in0=pst[bank][:, 0:tn], in1=mod[:, 0, g, 1:17], op=ALU.mult),
                     reads=["ps%d" % bank, "mod"], writes=["tmp32_%d" % v])
                P.op("dve", lambda h: h.tensor_tensor(out=xk(m, t0, tn), in0=xk(m, t0, tn), in1=tmp32[:, v, 0:tn], op=ALU.add),
                     reads=["tmp32_%d" % v, R("x", m, ti)], writes=[R("x", m, ti)])
        return f

    def ada_gen(l, dmod, dmodA, mres, ares, banks=None):
        for half in range(2):
            bank = next_bank() if banks is None else banks[half]
            m0 = half * 24
            for m in range(m0, m0 + 24):
                s_, sr = load_wtile(w_ada[l, :, m * 128:(m + 1) * 128])
                for k in range(8):
                    P.op("pe", lambda h, s_=s_, k=k, m=m, bank=bank, m0=m0: h.matmul(
                        pst[bank][:, (m - m0) * 17:(m - m0 + 1) * 17], lhsT=wts[:, s_, k, :], rhs=cs_bf[:, k, :],
                        start=(k == 0), stop=(k == 7)), reads=[sr, "cs_bf"], writes=["ps%d" % bank])
                yield
            P.op("dve", lambda h, m0=m0, bank=bank: h.tensor_tensor(
                out=dmod[:, m0:m0 + 24, :], in0=pst[bank][:, 0:24 * 17].rearrange("p (m n) -> p m n", n=17),
                in1=vec_sb[:, l, m0:m0 + 24].unsqueeze(2).to_broadcast([128, 24, 17]), op=ALU.add),
                reads=["ps%d" % bank, "vec"], writes=mres)
        for j in range(2):
            sc0 = 8 + 24 * j
            P.op("dve", lambda h, j=j, sc0=sc0: h.tensor_scalar(out=dmodA[:, j, :, :], in0=dmod[:, sc0:sc0 + 8, :],
                                                                scalar1=1.0, scalar2=None, op0=ALU.add),
                 reads=mres, writes=ares)
            P.op("dve", lambda h, j=j: h.tensor_tensor(
                out=dmodA[:, j, :, :], in0=dmodA[:, j, :, :],
                in1=vec_sb[:, l, 48 + 8 * j:56 + 8 * j].unsqueeze(2).to_broadcast([128, 8, 17]), op=ALU.mult),
                reads=ares + ["vec"], writes=ares)
        yield

    modS = apw[:].rearrange("p a t g -> p (a t g)")[:, 0:816].rearrange("p (m n) -> p m n", n=17)
    modAS = gcur[:].rearrange("p a b c d -> p (a b c d)")[:, 0:272].rearrange("p (j k n) -> p j k n", j=2, k=8)
    bg = {"gen": None}

    def bg_step(n=1):
        g = bg["gen"]
        for _ in range(n):
            if g is None:
                return
            try:
                next(g)
            except StopIteration:
                bg["gen"] = None
                return

    TWO_PI = 2.0 * float(np.pi)
    S5R = ["Qh0", "Qh1", "KTc", "arT0", "arT1", "arT2", "arT3", "arGW0", "arGW1", "arKTx"]
    ALLB = [R("B", kk, ti_) for kk in range(8) for ti_ in range(5)]
    L_RE, L_IM, L_DT, L_LR, L_TH, C_R, C_I, A16R, A16I, A256R, A256I, SC0, SC1, SC2 = range(14)

    def S(i):
        return s5[:, i, :]

    def dv(fn, reads, writes, eng="dve"):
        P.op(eng, fn, reads=reads, writes=writes)

    def s5_tt(o, a, b, op, eng="dve"):
        dv(lambda h: h.tensor_tensor(out=S(o), in0=S(a), in1=S(b), op=op), ["s5"], ["s5"], eng)

    def s5_ts(o, a, s1, op0):
        dv(lambda h: h.tensor_scalar(out=S(o), in0=S(a), scalar1=s1, scalar2=None, op0=op0), ["s5"], ["s5"])

    T_A = tmp32[:, 0, :]
    T_B = tmp32[:, 1, :]
    T_C = nrm[:, 0, :]
    T_D = nrm[:, 1, :]
    T_I = sq[:].rearrange("p a n -> p (a n)").bitcast(I32)
    TR = ["tmp32_0", "tmp32_1", "msb", "rstd", "sq0", "sq1"]

    def rr_big(t, r):
        dv(lambda h: h.tensor_copy(out=T_I, in_=t), TR, TR)
        dv(lambda h: h.tensor_copy(out=T_D, in_=T_I), TR, TR)
        dv(lambda h: h.tensor_tensor(out=r, in0=t, in1=T_D, op=ALU.subtract), TR, TR)
        dv(lambda h: h.tensor_single_scalar(out=T_D, in_=r, scalar=0.5, op=ALU.is_gt), TR, TR)
        dv(lambda h: h.tensor_tensor(out=r, in0=r, in1=T_D, op=ALU.subtract), TR, TR)
        dv(lambda h: h.tensor_single_scalar(out=T_D, in_=r, scalar=-0.5, op=ALU.is_lt), TR, TR)
        dv(lambda h: h.tensor_tensor(out=r, in0=r, in1=T_D, op=ALU.add), TR, TR)

    def cview(t):
        return t.rearrange("p (q h) -> p q h", q=4)

    def bc4(tab, k):
        return tab[:, 4 * k:4 * k + 4].unsqueeze(2).to_broadcast([128, 4, 16])

    def cmul(eng, dst_re, dst_im, src_re, src_im, fr, fi, res):
        t1, t2 = cview(gtmp[:, 0 if eng == "dve" else 1, 0, :]), cview(gtmp[:, 0 if eng == "dve" else 1, 1, :])
        tr = "gtmp_" + eng
        dv(lambda h: h.tensor_tensor(out=t1, in0=src_re, in1=fr, op=ALU.mult), res + ["s5", "apw"], [tr], eng)
        dv(lambda h: h.tensor_tensor(out=t2, in0=src_im, in1=fi, op=ALU.mult), res + ["s5", "apw", tr], [tr], eng)
        dv(lambda h: h.tensor_tensor(out=dst_re, in0=t1, in1=t2, op=ALU.subtract), [tr] + res, res, eng)
        dv(lambda h: h.tensor_tensor(out=t1, in0=src_re, in1=fi, op=ALU.mult), res + ["s5", "apw", tr], [tr], eng)
        dv(lambda h: h.tensor_tensor(out=t2, in0=src_im, in1=fr, op=ALU.mult), res + ["s5", "apw", tr], [tr], eng)
        dv(lambda h: h.tensor_tensor(out=dst_im, in0=t1, in1=t2, op=ALU.add), [tr] + res, res, eng)

    def expand(eng, dst, src, mcol, reads, writes):
        dv(lambda h: h.tensor_tensor(out=dst.rearrange("p (q g h) -> p q g h", q=4, g=2),
                                     in0=src.unsqueeze(2).to_broadcast([128, 4, 2, 16]),
                                     in1=msm[:, mcol:mcol + 2].unsqueeze(1).unsqueeze(3).to_broadcast([128, 4, 2, 16]), op=ALU.mult),
           reads + ["msm"], writes, eng)

    def gen_quarter(eng, k, qq, X_re, X_im, Y_re, Y_im, out_re, out_im, reads, writes):
        ar_ = apw[:, 0, 4 * qq:4 * qq + 4, 4 * k:4 * k + 4].unsqueeze(3).to_broadcast([128, 4, 4, 32])
        ai_ = apw[:, 1, 4 * qq:4 * qq + 4, 4 * k:4 * k + 4].unsqueeze(3).to_broadcast([128, 4, 4, 32])
        t1 = T_A.rearrange("p (t q m) -> p t q m", t=4, q=4)
        t2 = T_B.rearrange("p (t q m) -> p t q m", t=4, q=4)
        bx = lambda X: X.rearrange("p (q m) -> p q m", q=4).unsqueeze(1).to_broadcast([128, 4, 4, 32])
        o4 = lambda O: O.rearrange("p t (q m) -> p t q m", q=4)
        rd = reads + ["apw"]
        dv(lambda h: h.tensor_tensor(out=t1, in0=ar_, in1=bx(X_re), op=ALU.mult), rd, ["tmp32_0"], eng)
        dv(lambda h: h.tensor_tensor(out=t2, in0=ai_, in1=bx(X_im), op=ALU.mult), rd, ["tmp32_1"], eng)
        dv(lambda h: h.tensor_tensor(out=o4(out_re), in0=t1, in1=t2, op=ALU.subtract), ["tmp32_0", "tmp32_1"], writes, eng)
        dv(lambda h: h.tensor_tensor(out=t1, in0=ar_, in1=bx(Y_im), op=ALU.mult), rd, ["tmp32_0"], eng)
        dv(lambda h: h.tensor_tensor(out=t2, in0=ai_, in1=bx(Y_re), op=ALU.mult), rd, ["tmp32_1"], eng)
        dv(lambda h: h.tensor_tensor(out=o4(out_im), in0=t1, in1=t2, op=ALU.add), ["tmp32_0", "tmp32_1"], writes, eng)

    def s5_tables(l):
        P.op("sp", lambda h: h.dma_start(out=s5[:, L_RE, :], in_=lamre_d[l]), writes=["s5"], dma_key="s5a")
        P.op("sp", lambda h: h.dma_start(out=s5[:, L_IM, :], in_=lamim_d[l]), writes=["s5"], dma_key="s5b")
        P.op("sp", lambda h: h.dma_start(out=s5[:, L_DT, :], in_=ldt_d[l]), writes=["s5"], dma_key="s5c")
        dv(lambda h: h.activation(out=S(L_DT), in_=S(L_DT), func=AF.Exp), ["s5"], ["s5"], "act")
        s5_tt(L_LR, L_RE, L_DT, ALU.mult)
        s5_tt(L_TH, L_IM, L_DT, ALU.mult)
        s5_ts(L_TH, L_TH, 1.0 / TWO_PI, ALU.mult)
        b3 = lambda tab: tab.unsqueeze(1).to_broadcast([128, 16, 32])
        tv = tauv[:].unsqueeze(2).to_broadcast([128, 16, 32])
        v3 = lambda t: t.rearrange("p (t g) -> p t g", t=16)
        dv(lambda h: h.tensor_tensor(out=v3(T_A), in0=b3(S(L_LR)), in1=tv, op=ALU.mult), ["s5", "tauv"] + TR, TR)
        dv(lambda h: h.activation(out=T_A, in_=T_A, func=AF.Exp), TR, TR, "act")
        dv(lambda h: h.tensor_tensor(out=v3(T_B), in0=b3(S(L_TH)), in1=tv, op=ALU.mult), ["s5", "tauv"] + TR, TR)
        rr_big(T_B, T_C)
        dv(lambda h: h.activation(out=T_C, in_=T_C, func=AF.Sin, scale=6.28318), TR, TR, "act")
        dv(lambda h: h.tensor_tensor(out=apw[:, 1, :, :], in0=v3(T_A), in1=v3(T_C), op=ALU.mult), TR, ["apw"])
        dv(lambda h: h.tensor_scalar(out=T_B, in0=T_B, scalar1=0.25, scalar2=None, op0=ALU.add), TR, TR)
        rr_big(T_B, T_C)
        dv(lambda h: h.activation(out=T_C, in_=T_C, func=AF.Sin, scale=6.28318), TR, TR, "act")
        dv(lambda h: h.tensor_tensor(out=apw[:, 0, :, :], in0=v3(T_A), in1=v3(T_C), op=ALU.mult), TR, ["apw"])
        AR, AI = apw[:, 0, 1, :], apw[:, 1, 1, :]
        dv(lambda h: h.tensor_tensor(out=S(SC0), in0=S(L_RE), in1=S(L_RE), op=ALU.mult), ["s5"], ["s5"])
        dv(lambda h: h.tensor_tensor(out=S(SC1), in0=S(L_IM), in1=S(L_IM), op=ALU.mult), ["s5"], ["s5"])
        s5_tt(SC0, SC0, SC1, ALU.add)
        dv(lambda h: h.reciprocal(out=S(SC0), in_=S(SC0)), ["s5"], ["s5"])
        dv(lambda h: h.tensor_scalar(out=S(SC1), in0=AR, scalar1=-1.0, scalar2=None, op0=ALU.add), ["s5", "apw"], ["s5"])
        s5_tt(C_R, SC1, L_RE, ALU.mult)
        dv(lambda h: h.tensor_tensor(out=S(SC2), in0=AI, in1=S(L_IM), op=ALU.mult), ["s5", "apw"], ["s5"])
        s5_tt(C_R, C_R, SC2, ALU.add)
        s5_tt(C_R, C_R, SC0, ALU.mult)
        dv(lambda h: h.tensor_tensor(out=S(C_I), in0=AI, in1=S(L_RE), op=ALU.mult), ["s5", "apw"], ["s5"])
        s5_tt(SC2, SC1, L_IM, ALU.mult)
        s5_tt(C_I, C_I, SC2, ALU.subtract)
        s5_tt(C_I, C_I, SC0, ALU.mult)
        dv(lambda h: h.tensor_copy(out=S(A16R), in_=AR), ["s5", "apw"], ["s5"])
        dv(lambda h: h.tensor_copy(out=S(A16I), in_=AI), ["s5", "apw"], ["s5"])
        for rr_, ii_, n_ in ((A16R, A16I, 4), (A256R, A256I, 4)):
            if rr_ == A256R:
                dv(lambda h: h.tensor_copy(out=S(A256R), in_=S(A16R)), ["s5"], ["s5"])
                dv(lambda h: h.tensor_copy(out=S(A256I), in_=S(A16I)), ["s5"], ["s5"])
            for _ in range(n_):
                s5_tt(SC0, rr_, rr_, ALU.mult)
                s5_tt(SC1, ii_, ii_, ALU.mult)
                s5_tt(SC2, rr_, ii_, ALU.mult)
                s5_tt(rr_, SC0, SC1, ALU.subtract)
                s5_ts(ii_, SC2, 2.0, ALU.mult)

    def s5_layer(l):
        dv(lambda h: h.memset(gtmp[:, 0, 0, 0:1], 0.0), [], ALLB + S5R + ["gtmp_dve"])
        P.op("sp", lambda h: h.dma_start(out=h0f.rearrange("p g r t -> p (g r t)"), in_=h0_d[l]), writes=["tb0", "tb1", "tb2", "tb3"], dma_key="h0")
        dv(lambda h: h.tensor_copy(out=h0b, in_=h0f), ["tb0", "tb1", "tb2", "tb3"], ["sq0", "sq1"])
        AR, AI = apw[:, 0, 1, :], apw[:, 1, 1, :]
        XB = lambda i: gcur[:, 0, i // 2, i % 2, :].rearrange("p (a b) -> p a b", a=1)[:, 0, :]

        def sview(k):
            return bA[:, k * NT:k * NT + 2048].rearrange("p (i c) -> p i c", i=16)

        def zview(k):
            return bA[:, k * NT:k * NT + 2048].rearrange("p (c i) -> p i c", i=16)

        gflat = gcur[:].rearrange("p a b c d -> p (a b c d)")
        X0, X1, X2, X3 = (gflat[:, i * 128:(i + 1) * 128] for i in range(4))
        arWC0 = X2.bitcast(BF16).rearrange("p (r m) -> p r m", r=2)
        Bk_re, Bk_im = cview(gtmp[:, 0, 0, :]), cview(gtmp[:, 0, 1, :])

        wslots = {}
        qstate = {}

        def pool_prologue(k):
            for bc_, (dre, dim_) in enumerate([(Bre_d, Bim_d), (Cre_d, Cim_d)]):
                P.op("sp", lambda h, bc_=bc_, dre=dre: h.dma_start(out=BCk[:, bc_, 0, :], in_=dre[l, :, 64 * k:64 * k + 64]),
                     writes=["BCk%d" % bc_], dma_key="bck%d0" % bc_)
                P.op("sp", lambda h, bc_=bc_, dim_=dim_: h.dma_start(out=BCk[:, bc_, 1, :], in_=dim_[l, :, 64 * k:64 * k + 64]),
                     writes=["BCk%d" % bc_], dma_key="bck%d1" % bc_)
            if k not in wslots:
                wslots[k] = load_wtile(w_in[l, :, 2048 + k * 128:2048 + (k + 1) * 128])
            if k + 1 < 8:
                wslots[k + 1] = load_wtile(w_in[l, :, 2048 + (k + 1) * 128:2048 + (k + 2) * 128])
            expand("pool", arWC0[:, 0, :], cview(BCk[:, 1, 0, :]), 0, ["BCk1"], ["gX2"])
            expand("pool", arWC0[:, 1, :], cview(BCk[:, 1, 1, :]), 2, ["BCk1"], ["gX2"])
            cmul("pool", cview(smt[:, 0:64]), cview(smt[:, 64:128]), cview(BCk[:, 0, 0, :]), cview(BCk[:, 0, 1, :]),
                 bc4(S(C_R), k), bc4(S(C_I), k), ["smt", "BCk0"])
            expand("pool", X0, cview(smt[:, 0:64]), 0, ["smt"], ["gX"])
            expand("pool", X1, cview(smt[:, 64:128]), 0, ["smt"], ["gX"])

        def sproj(k):
            slot, sr = wslots.pop(k)
            for ti, (t0, tn) in enumerate(TT):
                bank = next_bank()
                for kk in range(8):
                    P.op("pe", lambda h, kk=kk, slot=slot, t0=t0, tn=tn, bank=bank: h.matmul(
                        pst[bank][:, 0:tn], lhsT=wts[:, slot, kk, :], rhs=bufk(hb, kk, t0, tn), start=(kk == 0), stop=(kk == 7)),
                        reads=[sr, R("h", kk, ti)], writes=["ps%d" % bank])
                if ti < 4:
                    P.op("act", lambda h, ti=ti, bank=bank: h.activation(out=sview(k)[:, :, 32 * ti:32 * ti + 32],
                                                                         in_=pst[bank][:, 0:512].rearrange("p (c i) -> p i c", i=16), func=AF.Copy),
                         reads=["ps%d" % bank], writes=[R("A", k, t_) for t_ in range(4)] + vtm_overlap(k * NT, k * NT + 2048))
                else:
                    P.op("act", lambda h, t0=t0, tn=tn, bank=bank: h.activation(out=bufk(bA, k, t0, tn), in_=pst[bank][:, 0:tn], func=AF.Copy),
                         reads=["ps%d" % bank], writes=[R("A", k, ti)] + vtm_overlap(k * NT + t0, k * NT + t0 + tn))

        def quarter_gen(k, qq):
            gi = ctr["gw"] % 2
            ctr["gw"] += 1
            arGW, gwr = arGWs[gi], "arGW%d" % gi
            qstate[(k, qq)] = [arGW, gwr, None]
            gen_quarter("dve", k, qq, X0, X1, X0, X1, arGW[:, :, 0, :], arGW[:, :, 1, :], ["gX"], [gwr, "arKTx"])

        def quarter_pe(k, qq):
            arGW, gwr, _ = qstate[(k, qq)]
            bt = next_bank()
            psT = pst[bt][:].bitcast(BF16)
            for t4 in range(4):
                for ri in range(2):
                    P.op("pe", lambda h, t4=t4, ri=ri: h.transpose(psT[:, (t4 * 2 + ri) * 128:(t4 * 2 + ri + 1) * 128], arGW[:, t4, ri, :], identb[:]),
                         reads=[gwr, "identb"], writes=["ps%d" % bt])
            P.op("act", lambda h: h.activation(out=arT[:, 4 * qq:4 * qq + 4, :, :].rearrange("p t r m -> p (t r m)"), in_=psT, func=AF.Copy),
                 reads=["ps%d" % bt], writes=["arT%d" % qq])
            bk = next_bank()
            qstate[(k, qq)][2] = bk
            for t4 in range(4):
                P.op("pe", lambda h, t4=t4: h.matmul(pst[bk][:, t4 * 128:(t4 + 1) * 128], lhsT=arGW[:, t4, 0, :], rhs=arWC0[:, 0, :], start=True, stop=False),
                     reads=[gwr, "gX2"], writes=["ps%d" % bk])
                P.op("pe", lambda h, t4=t4: h.matmul(pst[bk][:, t4 * 128:(t4 + 1) * 128], lhsT=arGW[:, t4, 1, :], rhs=arWC0[:, 1, :], start=False, stop=True),
                     reads=[gwr, "gX2"], writes=["ps%d" % bk])

        def quarter_evac(k, qq):
            bk = qstate.pop((k, qq))[2]
            dv(lambda h: h.tensor_tensor(out=pst[bk][:].rearrange("p (t m) -> p t m", t=4), in0=pst[bk][:].rearrange("p (t m) -> p t m", t=4),
                                         in1=bdm[:].unsqueeze(1).to_broadcast([128, 4, 128]), op=ALU.mult),
               ["ps%d" % bk, "bdm"], ["ps%d" % bk])
            if qq == 0:
                dv(lambda h: h.scalar_tensor_tensor(out=pst[bk][:, 0:128], in0=ident[:], scalar=vec_sb[:, l, 64 + k:65 + k], in1=pst[bk][:, 0:128],
                                                    op0=ALU.mult, op1=ALU.add), ["ps%d" % bk, "ident", "vec"], ["ps%d" % bk])
            dv(lambda h: h.tensor_reduce(out=smt[:, 256:320].rearrange("p (t h) -> p t h", t=4),
                                         in_=pst[bk][:].rearrange("p (t g h) -> p t h g", t=4, g=8), axis=AX.X, op=ALU.add),
               ["ps%d" % bk], ["smt2"])
            dv(lambda h: h.tensor_copy(out=KTc[:, k, 4 * qq:4 * qq + 4, :], in_=smt[:, 256:320].rearrange("p (t h) -> p t h", t=4)),
               ["smt2"], ["KTc"])

        def states(k):
            sb_ = [next_bank() for _ in range(4)]
            for jq in range(4):
                for q in range(4):
                    bank = sb_[q]
                    for j in range(4 * jq, 4 * jq + 4):
                        for ri in range(2):
                            first = (j == 0 and ri == 0)
                            P.op("pe", lambda h, q=q, ri=ri, j=j, bank=bank, first=first: h.matmul(
                                pst[bank][:, ri * 128:(ri + 1) * 128], lhsT=arT[32 * q:32 * q + 32, 15 - j, ri, :], rhs=sview(k)[32 * q:32 * q + 32, j, :],
                                start=first, stop=first, skip_group_check=(not first), tile_position=(32 * q, 0)),
                                reads=["arT%d" % (3 - jq)] + [R("A", k, t_) for t_ in range(4)], writes=["ps%d" % bank])
            for q in range(4):
                bank = sb_[q]
                for ri in range(2):
                    P.op("pe", lambda h, q=q, ri=ri, bank=bank: h.matmul(
                        pst[bank][:, 256 + ri * 16:256 + (ri + 1) * 16], lhsT=arT[32 * q:32 * q + 32, 0, ri, :], rhs=bA[32 * q:32 * q + 32, k * NT + 2048:k * NT + 2064],
                        start=False, stop=False, skip_group_check=True, tile_position=(32 * q, 0)),
                        reads=["arT0", R("A", k, 4)], writes=["ps%d" % bank])
                P.op("act", lambda h, q=q, bank=bank: h.activation(
                    out=Qv[:, 4 * k + q, :, :], in_=pst[bank][:, 0:256].rearrange("p (r c) -> p r c", r=2), func=AF.Copy),
                    reads=["ps%d" % bank], writes=["Qh%d" % (k // 4)])
                P.op("act", lambda h, q=q, bank=bank: h.activation(
                    out=Ssm[:, 4 * k + q, :, :], in_=pst[bank][:, 256:288].rearrange("p (r t) -> p r t", r=2), func=AF.Copy),
                    reads=["ps%d" % bank], writes=["msb", "rstd"])

        def sample_state(k):
            h0k = h0f[:, 4 * k:4 * k + 4, :, :]
            t13 = smt[:, 0:128].rearrange("p (q r t) -> p q r t", q=4, r=2)
            t24 = smt[:, 128:256].rearrange("p (q r t) -> p q r t", q=4, r=2)
            arb = AR[:, 4 * k:4 * k + 4].unsqueeze(2).unsqueeze(3).to_broadcast([128, 4, 2, 16])
            aib = AI[:, 4 * k:4 * k + 4].unsqueeze(2).unsqueeze(3).to_broadcast([128, 4, 2, 16])
            Sk = Ssm[:, 4 * k:4 * k + 4, :, :]
            HN = ["msb", "rstd"]
            dv(lambda h: h.tensor_tensor(out=t13, in0=h0k, in1=arb, op=ALU.mult), ["tb0", "tb1", "tb2", "tb3", "apw"], ["smt"], "pool")
            dv(lambda h: h.tensor_tensor(out=t24, in0=h0k, in1=aib, op=ALU.mult), ["tb0", "tb1", "tb2", "tb3", "apw"], ["smt"], "pool")
            dv(lambda h: h.tensor_tensor(out=Sk, in0=Sk, in1=t13, op=ALU.add), ["smt"] + HN, HN, "pool")
            dv(lambda h: h.tensor_tensor(out=Sk[:, :, 0, :], in0=Sk[:, :, 0, :], in1=t24[:, :, 1, :], op=ALU.subtract), ["smt"] + HN, HN, "pool")
            dv(lambda h: h.tensor_tensor(out=Sk[:, :, 1, :], in0=Sk[:, :, 1, :], in1=t24[:, :, 0, :], op=ALU.add), ["smt"] + HN, HN, "pool")

        for k in range(8):
            pool_prologue(k)
            if k > 0:
                sample_state(k - 1)
            quarter_gen(k, 3)
            sproj(k)
            quarter_pe(k, 3)
            quarter_gen(k, 2)
            quarter_pe(k, 2)
            quarter_evac(k, 3)
            quarter_gen(k, 1)
            quarter_pe(k, 1)
            quarter_evac(k, 2)
            quarter_gen(k, 0)
            quarter_pe(k, 0)
            quarter_evac(k, 1)
            quarter_evac(k, 0)
            states(k)
        sample_state(7)
        P.op("sp", lambda h: h.dma_start(out=ss_o[l], in_=Ssm.rearrange("p g r t -> p (g r t)")), reads=["msb", "rstd"], writes=["ss_o%d" % l], dma_key="sso")

        mark('L%d chain' % l)
        for hf, eng in enumerate(["dve", "pool"]):
            gs = slice(16 * hf, 16 * hf + 16)
            base = tmp32[:].rearrange("p a n -> p (a n)") if hf == 0 else nrm[:].rearrange("p a n -> p (a n)")
            cr_ = ["tmp32_0", "tmp32_1"] if hf == 0 else ["msb", "rstd"]
            v4 = lambda t: t.rearrange("p (g r b) -> p g r b", g=16, r=2)
            Hs, T13, T24, Nn = (v4(base[:, i * 256:(i + 1) * 256]) for i in range(4))
            qr = "Qh%d" % hf
            Qb = Qv[:, gs, :, :].rearrange("p g r (b i) -> p g r b i", i=16)
            bc_ = lambda idx, shp: s5[:, idx, gs].unsqueeze(2).unsqueeze(3).to_broadcast(shp)
            ArB, AiB = bc_(A16R, [128, 16, 2, 8]), bc_(A16I, [128, 16, 2, 8])
            dv(lambda h, Hs=Hs: h.memset(Hs, 0.0), [], cr_, eng)
            for pas in range(2):
                for i in range(16):
                    Qc = Qb[:, :, :, :, i]
                    dv(lambda h, Hs=Hs, T13=T13, ArB=ArB: h.tensor_tensor(out=T13, in0=Hs, in1=ArB, op=ALU.mult), cr_ + ["s5"], cr_, eng)
                    dv(lambda h, Hs=Hs, T24=T24, AiB=AiB: h.tensor_tensor(out=T24, in0=Hs, in1=AiB, op=ALU.mult), cr_ + ["s5"], cr_, eng)
                    dv(lambda h, Nn=Nn, Qc=Qc, T13=T13: h.tensor_tensor(out=Nn, in0=Qc, in1=T13, op=ALU.add), cr_ + [qr], cr_, eng)
                    if pas == 1:
                        dv(lambda h, Qc=Qc, Hs=Hs: h.tensor_copy(out=Qc, in_=Hs), cr_, [qr], eng)
                    dv(lambda h, Hs=Hs, Nn=Nn, T24=T24: h.tensor_tensor(out=Hs[:, :, 0, :], in0=Nn[:, :, 0, :], in1=T24[:, :, 1, :], op=ALU.subtract), cr_, cr_, eng)
                    dv(lambda h, Hs=Hs, Nn=Nn, T24=T24: h.tensor_tensor(out=Hs[:, :, 1, :], in0=Nn[:, :, 1, :], in1=T24[:, :, 0, :], op=ALU.add), cr_, cr_, eng)
                if pas == 0:
                    Cc = T13[:, :, :, 0]
                    Ta = T13[:, :, :, 1]
                    Tb = T13[:, :, :, 2]
                    Tc = T13[:, :, :, 3]
                    A2r, A2i = (s5[:, idx, gs].unsqueeze(2).to_broadcast([128, 16, 2]) for idx in (A256R, A256I))
                    dv(lambda h, Cc=Cc: h.memset(Cc, 0.0), [], cr_, eng)
                    for b in range(8):
                        Lb = Hs[:, :, :, b]
                        dv(lambda h, Ta=Ta, Cc=Cc, A2r=A2r: h.tensor_tensor(out=Ta, in0=Cc, in1=A2r, op=ALU.mult), cr_ + ["s5"], cr_, eng)
                        dv(lambda h, Tb=Tb, Cc=Cc, A2i=A2i: h.tensor_tensor(out=Tb, in0=Cc, in1=A2i, op=ALU.mult), cr_ + ["s5"], cr_, eng)
                        dv(lambda h, Tc=Tc, Lb=Lb, Ta=Ta: h.tensor_tensor(out=Tc, in0=Lb, in1=Ta, op=ALU.add), cr_, cr_, eng)
                        dv(lambda h, Lb=Lb, Cc=Cc: h.tensor_copy(out=Lb, in_=Cc), cr_, cr_, eng)
                        dv(lambda h, Cc=Cc, Tc=Tc, Tb=Tb: h.tensor_tensor(out=Cc[:, :, 0], in0=Tc[:, :, 0], in1=Tb[:, :, 1], op=ALU.subtract), cr_, cr_, eng)
                        dv(lambda h, Cc=Cc, Tc=Tc, Tb=Tb: h.tensor_tensor(out=Cc[:, :, 1], in0=Tc[:, :, 1], in1=Tb[:, :, 0], op=ALU.add), cr_, cr_, eng)
                    dv(lambda h, hf=hf, Cc=Cc: h.tensor_copy(out=chn[:, hf, 0, :].rearrange("p (g r) -> p g r", r=2), in_=Cc), cr_, ["chn%d" % hf], eng)
                    P.op("sp", lambda h, hf=hf: h.dma_start(out=sp_o[l, :, 32 * hf:32 * hf + 32], in_=chn[:, hf, 0, :]), reads=["chn%d" % hf],
                         writes=["sp_o%d_%d" % (l, hf)], dma_key="spo%d" % hf)

        mark('L%d s5B' % l)
        for k in range(8):
            P.op("sp", lambda h, k=k: h.dma_start(out=BCk[:, 1, 0, :], in_=Cre_d[l, :, 64 * k:64 * k + 64]), writes=["BCk1"], dma_key="bck10")
            P.op("sp", lambda h, k=k: h.dma_start(out=BCk[:, 1, 1, :], in_=Cim_d[l, :, 64 * k:64 * k + 64]), writes=["BCk1"], dma_key="bck11")
            dv(lambda h, k=k: h.tensor_tensor(out=arKTx.rearrange("p t (g h) -> p t g h", g=8),
                                              in0=KTc[:, k, :, :].unsqueeze(2).to_broadcast([128, 16, 8, 16]),
                                              in1=bd8[:].unsqueeze(1).unsqueeze(3).to_broadcast([128, 16, 8, 16]), op=ALU.mult),
               ["KTc", "bd8"], ["arKTx", "arGW0", "arGW1"])
            cmul("pool", cview(smt[:, 0:64]), cview(smt[:, 64:128]), cview(BCk[:, 1, 0, :]), cview(BCk[:, 1, 1, :]),
                 bc4(AR, k), bc4(AI, k), ["smt", "BCk1"])
            expand("pool", X0, cview(smt[:, 0:64]), 0, ["smt"], ["gX"])
            expand("pool", X1, cview(smt[:, 64:128]), 0, ["smt"], ["gX"])
            expand("pool", X2, cview(smt[:, 0:64]), 2, ["smt"], ["gX", "gX2"])
            expand("pool", X3, cview(smt[:, 64:128]), 2, ["smt"], ["gX", "gX2"])
            for qq in range(4):
                gen_quarter("dve", k, qq, X0, X1, X2, X3, arT[:, 4 * qq:4 * qq + 4, 0, :], arT[:, 4 * qq:4 * qq + 4, 1, :], ["gX"], ["arT%d" % qq])
            yb = [next_bank() for _ in range(4)]
            ys = next_bank()
            AK = [R("A", k, t_) for t_ in range(4)]
            for b in range(4):
                for tau in range(4 * b + 4):
                    i_lo = max(tau, 4 * b)
                    ni = 4 * b + 4 - i_lo
                    P.op("pe", lambda h, k=k, b=b, tau=tau, i_lo=i_lo, ni=ni, bank=yb[b]: h.matmul(
                        pst[bank][:, (i_lo - 4 * b) * 128:512].rearrange("p (i c) -> p i c", c=128),
                        lhsT=arKTx[:, tau, :], rhs=sview(k)[:, i_lo - tau:i_lo - tau + ni, :], start=(tau == 0), stop=(tau == 0), skip_group_check=(tau > 0)),
                        reads=["arKTx"] + AK, writes=["ps%d" % yb[b]])
                for i4 in range(4):
                    for q in range(4):
                        for ri in range(2):
                            P.op("pe", lambda h, k=k, b=b, i4=i4, q=q, ri=ri, bank=yb[b]: h.matmul(
                                pst[bank][32 * q:32 * q + 32, i4 * 128:(i4 + 1) * 128], lhsT=arT[:, 4 * b + i4, ri, 32 * q:32 * q + 32],
                                rhs=Qv[:, 4 * k + q, ri, :], start=False, stop=False, skip_group_check=True, tile_position=(0, 32 * q)),
                                reads=["arT%d" % b, "Qh%d" % (k // 4)], writes=["ps%d" % yb[b]])
            P.op("pe", lambda h, k=k, ys=ys: h.matmul(pst[ys][:, 0:16], lhsT=arKTx[:, 0, :], rhs=bA[:, k * NT + 2048:k * NT + 2064], start=True, stop=True),
                 reads=["arKTx", R("A", k, 4)], writes=["ps%d" % ys])
            for q in range(4):
                for ri in range(2):
                    P.op("pe", lambda h, k=k, q=q, ri=ri, ys=ys: h.matmul(
                        pst[ys][32 * q:32 * q + 32, 0:16], lhsT=arT[:, 0, ri, 32 * q:32 * q + 32], rhs=h0b[:, 4 * k + q, ri, :],
                        start=False, stop=False, skip_group_check=True, tile_position=(0, 32 * q)),
                        reads=["arT0", "sq0", "sq1"], writes=["ps%d" % ys])
            for b in range(4):
                P.op("act", lambda h, k=k, b=b, bank=yb[b]: h.activation(out=zview(k)[:, 4 * b:4 * b + 4, :], in_=pst[bank][:].rearrange("p (i c) -> p i c", c=128),
                                                                         func=AF.Gelu_apprx_tanh),
                     reads=["ps%d" % yb[b]], writes=AK)
            P.op("act", lambda h, k=k, ys=ys: h.activation(out=bufk(bA, k, 2048, 16), in_=pst[ys][:, 0:16], func=AF.Gelu_apprx_tanh),
                 reads=["ps%d" % ys], writes=[R("A", k, 4)])
        dv(lambda h: h.memset(gtmp[:, 0, 0, 0:1], 0.0), [], ALLB + S5R + ["gtmp_dve"])

    def layer(l):
        mark('L%d ada' % l)
        if l == 0:
            for _ in ada_gen(0, mod[:, 0, :, :], modA[:, 0, :, :, :], ["mod"], ["modA"]):
                pass
        else:
            P.op("dve", lambda h: h.tensor_copy(out=mod[:, 0, :, :], in_=modS), reads=["apw"], writes=["mod"])
            P.op("dve", lambda h: h.tensor_copy(out=modA[:, 0, :, :, :], in_=modAS), reads=["gX", "gX2"], writes=["modA"])
        s5_tables(l)
        mark('L%d norm1' % l)
        norm_fm(lambda k, smp: (modA[:, 0, 0, k, 0:1] if smp == 0 else modA[:, 0, 0, k, 1:17]),
                lambda k, smp: (mod[:, 0, 0 + k, 0:1] if smp == 0 else mod[:, 0, 0 + k, 1:17]), hb, "h")

        mark('L%d v' % l)
        TBALL = ["tb0", "tb1", "tb2", "tb3"]
        VG = ["msb", "rstd"]
        VSQ = ["tmp32_0", "tmp32_1"]
        P.op("pool", lambda h: h.dma_start(out=wbig, in_=w_in[l, :, 1024:2048].rearrange("(k p) n -> p k n", p=128)),
             writes=["wbig"] + B_overlap(0, 8192), dma_key="wbig")
        P.op("sp", lambda h: h.dma_start(out=gvb_sb[:], in_=gvb[l]), writes=TBALL, dma_key="gvb")
        P.op("sp", lambda h: h.dma_start(out=wsT32, in_=wsT[l].rearrange("h s t -> s h t")), writes=["sq0", "sq1"], dma_key="wsT32")
        P.op("pool", lambda h: h.dma_start(out=bspb[:], in_=bsp[l]), writes=["bspb"], dma_key="bspb")
        P.op("sp", lambda h: h.dma_start(out=w00_sb[:], in_=w00[l]), writes=["w00"], dma_key="w00")
        P.op("pool", lambda h: h.dma_start(out=b00b[:], in_=b00[l]), writes=["b00b"], dma_key="b00b")
        P.op("dve", lambda h: h.tensor_tensor(out=wsTb[:], in0=wsT32, in1=triu[:].unsqueeze(1).to_broadcast([128, 4, 128]), op=ALU.mult),
             reads=["sq0", "sq1", "triu"], writes=["wsTb"])
        for hh in range(4):
            P.op("dve", lambda h, hh=hh: h.tensor_scalar(out=WI[:, hh, :], in0=ident[0:16, 0:16], scalar1=w00_sb[:, hh:hh + 1], scalar2=None, op0=ALU.mult),
                 reads=["ident", "w00"], writes=["WI"])
        VGB = [(nrm[:].rearrange("p a n -> p (a n)"), ["msb", "rstd"]), (tmp32[:].rearrange("p a n -> p (a n)"), ["tmp32_0", "tmp32_1"])]
        vjunk = sq[:].rearrange("p a n -> p (a n)")
        for i in range(17):
            rows = 128 if i < 16 else 16
            tok0 = i * 128
            ti = min(i // 4, 4)
            vgt, VG = VGB[i % 2]
            banks = [next_bank(), next_bank()]
            for cb in range(2):
                for k in range(8):
                    P.op("pe", lambda h, k=k, cb=cb, tok0=tok0, rows=rows, bank=banks[cb]: h.matmul(
                        pst[bank][0:rows, :], lhsT=hb[:, k * NT + tok0:k * NT + tok0 + rows], rhs=wbig[:, k, cb * 512:(cb + 1) * 512],
                        start=(k == 0), stop=(k == 7)), reads=["wbig", R("h", k, ti)], writes=["ps%d" % banks[cb]])
                P.op("act", lambda h, cb=cb, rows=rows, bank=banks[cb], vgt=vgt: h.activation(out=vgt[0:rows, cb * 512:(cb + 1) * 512], in_=pst[bank][0:rows, :],
                                                                                           func=AF.Gelu_apprx_tanh),
                     reads=["ps%d" % banks[cb]], writes=[VG[cb]])
            P.op("act", lambda h, rows=rows, vgt=vgt: h.activation(out=vjunk[0:rows, :], in_=vgt[0:rows, :], func=AF.Square, accum_out=vss[0:rows, 0:1]),
                 reads=VG + ["vss"], writes=["sq0", "sq1", "vss"])
            P.op("dve", lambda h, rows=rows: h.tensor_scalar(out=vss[0:rows, 0:1], in0=vss[0:rows, 0:1], scalar1=1.0 / 1024.0, scalar2=EPS,
                                                           op0=ALU.mult, op1=ALU.add), reads=["vss"], writes=["vss"])
            P.op("act", lambda h, rows=rows: h.activation(out=vss[0:rows, 1:2], in_=vss[0:rows, 0:1], func=AF.Sqrt),
                 reads=["vss"], writes=["vrs"])
            P.op("dve", lambda h, rows=rows: h.reciprocal(out=vss[0:rows, 1:2], in_=vss[0:rows, 1:2]),
                 reads=["vrs"], writes=["vrs"])
            if i < 16:
                P.op("dve", lambda h, i=i, vgt=vgt: h.scalar_tensor_tensor(out=bA[:, i * 1024:(i + 1) * 1024], in0=vgt[:, :], scalar=vss[:, 1:2],
                                                                        in1=gvb_sb[:], op0=ALU.mult, op1=ALU.mult),
                     reads=VG + ["vrs"] + TBALL, writes=["vtm%d" % i] + A_overlap(i * 1024, (i + 1) * 1024))
            else:
                vs32t, VS = VGB[(i + 1) % 2]
                vs32v = vs32t[0:16, :]
                P.op("dve", lambda h, vgt=vgt: h.scalar_tensor_tensor(out=vs32v, in0=vgt[0:16, :], scalar=vss[0:16, 1:2],
                                                                   in1=gvb_sb[0:16, :], op0=ALU.mult, op1=ALU.mult),
                     reads=VG + ["vrs"] + TBALL, writes=VS)
                P.op("dve", lambda h: h.tensor_copy(out=vs_bf[:], in_=vs32v), reads=VS, writes=["vs_bf"])
                P.op("sp", lambda h: h.dma_start(out=gv_o[l], in_=vs32v), reads=VS, writes=["gv_o%d" % l], dma_key="gvst")

        mark('L%d gmlp' % l)
        def gmlp_evac(m, ti, t0, tn, banks):
            hh = m // 2
            mb = next_bank()
            if ti < 4:
                for c in range(4):
                    ch = ti * 4 + c
                    P.op("pe", lambda h, c=c, ch=ch: h.matmul(pst[mb][:, c * 128:(c + 1) * 128], lhsT=bA[:, ch * 1024 + m * 128:ch * 1024 + (m + 1) * 128],
                                                              rhs=wsTb[:, hh, :], start=True, stop=False),
                         reads=["vtm%d" % ch, "wsTb"], writes=["ps%d" % mb])
                    P.op("pe", lambda h, c=c: h.matmul(pst[mb][:, c * 128:(c + 1) * 128], lhsT=ones_r[0:1, :], rhs=bspb[0:1, hh * 128:(hh + 1) * 128],
                                                       start=False, stop=True),
                         reads=["ones_r", "bspb"], writes=["ps%d" % mb])
            else:
                P.op("pe", lambda h: h.matmul(pst[mb][:, 0:16], lhsT=vs_bf[0:16, m * 128:(m + 1) * 128], rhs=WI[0:16, hh, :], start=True, stop=False),
                     reads=["vs_bf", "WI"], writes=["ps%d" % mb])
                P.op("pe", lambda h: h.matmul(pst[mb][:, 0:16], lhsT=ones_r[0:1, :], rhs=b00b[0:1, hh * 16:(hh + 1) * 16], start=False, stop=True),
                     reads=["ones_r", "b00b"], writes=["ps%d" % mb])
            a, b = uid() % 4, uid() % 4
            if b == a:
                b = (a + 1) % 4
            P.op("act", lambda h: h.activation(out=tb[:, a, 0:tn], in_=pst[banks[0]][:, 0:tn], func=AF.Gelu_apprx_tanh),
                 reads=["ps%d" % banks[0]], writes=["tb%d" % a])
            P.op("act", lambda h: h.activation(out=tb[:, b, 0:tn], in_=pst[banks[1]][:, 0:tn], func=AF.Sigmoid),
                 reads=["ps%d" % banks[1]], writes=["tb%d" % b])
            P.op("dve", lambda h: h.tensor_tensor(out=tb[:, a, 0:tn], in0=pst[mb][:, 0:tn], in1=tb[:, a, 0:tn], op=ALU.mult),
                 reads=["ps%d" % mb, "tb%d" % a], writes=["tb%d" % a])
            P.op("dve", lambda h: h.tensor_tensor(out=bufk(bB, m, t0, tn), in0=tb[:, a, 0:tn], in1=tb[:, b, 0:tn], op=ALU.mult),
                 reads=["tb%d" % a, "tb%d" % b], writes=[R("B", m, ti)] + (["wbig"] if m * NT + t0 < 8192 else []))

        multi_proj([(lambda m: w_in[l, :, m * 128:(m + 1) * 128], hb, "h", 8),
                    (lambda m: w_in[l, :, 3072 + m * 128:3072 + (m + 1) * 128], hb, "h", 8)], 8, gmlp_evac)

        mark('L%d wout1' % l)
        multi_proj([(lambda m: w_out[l, :, m * 128:(m + 1) * 128], bB, "B", 8)], 8, resid_evac(l, 2))

        mark('L%d s5A' % l)
        if WITH_S5:
            s5_layer(l)
        else:
            for k in range(8):
                for ti, (t0, tn) in enumerate(TT):
                    P.op("pool", lambda h, k=k, t0=t0, tn=tn: h.memset(bufk(bA, k, t0, tn), 0.0),
                         writes=[R("A", k, ti)] + vtm_overlap(k * NT + t0, k * NT + t0 + tn))

        mark('L%d glu' % l)
        def glu_evac(m, ti, t0, tn, banks):
            a, b = uid() % 4, uid() % 4
            if b == a:
                b = (a + 1) % 4
            P.op("act", lambda h: h.activation(out=tb[:, a, 0:tn], in_=pst[banks[0]][:, 0:tn], func=AF.Sigmoid, bias=vec_sb[:, l, 72 + m:73 + m]),
                 reads=["ps%d" % banks[0], "vec"], writes=["tb%d" % a])
            P.op("act", lambda h: h.activation(out=tb[:, b, 0:tn], in_=pst[banks[1]][:, 0:tn], func=AF.Sigmoid),
                 reads=["ps%d" % banks[1]], writes=["tb%d" % b])
            P.op("dve", lambda h: h.tensor_tensor(out=tb[:, a, 0:tn], in0=bufk(bA, m, t0, tn), in1=tb[:, a, 0:tn], op=ALU.mult),
                 reads=[R("A", m, ti), "tb%d" % a], writes=["tb%d" % a])
            P.op("dve", lambda h: h.tensor_tensor(out=bufk(bB, m, t0, tn), in0=tb[:, a, 0:tn], in1=tb[:, b, 0:tn], op=ALU.mult),
                 reads=["tb%d" % a, "tb%d" % b], writes=[R("B", m, ti)])

        multi_proj([(lambda m: w_glu[l, :, m * 128:(m + 1) * 128], bA, "A", 8),
                    (lambda m: w_in[l, :, 4096 + m * 128:4096 + (m + 1) * 128], hb, "h", 8)], 8, glu_evac)
        multi_proj([(lambda m: w_out[l, :, m * 128:(m + 1) * 128], bB, "B", 8)], 8, resid_evac(l, 2))

        mark('L%d ffn' % l)
        norm_fm(lambda k, smp: (modA[:, 0, 1, k, 0:1] if smp == 0 else modA[:, 0, 1, k, 1:17]),
                lambda k, smp: (mod[:, 0, 24 + k, 0:1] if smp == 0 else mod[:, 0, 24 + k, 1:17]), hb, "h")
        def ff1(g):
            fbuf, fpref = (bA, "A") if g % 2 == 0 else (bB, "B")

            def ff1_evac(m, ti, t0, tn, banks):
                a = uid() % 4
                P.op("act", lambda h: h.activation(out=tb[:, a, 0:tn], in_=pst[banks[0]][:, 0:tn], func=AF.Relu),
                     reads=["ps%d" % banks[0]], writes=["tb%d" % a])
                P.op("dve", lambda h: h.tensor_tensor(out=bufk(fbuf, m, t0, tn), in0=tb[:, a, 0:tn], in1=tb[:, a, 0:tn], op=ALU.mult),
                     reads=["tb%d" % a], writes=[R(fpref, m, ti)])

            multi_proj([(lambda m: w_ff1[l, :, g * 1024 + m * 128:g * 1024 + (m + 1) * 128], hb, "h", 8)], 8, ff1_evac)

        def ff2(g):
            fbuf, fpref = (bA, "A") if g % 2 == 0 else (bB, "B")
            multi_proj([(lambda m: w_ff2[l, g * 1024:(g + 1) * 1024, m * 128:(m + 1) * 128], fbuf, fpref, 8)], 8, resid_evac(l, 5))

        if l + 1 < DEPTH:
            brange[1] = 6
            bg["gen"] = ada_gen(l + 1, modS, modAS, ["apw"], ["gX", "gX2"], banks=[6, 7])
        ff1(0)
        for g in range(4):
            if g + 1 < 4:
                ff1(g + 1)
            ff2(g)
        bg_step(1000)
        brange[1] = 8

    for l_ in range(DEPTH):
        layer(l_)

    mark('final')
    norm_fm(None, None, None, None, final=True)

    P.finalize()
    st.close()
    return nc


_NC_CACHE = {}
PHASES = []


def _host_inputs(inp):
    f = np.float32
    g = lambda n: np.asarray(inp[n], dtype=f)
    x_prompt, x_sample = g("x_prompt"), g("x_sample")
    c_prompt, c_sample = g("c_prompt"), g("c_sample")
    b_ada, g1, g2, dsk, bglu = g("b_ada"), g("g_norm1"), g("g_norm2"), g("d_skip"), g("b_glu")
    vecs = np.zeros((DEPTH, 128, 80), f)
    for l in range(DEPTH):
        vecs[l, :, 0:48] = b_ada[l].reshape(48, 128).T
        vecs[l, :, 48:56] = g1[l].reshape(8, 128).T
        vecs[l, :, 56:64] = g2[l].reshape(8, 128).T
        vecs[l, :, 64:72] = dsk[l].reshape(8, 128).T
        vecs[l, :, 72:80] = bglu[l].reshape(8, 128).T
    gfin = np.ascontiguousarray(g("g_final").reshape(8, 128).T)
    gvb = np.ascontiguousarray(np.broadcast_to(g("g_v")[:, None, :], (DEPTH, 128, D)))
    ws = g("w_spatial")
    wsT = np.ascontiguousarray(np.transpose(ws, (0, 1, 3, 2)))
    bs = g("b_spatial")
    bsp = np.ascontiguousarray(bs.reshape(DEPTH, 1, 512))
    w00 = np.ascontiguousarray(np.broadcast_to(ws[:, None, :, 0, 0], (DEPTH, 16, 4)))
    b00 = np.ascontiguousarray(np.broadcast_to(bs[:, :, 0:1], (DEPTH, 4, 16)).reshape(DEPTH, 1, 64))
    lam_re, lam_im, log_dt = g("lam_re"), g("lam_im"), g("log_dt")
    sm2 = lambda a: np.ascontiguousarray(a.reshape(DEPTH, 32, 2, 64).transpose(0, 2, 3, 1).reshape(DEPTH, 128, 32))
    lamre, lamim = sm2(lam_re), sm2(lam_im)
    ldt = np.ascontiguousarray(np.broadcast_to(log_dt.reshape(DEPTH, 32, 2).transpose(0, 2, 1)[:, :, None, :], (DEPTH, 2, 64, 32)).reshape(DEPTH, 128, 32))
    smB = lambda a: np.ascontiguousarray(a.reshape(DEPTH, 32, 2, 64, 16).transpose(0, 2, 3, 1, 4).reshape(DEPTH, 128, 512))
    smC = lambda a: np.ascontiguousarray(a.reshape(DEPTH, 32, 2, 16, 64).transpose(0, 2, 4, 1, 3).reshape(DEPTH, 128, 512))
    msm = np.zeros((128, 4), f)
    msm[0:64, 0] = 1; msm[64:128, 1] = 1; msm[:, 2:4] = -msm[:, 0:2]
    ii = np.arange(128)
    bdm = (ii[:, None] // 16 == ii[None, :] // 16).astype(f)
    bd8 = (ii[:, None] // 16 == np.arange(8)[None, :]).astype(f)
    sre, sim = g("state_ssm_re"), g("state_ssm_im")
    shared = dict(lamre=lamre, lamim=lamim, ldt=ldt, Bre=smB(g("b_re")), Bim=smB(g("b_im")), Cre=smC(g("c_re")), Cim=smC(g("c_im")),
                  msm=msm, bdm=bdm, bd8=bd8, tauv=np.ascontiguousarray(np.broadcast_to(np.arange(16, dtype=f)[None, :], (128, 16))), vecs=vecs, gfin=gfin, gvb=gvb, wsT=wsT, bsp=bsp, w00=w00, b00=b00,
                  ident=np.eye(128, dtype=f), triu=np.triu(np.ones((128, 128), f)),
                  w_ada=g("w_ada"), w_in=g("w_in"), w_glu=g("w_glu"), w_out=g("w_out"), w_ff1=g("w_ff1"), w_ff2=g("w_ff2"))
    maps = []
    for c in range(8):
        xs = x_sample[16 * c:16 * c + 16, 0, :]
        xT = np.ascontiguousarray(np.concatenate([x_prompt[c], xs], axis=0).T)
        cT = np.ascontiguousarray(np.concatenate([c_prompt[c:c + 1], c_sample[16 * c:16 * c + 16]], axis=0).T)
        hre = sre[:, 16 * c:16 * c + 16].reshape(DEPTH, 16, 32, 2, 64).transpose(0, 3, 4, 2, 1)
        him = sim[:, 16 * c:16 * c + 16].reshape(DEPTH, 16, 32, 2, 64).transpose(0, 3, 4, 2, 1)
        h0 = np.ascontiguousarray(np.stack([hre, him], axis=4).reshape(DEPTH, 128, 1024))
        m = dict(shared)
        m.update(xT=xT, cT=cT, h0=h0)
        maps.append(m)
    return maps


def kernel(**inputs):
    if "nc" not in _NC_CACHE:
        _NC_CACHE["nc"] = build_nc()
    nc = _NC_CACHE["nc"]
    maps = _host_inputs(inputs)
    res = run_bass_kernel_spmd(nc, maps, core_ids=list(range(8)))
    rs = res.results
    y_prompt = np.stack([rs[c]["yT"][:, :2048].T for c in range(8)], axis=0)
    y_sample = np.concatenate([rs[c]["yT"][:, 2048:].T for c in range(8)], axis=0)[:, None, :]
    gv = np.concatenate([rs[c]["gv_o"] for c in range(8)], axis=1)[:, :, None, :]
    sp = np.stack([rs[c]["sp_o"].reshape(DEPTH, 2, 64, 32, 2) for c in range(8)], axis=1)
    sp = sp.transpose(0, 1, 5, 4, 2, 3).reshape(DEPTH, 8, 2, 64, 64)
    ss = np.stack([rs[c]["ss_o"].reshape(DEPTH, 2, 64, 32, 2, 16) for c in range(8)], axis=1)
    ss = ss.transpose(0, 5, 1, 6, 4, 2, 3).reshape(DEPTH, 2, 128, 64, 64)
    return (np.ascontiguousarray(y_prompt), np.ascontiguousarray(y_sample),
            np.ascontiguousarray(sp[:, :, 0]), np.ascontiguousarray(sp[:, :, 1]),
            np.ascontiguousarray(ss[:, 0]), np.ascontiguousarray(ss[:, 1]),
            np.ascontiguousarray(gv))
```
